# Optimizing a Trainium2 kernel written in Bass

```python
import math
import jax, jax.numpy as jnp
from jax import lax
import numpy as np

D_MODEL = 1024
BATCH = 2
SEQ = 8192
DEPTH = 2
DEC_BATCH = 8
DEC_SEQ = 8192
PAST_LEN = 128

GRID_W = 64
HEAD_DIM = 64
GROUP_W = D_MODEL // 4
N_HEADS_A = GROUP_W // HEAD_DIM
N_GROUPS_B = GROUP_W // HEAD_DIM
HY_CH = GROUP_W
N_HEADS_D = GROUP_W // HEAD_DIM
KH_MAX = 8
KW = 16
HY_ORDER = 2
HY_EMB = 33
HY_BANDS = (HY_EMB - 1) // 2
HY_FILT = 64
ML_CHUNK = 128
PLE_DIM = 256
D_FF = -(-8 * D_MODEL // (3 * 256)) * 256
EPS = 1e-6
A_COLS = 3 * GROUP_W
B_COLS = GROUP_W
C_COLS = 3 * HY_CH
D_COLS = 4 * GROUP_W + 4 * N_HEADS_D
N_IN = A_COLS + B_COLS + C_COLS + D_COLS

kernel_name = 'hybrid_bidir_na_fnet_hyena_mlstm'


def rmsnorm(x, g):
    xf = x.astype(jnp.float32)
    xf = xf * lax.rsqrt(jnp.mean(xf * xf, axis=-1, keepdims=True) + EPS)
    return (xf * g.astype(jnp.float32)).astype(x.dtype)


def head_rmsnorm(y, g):
    yf = y.astype(jnp.float32)
    sh = yf.shape
    yh = yf.reshape(sh[:-1] + (sh[-1] // HEAD_DIM, HEAD_DIM))
    yh = yh * lax.rsqrt(jnp.mean(yh * yh, axis=-1, keepdims=True) + EPS)
    return (yh.reshape(sh) * g.astype(jnp.float32)).astype(y.dtype)


def neighbourhood_attention(q, k, v, rpb):
    bsz, L, _ = q.shape
    rows = L // GRID_W
    kh = min(KH_MAX, rows)
    n_keys = kh * KW
    r = np.arange(rows)
    c = np.arange(GRID_W)
    key_r = np.clip(r - kh // 2, 0, rows - kh)[:, None] + np.arange(kh)[None, :]
    key_c = np.clip(c - KW // 2, 0, GRID_W - KW)[:, None] + np.arange(KW)[None, :]
    idx = (key_r[:, None, :, None] * GRID_W + key_c[None, :, None, :]).reshape(rows, GRID_W, n_keys)
    dr = (key_r - r[:, None] + KH_MAX - 1)[:, None, :, None]
    dc = (key_c - c[:, None] + KW - 1)[None, :, None, :]
    bias = rpb.astype(jnp.float32)[:, dr, dc].reshape(N_HEADS_A, rows, GRID_W, n_keys).transpose(1, 0, 2, 3)
    qr = q.reshape(bsz, rows, GRID_W, N_HEADS_A, HEAD_DIM).transpose(1, 0, 3, 2, 4)
    kk = k.reshape(bsz, L, N_HEADS_A, HEAD_DIM).transpose(0, 2, 1, 3)
    vv = v.reshape(bsz, L, N_HEADS_A, HEAD_DIM).transpose(0, 2, 1, 3)
    scale = HEAD_DIM ** -0.5

    def one_row(args):
        q_r, idx_r, bias_r = args
        kg = jnp.take(kk, idx_r, axis=2)
        vg = jnp.take(vv, idx_r, axis=2)
        s = jnp.einsum('bhwd,bhwnd->bhwn', q_r, kg).astype(jnp.float32) * scale + bias_r[None]
        pr = jax.nn.softmax(s, axis=-1).astype(vv.dtype)
        return jnp.einsum('bhwn,bhwnd->bhwd', pr, vg)

    out = lax.map(one_row, (qr, jnp.asarray(idx, dtype=jnp.int32), bias))
    return out.transpose(1, 0, 3, 2, 4).reshape(bsz, L, N_HEADS_A * HEAD_DIM)


def fourier_mix(u, w):
    bsz, L, _ = u.shape
    uf = u.astype(jnp.float32).reshape(bsz, L, N_GROUPS_B, HEAD_DIM)
    f = jnp.fft.fft2(uf, axes=(1, 3), norm='ortho').real
    y = jnp.einsum('blgc,gcd->blgd', f, w.astype(jnp.float32))
    return y.reshape(bsz, L, N_GROUPS_B * HEAD_DIM).astype(u.dtype)


def hyena_filters(L, w1, b1, freq, w2, b2, w3, decay):
    f32 = jnp.float32
    s = jnp.arange(L, dtype=f32)
    t = s / max(L - 1, 1)
    ang = (2.0 * math.pi / L) * s
    bands = jnp.linspace(1e-4, HY_BANDS - 1, HY_BANDS, dtype=f32)
    fb = ang[:, None] * bands[None, :]
    feats = jnp.concatenate([t[:, None], jnp.cos(fb), -jnp.sin(fb)], axis=-1)
    freq = freq.astype(f32)
    h = jnp.sin(freq[0] * (feats @ w1.astype(f32) + b1.astype(f32)))
    h = jnp.sin(freq[1] * (h @ w2.astype(f32) + b2.astype(f32)))
    h = (h @ w3.astype(f32)).reshape(L, HY_ORDER, 2, HY_CH)
    h = h * jnp.exp(-t[:, None, None, None] * decay.astype(f32))
    fwd = h[:, :, 0]
    bwd = h[1:, :, 1][::-1]
    full = jnp.concatenate([fwd, jnp.zeros((1, HY_ORDER, HY_CH), f32), bwd], axis=0)
    return jnp.fft.rfft(full, axis=0)


def fftconv(u, kf, skip):
    L = u.shape[1]
    U = jnp.fft.rfft(u, n=2 * L, axis=1)
    y = jnp.fft.irfft(U * kf[None], n=2 * L, axis=1)[:, :L]
    return y + u * skip


def short_conv(z, w, b):
    zp = jnp.pad(z, ((0, 0), (1, 1), (0, 0)))
    return zp[:, :-2] * w[0] + zp[:, 1:-1] * w[1] + zp[:, 2:] * w[2] + b


def hyena(u, conv_w, conv_b, w1, b1, freq, w2, b2, w3, decay, skip):
    f32 = jnp.float32
    L = u.shape[1]
    z = short_conv(u.astype(f32), conv_w.astype(f32), conv_b.astype(f32))
    v, x1, x2 = jnp.split(z, 3, axis=-1)
    kf = hyena_filters(L, w1, b1, freq, w2, b2, w3, decay)
    skip = skip.astype(f32)
    y = x1 * fftconv(v, kf[:, 0], skip[0])
    y = x2 * fftconv(y, kf[:, 1], skip[1])
    return y.astype(u.dtype)


def mlstm_chunkwise(q, k, v, ipre, fpre):
    bsz, H, L, dh = q.shape
    nc = L // ML_CHUNK
    q = q.reshape(bsz, H, nc, ML_CHUNK, dh)
    k = k.reshape(bsz, H, nc, ML_CHUNK, dh)
    v = v.reshape(bsz, H, nc, ML_CHUNK, dh)
    lf = jax.nn.log_sigmoid(fpre).reshape(bsz, H, nc, ML_CHUNK)
    li = ipre.reshape(bsz, H, nc, ML_CHUNK)
    b = jnp.cumsum(lf, axis=-1)
    g = b[..., -1]
    a = g[..., None] - b + li
    m_loc = jnp.max(a, axis=-1)
    w = jnp.exp(a - m_loc[..., None])
    S_loc = jnp.einsum('bhnc,bhncd,bhnce->bhnde', w, k, v)
    n_loc = jnp.einsum('bhnc,bhncd->bhnd', w, k)

    def step(carry, xs):
        S, n, m = carry
        g_c, m_c, S_c, n_c = xs
        m_new = jnp.maximum(g_c + m, m_c)
        d_old = jnp.exp(g_c + m - m_new)
        d_new = jnp.exp(m_c - m_new)
        S_new = d_old[..., None, None] * S + d_new[..., None, None] * S_c
        n_new = d_old[..., None] * n + d_new[..., None] * n_c
        return (S_new, n_new, m_new), (S, n, m)

    init = (jnp.zeros((bsz, H, dh, dh), jnp.float32), jnp.zeros((bsz, H, dh), jnp.float32),
            jnp.zeros((bsz, H), jnp.float32))
    xs = (jnp.moveaxis(g, 2, 0), jnp.moveaxis(m_loc, 2, 0), jnp.moveaxis(S_loc, 2, 0), jnp.moveaxis(n_loc, 2, 0))
    _, (S0, n0, m0) = lax.scan(step, init, xs)
    S0 = jnp.moveaxis(S0, 0, 2)
    n0 = jnp.moveaxis(n0, 0, 2)
    m0 = jnp.moveaxis(m0, 0, 2)
    inter = b + m0[..., None]
    lower = np.tril(np.ones((ML_CHUNK, ML_CHUNK), dtype=bool))
    Dm = jnp.where(lower, b[..., :, None] - b[..., None, :] + li[..., None, :], -jnp.inf)
    m_t = jnp.maximum(inter, jnp.max(Dm, axis=-1))
    P = jnp.exp(Dm - m_t[..., None]) * jnp.einsum('bhncd,bhnsd->bhncs', q, k)
    wi = jnp.exp(inter - m_t)
    num = wi[..., None] * jnp.einsum('bhncd,bhnde->bhnce', q, S0) + jnp.einsum('bhncs,bhnse->bhnce', P, v)
    den = wi * jnp.einsum('bhncd,bhnd->bhnc', q, n0) + jnp.sum(P, axis=-1)
    h = num / jnp.maximum(jnp.abs(den), jnp.exp(-m_t))[..., None]
    return h.reshape(bsz, H, L, dh)


def mlstm_bidir(q, k, v, o, gates, gate_b):
    f32 = jnp.float32
    bsz, L, _ = q.shape
    def heads(t):
        return t.astype(f32).reshape(bsz, L, N_HEADS_D, HEAD_DIM).transpose(0, 2, 1, 3)
    qh = heads(q)
    kh = heads(k) * (HEAD_DIM ** -0.5)
    vh = heads(v)
    gt = (gates.astype(f32).reshape(bsz, L, 4, N_HEADS_D) + gate_b.astype(f32)).transpose(2, 0, 3, 1)
    h_f = mlstm_chunkwise(qh, kh, vh, gt[0], gt[1])
    fl = lambda t: jnp.flip(t, axis=2)
    h_b = fl(mlstm_chunkwise(fl(qh), fl(kh), fl(vh), fl(gt[2]), fl(gt[3])))
    h = (h_f + h_b).transpose(0, 2, 1, 3).reshape(bsz, L, N_HEADS_D * HEAD_DIM)
    return (jax.nn.sigmoid(o.astype(f32)) * h).astype(q.dtype)


def _layer(x, p_i, norm_mix, w_in, attn_rpb, fnet_w, hy_conv_w, hy_conv_b, hy_w1, hy_b1, hy_freq, hy_w2,
           hy_b2, hy_w3, hy_decay, hy_skip, ml_gate_b, out_norm, w_out, norm_ffn, w_gate, w_up, w_down,
           ple_norm, w_ple_gate, w_ple_proj):
    xn = rmsnorm(x, norm_mix)
    z = xn @ w_in
    splits = [GROUP_W, GROUP_W, GROUP_W, B_COLS, C_COLS, GROUP_W, GROUP_W, GROUP_W, GROUP_W, 4 * N_HEADS_D]
    qa, ka, va, ub, uc, qd, kd, vd, od, gd = jnp.split(z, list(np.cumsum(splits)[:-1]), axis=-1)
    ya = neighbourhood_attention(qa, ka, va, attn_rpb)
    yb = fourier_mix(ub, fnet_w)
    yc = hyena(uc, hy_conv_w, hy_conv_b, hy_w1, hy_b1, hy_freq, hy_w2, hy_b2, hy_w3, hy_decay, hy_skip)
    yd = mlstm_bidir(qd, kd, vd, od, gd, ml_gate_b)
    y = head_rmsnorm(jnp.concatenate([ya, yb, yc, yd], axis=-1), out_norm)
    x = x + y @ w_out
    hn = rmsnorm(x, norm_ffn)
    x = x + (jax.nn.silu(hn @ w_gate) * (hn @ w_up)) @ w_down
    gate = jax.nn.sigmoid(rmsnorm(x, ple_norm) @ w_ple_gate)
    return x + gate * (p_i @ w_ple_proj)


def _trunk(x, p, layer_weights, final_norm):
    for i in range(DEPTH):
        x = _layer(x, p[i], *[w[i] for w in layer_weights])
    return rmsnorm(x, final_norm)


def setup_inputs(seed: int = 0) -> dict:
    key = jax.random.key(seed)
    ks = list(jax.random.split(key, 40))
    f32 = jnp.float32
    def nk():
        return ks.pop()
    def nrm(shape, scale):
        return jax.random.normal(nk(), shape, f32) * scale
    def gain(shape):
        return 1.0 + 0.05 * jax.random.normal(nk(), shape, f32)
    def unif(shape, lo, hi):
        return jax.random.uniform(nk(), shape, f32, lo, hi)
    inp = {}
    inp['x_prompt'] = nrm((BATCH, SEQ, D_MODEL), 1.0)
    inp['x_sample'] = nrm((DEC_BATCH, DEC_SEQ, D_MODEL), 1.0)
    inp['p_prompt'] = nrm((DEPTH, BATCH, SEQ, PLE_DIM), 1.0)
    inp['p_sample'] = nrm((DEPTH, DEC_BATCH, DEC_SEQ, PLE_DIM), 1.0)
    inp['norm_mix'] = gain((DEPTH, D_MODEL))
    inp['w_in'] = nrm((DEPTH, D_MODEL, N_IN), D_MODEL ** -0.5)
    inp['attn_rpb'] = nrm((DEPTH, N_HEADS_A, 2 * KH_MAX - 1, 2 * KW - 1), 0.1)
    inp['fnet_w'] = nrm((DEPTH, N_GROUPS_B, HEAD_DIM, HEAD_DIM), HEAD_DIM ** -0.5)
    inp['hy_conv_w'] = nrm((DEPTH, 3, C_COLS), 3 ** -0.5)
    inp['hy_conv_b'] = nrm((DEPTH, C_COLS), 0.02)
    inp['hy_w1'] = nrm((DEPTH, HY_EMB, HY_FILT), HY_EMB ** -0.5)
    inp['hy_b1'] = nrm((DEPTH, HY_FILT), 0.1)
    inp['hy_freq'] = gain((DEPTH, 2, HY_FILT))
    inp['hy_w2'] = nrm((DEPTH, HY_FILT, HY_FILT), HY_FILT ** -0.5)
    inp['hy_b2'] = nrm((DEPTH, HY_FILT), 0.1)
    inp['hy_w3'] = nrm((DEPTH, HY_FILT, HY_ORDER * 2 * HY_CH), HY_FILT ** -0.5)
    inp['hy_decay'] = unif((DEPTH, HY_ORDER, 2, HY_CH), 3.0, 15.0)
    inp['hy_skip'] = nrm((DEPTH, HY_ORDER, HY_CH), 1.0)
    ib = nrm((DEPTH, 2, N_HEADS_D), 0.1)
    fb = unif((DEPTH, 2, N_HEADS_D), 3.0, 6.0)
    inp['ml_gate_b'] = jnp.stack([ib[:, 0], fb[:, 0], ib[:, 1], fb[:, 1]], axis=1)
    inp['out_norm'] = gain((DEPTH, D_MODEL))
    inp['w_out'] = nrm((DEPTH, D_MODEL, D_MODEL), D_MODEL ** -0.5)
    inp['norm_ffn'] = gain((DEPTH, D_MODEL))
    inp['w_gate'] = nrm((DEPTH, D_MODEL, D_FF), D_MODEL ** -0.5)
    inp['w_up'] = nrm((DEPTH, D_MODEL, D_FF), D_MODEL ** -0.5)
    inp['w_down'] = nrm((DEPTH, D_FF, D_MODEL), D_FF ** -0.5)
    inp['ple_norm'] = gain((DEPTH, D_MODEL))
    inp['w_ple_gate'] = nrm((DEPTH, D_MODEL, D_MODEL), D_MODEL ** -0.5)
    inp['w_ple_proj'] = nrm((DEPTH, PLE_DIM, D_MODEL), PLE_DIM ** -0.5)
    inp['final_norm'] = gain((D_MODEL,))
    return inp


def reference(x_prompt, x_sample, p_prompt, p_sample, norm_mix, w_in, attn_rpb, fnet_w, hy_conv_w, hy_conv_b,
              hy_w1, hy_b1, hy_freq, hy_w2, hy_b2, hy_w3, hy_decay, hy_skip, ml_gate_b, out_norm, w_out,
              norm_ffn, w_gate, w_up, w_down, ple_norm, w_ple_gate, w_ple_proj, final_norm):
    layer_weights = (norm_mix, w_in, attn_rpb, fnet_w, hy_conv_w, hy_conv_b, hy_w1, hy_b1, hy_freq, hy_w2,
                     hy_b2, hy_w3, hy_decay, hy_skip, ml_gate_b, out_norm, w_out, norm_ffn, w_gate, w_up,
                     w_down, ple_norm, w_ple_gate, w_ple_proj)
    y_prompt = _trunk(x_prompt, p_prompt, layer_weights, final_norm)
    y_sample = _trunk(x_sample, p_sample, layer_weights, final_norm)
    return (y_prompt, y_sample)
```

```python
import math
import contextlib
import numpy as np
import ml_dtypes
import concourse.bass as bass
import concourse.mybir as mybir
from concourse.bass_utils import run_bass_kernel_spmd

F32 = mybir.dt.float32
BF16 = mybir.dt.bfloat16
ALU = mybir.AluOpType
AF = mybir.ActivationFunctionType
AX = mybir.AxisListType

L = 8192
D = 1024
NIN = 2832
DFF = 2816
DEPTH = 2
EPS = 1e-6
NFFT = 16384
ENGS = ("pe", "act", "dve", "pool", "sp")
NDMA = 8


class Sched:
    def __init__(self, nc):
        self.nc = nc
        self.ops = {e: [] for e in ENGS}
        self.cnt = {e: 0 for e in ENGS}
        self.seen = {e: {} for e in ENGS}
        self.last_w = {}
        self.readers = {}
        self.dma_use = {e: [0] * NDMA for e in ENGS}
        self.dma_rr = {e: 0 for e in ENGS}

    def _need(self, eng, tok, waits):
        key, val = tok
        if self.seen[eng].get(key, 0) >= val:
            return
        self.seen[eng][key] = val
        waits.append(tok)

    def op(self, eng, fn, reads=(), writes=(), dma=False, self_wait=False):
        waits = []
        same = ("c", eng)
        if self_wait and self.cnt[eng]:
            self._need(eng, (same, self.cnt[eng]), waits)
        for r in reads:
            t = self.last_w.get(r)
            if t is not None:
                self._need(eng, t, waits)
        strict = dma or eng != "pe"
        for w in writes:
            t = self.last_w.get(w)
            if t is not None and (strict or t[0] != same):
                self._need(eng, t, waits)
            for k, v in self.readers.get(w, {}).items():
                if strict or k != same:
                    self._need(eng, (k, v), waits)
        if dma:
            i = self.dma_rr[eng]
            self.dma_rr[eng] = (i + 1) % NDMA
            key = ("d", eng, i)
            prev = self.dma_use[eng][i]
            if prev:
                self._need(eng, (key, prev), waits)
            self.dma_use[eng][i] = prev + 16
            tok = (key, prev + 16)
            inc = 16
        else:
            self.cnt[eng] += 1
            tok = (same, self.cnt[eng])
            inc = 1
        self.ops[eng].append((waits, fn, tok[0], inc))
        for w in writes:
            self.last_w[w] = tok
            self.readers[w] = {}
        for r in reads:
            d = self.readers.setdefault(r, {})
            if d.get(tok[0], 0) < tok[1]:
                d[tok[0]] = tok[1]
        return tok

    def latest(self):
        latest = {}
        for e in ENGS:
            if self.cnt[e]:
                latest[("c", e)] = self.cnt[e]
            for i, v in enumerate(self.dma_use[e]):
                if v:
                    latest[("d", e, i)] = v
        return latest

    def barrier(self):
        latest = self.latest()
        for e in ENGS:
            waits = []
            for k, v in latest.items():
                if k == ("c", e):
                    continue
                self._need(e, (k, v), waits)
            if waits:
                self.ops[e].append((waits, None, None, 0))
        self.last_w = {}
        self.readers = {}

    def emit(self):
        nc = self.nc
        final = self.latest()
        keys = set()
        for e in ENGS:
            for waits, fn, key, inc in self.ops[e]:
                if key is not None:
                    keys.add(key)
        keys = sorted(keys)
        with contextlib.ExitStack() as st:
            sems = {}
            for k in keys:
                sems[k] = st.enter_context(nc.semaphore("s_" + "_".join(str(x) for x in k)))
            block = st.enter_context(nc.Block())

            def run(e):
                def body(engine):
                    for waits, fn, key, inc in self.ops[e]:
                        for (k, v) in waits:
                            engine.wait_ge(sems[k], v)
                        if fn is not None:
                            fn(engine).then_inc(sems[key], inc)
                    if e == "sp":
                        for k, v in final.items():
                            engine.wait_ge(sems[k], v)
                return body

            block.tensor(run("pe"))
            block.scalar(run("act"))
            block.vector(run("dve"))
            block.gpsimd(run("pool"))
            block.sync(run("sp"))


class Ctx:
    def __init__(self, nc, S, T, nseq):
        self.nc, self.S, self.T, self.nseq = nc, S, T, nseq
        self._ev = 0

    def sbt(self, name, shape, dt):
        self._uid = getattr(self, "_uid", 0) + 1
        return self.nc.sbuf_tensor(f"{name}_u{self._uid}", shape, dt)

    def pst(self, name, shape, dt):
        self._uid = getattr(self, "_uid", 0) + 1
        return self.nc.psum_tensor(f"{name}_u{self._uid}", shape, dt)

    def dma(self, out, in_, r, w):
        return self.S.op("sp", lambda e: e.dma_start(out=out, in_=in_), reads=r, writes=w, dma=True)

    def mm(self, out, lhsT, rhs, start, stop, r, w):
        sig = (lhsT.base_partition(), lhsT.shape[0])
        sw = getattr(self, "_pe_sig", None) not in (None, sig)
        self._pe_sig = sig
        return self.S.op("pe", lambda e: e.matmul(out=out, lhsT=lhsT, rhs=rhs, start=start, stop=stop), reads=r, writes=w, self_wait=sw)

    def tr(self, out, in_, ident, r, w):
        sig = (in_.base_partition(), in_.shape[0])
        sw = getattr(self, "_pe_sig", None) not in (None, sig)
        self._pe_sig = sig
        return self.S.op("pe", lambda e: e.transpose(out=out, in_=in_, identity=ident), reads=r, writes=w, self_wait=sw)

    def act(self, out, in_, func, r, w, **kw):
        return self.S.op("act", lambda e: e.activation(out=out, in_=in_, func=func, **kw), reads=r, writes=w)

    def cp(self, eng, out, in_, r, w):
        if eng == "act":
            return self.act(out, in_, AF.Copy, r, w)
        return self.S.op(eng, lambda e: e.tensor_copy(out=out, in_=in_), reads=r, writes=w)

    def ev(self):
        self._ev ^= 1
        return "act" if self._ev else "dve"

    def ts(self, eng, out, in0, s1, s2, op0, op1, r, w):
        if op1 is None:
            return self.S.op(eng, lambda e: e.tensor_scalar(out=out, in0=in0, scalar1=s1, scalar2=None, op0=op0), reads=r, writes=w)
        return self.S.op(eng, lambda e: e.tensor_scalar(out=out, in0=in0, scalar1=s1, scalar2=s2, op0=op0, op1=op1), reads=r, writes=w)

    def tt(self, eng, out, in0, in1, op, r, w):
        return self.S.op(eng, lambda e: e.tensor_tensor(out=out, in0=in0, in1=in1, op=op), reads=r, writes=w)

    def stt(self, eng, out, in0, scalar, in1, op0, op1, r, w):
        return self.S.op(eng, lambda e: e.scalar_tensor_tensor(out=out, in0=in0, scalar=scalar, in1=in1, op0=op0, op1=op1), reads=r, writes=w)

    def memset(self, eng, ap, val, w):
        return self.S.op(eng, lambda e: e.memset(ap, val), writes=w)

    def recip(self, out, in_, r, w):
        return self.S.op("dve", lambda e: e.reciprocal(out=out, in_=in_), reads=r, writes=w)

    def rstd(self, out, in_, scale, r, w, name):
        self.ts("dve", out, in_, scale, EPS, ALU.mult, ALU.add, r, w)
        self.act(out, out, AF.Sqrt, w, w)
        self.recip(out, out, w, w)


def make_staging(cx, ph, nkmax, cw, tag):
    return [ph.enter_context(cx.sbt(f"{tag}_st{i}", [128, nkmax, cw], F32)) for i in range(2)], cw, tag


def load_scaled_weight(cx, ph, dst, dst_key, w_ap, g_ap, nk, ncols, tag, staging):
    nc = cx.nc
    st, CW, stag = staging
    g = None
    if g_ap is not None:
        g = ph.enter_context(cx.sbt(f"{tag}_g", [128, nk], F32))
        cx.S.op("sp", lambda e: e.dma_start(out=g[:], in_=g_ap.rearrange("(k p) -> p k", p=128), allow_slow_non_contiguous=True),
                writes=[tag + "_g"], dma=True)
    wv = w_ap.rearrange("(k p) n -> p k n", p=128)
    i = 0
    for c0 in range(0, ncols, CW):
        cw = min(CW, ncols - c0)
        b = i % 2
        sk = f"{stag}_st{b}"
        cx.dma(st[b][:, 0:nk, :cw], wv[:, :, c0:c0 + cw], [], [sk])
        for k in range(nk):
            eng = ("dve", "pool")[k % 2]
            if g is not None:
                cx.ts(eng, dst[:, k, c0:c0 + cw], st[b][:, k, :cw], g[:, k:k + 1], None, ALU.mult, None, [sk, tag + "_g"], [dst_key])
            else:
                cx.cp(eng, dst[:, k, c0:c0 + cw], st[b][:, k, :cw], [sk], [dst_key])
        i += 1


def norm_transpose(cx, src, src_key, j, xs, xs_key, junk, ss, rs, pT, xT, xT_key, identb, tagk):
    cx.act(junk[:], src[:, j, :], AF.Square, [src_key], [tagk + "junk", tagk + "ss"], accum_out=ss[:, j:j + 1])
    cx.rstd(rs[:, j:j + 1], ss[:, j:j + 1], 1.0 / D, [tagk + "ss"], [tagk + "rs"], tagk)
    cx.act(xs[:, j, :], src[:, j, :], AF.Copy, [src_key, tagk + "rs"], [xs_key], scale=rs[:, j:j + 1])
    for k in range(8):
        cx.tr(pT[:, k, :], xs[:, j, k * 128:(k + 1) * 128], identb[:], [xs_key, "identb"], [tagk + "pT"])
    cx.cp(cx.ev(), xT[:, :, j * 128:(j + 1) * 128], pT[:], [tagk + "pT"], [xT_key])


FM_BLOCKS = [("a", 0, 0), ("a", 1, 128), ("a", 2, 256), ("a", 3, 384), ("u", 0, 768), ("u", 1, 896),
             ("d", 0, 1792), ("d", 1, 1920), ("d", 2, 2048), ("d", 3, 2176)]
TM_GROUPS = [(512, 256, 0), (1024, 512, 256), (1536, 256, 768), (2048, 512, 1024), (2560, 272, 1536)]


def phase_A(cx, layer):
    nc, S, T = cx.nc, cx.S, cx.T
    with contextlib.ExitStack() as ph:
        sb = lambda n, s, d: ph.enter_context(cx.sbt(n, s, d))
        ps = lambda n, s, d: ph.enter_context(cx.pst(n, s, d))
        wb = sb("A_wb", [128, 8, NIN], BF16)
        identb = sb("A_idb", [128, 128], BF16)
        cx.dma(identb[:], T["ident_b"][:, :], [], ["identb"])
        stg = make_staging(cx, ph, 8, 256, "Astg")
        load_scaled_weight(cx, ph, wb, "A_wb", T["w_in"][layer], T["norm_mix"][layer], 8, NIN, "Aw", stg)
        AB = sb("A_AB", [128, 2, 512], BF16)
        fw = sb("A_fw", [64, 4, 64], F32)
        csp = sb("A_csp", [64, 2, 2, 128], F32)
        pab = ps("A_pab", [128, 64], F32)
        cx.memset("pool", AB[:], 0.0, ["AB"])
        cx.dma(fw[:], T["fnet_w"][layer].rearrange("g j c -> j g c"), [], ["fw"])
        cx.dma(csp[:], T["cs64"][:, :, :, :], [], ["csp"])
        for g in range(4):
            for ab in range(2):
                cx.mm(pab[:], csp[:, ab, g % 2, :], fw[:, g, :], True, True, ["csp", "fw"], ["pab"])
                P = slice((g % 2) * 64, (g % 2) * 64 + 64)
                cx.cp("dve", AB[P, g // 2, g * 128 + ab * 64: g * 128 + ab * 64 + 64], pab[P, :], ["pab"], ["AB"])
        xin = [sb(f"A_xin{i}", [128, 4, D], F32) for i in range(2)]
        xs = sb("A_xs", [128, 4, D], BF16)
        xT = [sb(f"A_xT{i}", [128, 8, 512], BF16) for i in range(2)]
        junk = sb("A_junk", [128, D], BF16)
        ss = sb("A_ss", [128, 4], F32)
        rs = sb("A_rs", [128, 4], F32)
        ubT = sb("A_ubT", [128, 2, 512], BF16)
        fm = [sb(f"A_fm{i}", [128, 512], BF16) for i in range(3)]
        ztm = [sb(f"A_ztm{i}", [128, 4, 1792], BF16) for i in range(2)]
        zg = [sb(f"A_zg{i}", [128, 4, 16], F32) for i in range(2)]
        zf = [sb(f"A_zf{i}", [128, 4, 512], BF16) for i in range(2)]
        pT = ps("A_pT", [128, 8, 128], BF16)
        pf = [ps(f"A_pf{i}", [128, 512], F32) for i in range(2)]
        pm = [ps(f"A_pm{i}", [128, 512], F32) for i in range(2)]
        src = T["xs"] if layer == 0 else T["xres"]
        it = 0
        nfm = 0
        npm = 0
        for s in range(cx.nseq):
            for i in range(L // 512):
                t0 = i * 512
                b = it % 2
                it += 1
                xk, xtk, ztk, zgk, zfk = f"xin{b}", f"xT{b}", f"ztm{b}", f"zg{b}", f"zf{b}"
                cx.dma(xin[b][:], src[s, t0:t0 + 512, :].rearrange("(j p) d -> p j d", p=128), [], [xk])
                for j in range(4):
                    norm_transpose(cx, xin[b], xk, j, xs, "A_xs", junk, ss, rs, pT, xT[b], xtk, identb, "A")
                for (kind, bi, c0) in FM_BLOCKS:
                    p = pf[nfm % 2]
                    pk = f"pf{nfm % 2}"
                    for k in range(8):
                        cx.mm(p[:], wb[:, k, c0:c0 + 128], xT[b][:, k, :], k == 0, k == 7, ["A_wb", xtk], [pk])
                    if kind == "u":
                        cx.cp(cx.ev(), ubT[:, bi, :], p[:], [pk], ["ubT"])
                    else:
                        fb = nfm % 3
                        fk = f"fm{fb}"
                        cx.cp(cx.ev(), fm[fb][:], p[:], [pk], [fk])
                        dst = T["zT_a"] if kind == "a" else T["zT_d"]
                        cx.dma(dst[s, bi * 128:(bi + 1) * 128, t0:t0 + 512], fm[fb][:], [fk], [])
                    nfm += 1
                for j in range(4):
                    tj = slice(j * 128, (j + 1) * 128)
                    for (c0, cw, off) in TM_GROUPS:
                        p = pm[npm % 2]
                        pk = f"pm{npm % 2}"
                        npm += 1
                        for k in range(8):
                            cx.mm(p[:, :cw], xT[b][:, k, tj], wb[:, k, c0:c0 + cw], k == 0, k == 7, [xtk, "A_wb"], [pk])
                        if cw == 272:
                            cx.cp(cx.ev(), ztm[b][:, j, off:off + 256], p[:, :256], [pk], [ztk])
                            cx.cp(cx.ev(), zg[b][:, j, :], p[:, 256:272], [pk], [zgk])
                        else:
                            cx.cp(cx.ev(), ztm[b][:, j, off:off + cw], p[:, :cw], [pk], [ztk])
                    p = pm[npm % 2]
                    pk = f"pm{npm % 2}"
                    npm += 1
                    for ct in range(2):
                        cx.mm(p[:], ubT[:, ct, tj], AB[:, ct, :], ct == 0, ct == 1, ["ubT", "AB"], [pk])
                    cx.cp(cx.ev(), zf[b][:, j, :], p[:], [pk], [zfk])
                tv = lambda ap: ap[s, t0:t0 + 512, :].rearrange("(j p) c -> p j c", p=128)
                cx.dma(tv(T["z_va"]), ztm[b][:, :, 0:256], [ztk], [])
                cx.dma(T["z_hy"][s, 1 + t0:1 + t0 + 512, :].rearrange("(j p) c -> p j c", p=128), ztm[b][:, :, 256:1024], [ztk], [])
                cx.dma(tv(T["z_d"]), ztm[b][:, :, 1024:1792], [ztk], [])
                cx.dma(T["z_g"][s, :, 4 * i:4 * i + 4, :], zg[b][:], [zgk], [])
                cx.dma(tv(T["z_fn"]), zf[b][:], [zfk], [])
    S.barrier()


def phase_NA(cx, layer):
    nc, S, T = cx.nc, cx.S, cx.T
    with contextlib.ExitStack() as ph:
        sb = lambda n, s, d: ph.enter_context(cx.sbt(n, s, d))
        ps = lambda n, s, d: ph.enter_context(cx.pst(n, s, d))
        QT = sb("N_QT", [128, 2, L], BF16)
        KT = sb("N_KT", [128, 2, L], BF16)
        Ve = sb("N_Ve", [128, 64, 256], BF16)
        Vo = sb("N_Vo", [128, 63, 256], BF16)
        bias = sb("N_bias", [128, 16, 512], F32)
        identb = sb("N_idb", [128, 128], BF16)
        NB = 2
        qbd = [sb(f"N_qbd{i}", [128, 128], BF16) for i in range(NB)]
        ssb = [sb(f"N_s{i}", [128, 512], F32) for i in range(NB)]
        pb = [sb(f"N_p{i}", [128, 512], BF16) for i in range(NB)]
        pTs = [sb(f"N_pT{i}", [128, 4, 128], BF16) for i in range(NB)]
        mx = [sb(f"N_mx{i}", [128, 1], F32) for i in range(NB)]
        rsum = [sb(f"N_rs{i}", [128, 1], F32) for i in range(NB)]
        ybuf = [sb(f"N_yb{i}", [128, 8, 64], BF16) for i in range(4)]
        Sp = [ps(f"N_Sp{i}", [128, 512], F32) for i in range(2)]
        pTp = [ps(f"N_pTp{i}", [128, 4, 128], BF16) for i in range(2)]
        po = [ps(f"N_po{i}", [128, 128], F32) for i in range(2)]
        cx.dma(identb[:], T["ident_b"][:, :], [], ["identb"])
        cx.dma(bias[:], T["na_bias"][layer].rearrange("v p n -> p v n"), [], ["bias"])
        for i in range(NB):
            cx.memset("pool", qbd[i][:], 0.0, [f"qbd{i}"])
        u = 0
        for s in range(cx.nseq):
            for hp in range(2):
                cx.dma(QT[:, hp, :], T["zT_a"][s, hp * 128:(hp + 1) * 128, :], [], ["QT"])
                cx.dma(KT[:, hp, :], T["zT_a"][s, 256 + hp * 128:256 + (hp + 1) * 128, :], [], ["KT"])
            cx.dma(Ve[:], T["z_va"][s].rearrange("(n p) c -> p n c", p=128), [], ["Ve"])
            cx.dma(Vo[:], T["z_va"][s, 64:64 + 63 * 128, :].rearrange("(n p) c -> p n c", p=128), [], ["Vo"])
            for r in range(128):
                kr0 = min(max(r - 4, 0), 120)
                v = r if r < 4 else (4 if r <= 124 else r - 120)
                for hp in range(2):
                    b = u % NB
                    b2 = u % 2
                    u += 1
                    yb = ybuf[hp * 2 + (r // 8) % 2]
                    ybk = f"yb{hp * 2 + (r // 8) % 2}"
                    tq = slice(r * 64, (r + 1) * 64)
                    cx.cp("pool", qbd[b][0:64, 0:64], QT[0:64, hp, tq], ["QT"], [f"qbd{b}"])
                    cx.cp("pool", qbd[b][64:128, 64:128], QT[64:128, hp, tq], ["QT"], [f"qbd{b}"])
                    cx.mm(Sp[b2][:], qbd[b][:], KT[:, hp, kr0 * 64:kr0 * 64 + 512], True, True, [f"qbd{b}", "KT"], [f"Sp{b2}"])
                    cx.stt("dve", ssb[b][:], Sp[b2][:], 0.125, bias[:, v * 2 + hp, :], ALU.mult, ALU.add, [f"Sp{b2}", "bias"], [f"s{b}"])
                    cx.S.op("dve", lambda e, b=b: e.tensor_reduce(out=mx[b][:], in_=ssb[b][:], axis=AX.X, op=ALU.max), reads=[f"s{b}"], writes=[f"mx{b}"])
                    cx.ts("pool", mx[b][:], mx[b][:], -1.0, None, ALU.mult, None, [f"mx{b}"], [f"mx{b}"])
                    cx.act(pb[b][:], ssb[b][:], AF.Exp, [f"s{b}", f"mx{b}"], [f"p{b}", f"rs{b}"], bias=mx[b][:], accum_out=rsum[b][:])
                    for c in range(4):
                        cx.tr(pTp[b2][:, c, :], pb[b][:, c * 128:(c + 1) * 128], identb[:], [f"p{b}", "identb"], [f"pTp{b2}"])
                    cx.cp(cx.ev(), pTs[b][:], pTp[b2][:], [f"pTp{b2}"], [f"pT{b}"])
                    for c in range(4):
                        if kr0 % 2 == 0:
                            vv = Ve[:, kr0 // 2 + c, hp * 128:(hp + 1) * 128]
                            vk = "Ve"
                        else:
                            vv = Vo[:, (kr0 - 1) // 2 + c, hp * 128:(hp + 1) * 128]
                            vk = "Vo"
                        cx.mm(po[b2][:], pTs[b][:, c, :], vv, c == 0, c == 3, [f"pT{b}", vk], [f"po{b2}"])
                    cx.recip(rsum[b][:], rsum[b][:], [f"rs{b}"], [f"rs{b}"])
                    cx.act(yb[0:64, r % 8, :], po[b2][0:64, 0:64], AF.Copy, [f"po{b2}", f"rs{b}"], [ybk], scale=rsum[b][0:64, :])
                    cx.act(yb[64:128, r % 8, :], po[b2][64:128, 64:128], AF.Copy, [f"po{b2}", f"rs{b}"], [ybk], scale=rsum[b][64:128, :])
                    if r % 8 == 7:
                        r0 = r - 7
                        for h in range(2):
                            col = (2 * hp + h) * 64
                            cx.dma(T["y"][s, r0 * 64:(r0 + 8) * 64, col:col + 64].rearrange("(r q) d -> q r d", q=64),
                                   yb[h * 64:(h + 1) * 64, :, :], [ybk], [])
    S.barrier()


def phase_FN(cx, layer):
    nc, S, T = cx.nc, cx.S, cx.T
    with contextlib.ExitStack() as ph:
        sb = lambda n, s, d: ph.enter_context(cx.sbt(n, s, d))
        ps = lambda n, s, d: ph.enter_context(cx.pst(n, s, d))
        F1 = sb("F_F1", [64, 2, 128], BF16)
        T2c = sb("F_T2c", [128, 64, 128], BF16)
        T2s = sb("F_T2s", [128, 64, 128], BF16)
        uin = sb("F_uin", [64, 128, 256], BF16)
        Ysb = sb("F_Y", [128, 2, 64, 128], BF16)
        yfn = sb("F_yfn", [128, 64, 128], BF16)
        Yp = [ps(f"F_Yp{i}", [128, 4, 128], F32) for i in range(2)]
        Xp = [ps(f"F_Xp{i}", [128, 4, 128], F32) for i in range(2)]
        cx.dma(F1[:], T["f_F1"][:, :, :], [], ["F1"])
        cx.dma(T2c[:], T["f_T2c"][:, :, :], [], ["T2c"])
        cx.dma(T2s[:], T["f_T2s"][:, :, :], [], ["T2s"])
        n1 = 0
        n2 = 0
        for s in range(cx.nseq):
            for hf in range(2):
                cx.dma(uin[:], T["z_fn"][s, :, hf * 256:(hf + 1) * 256].rearrange("(a b) c -> a b c", b=128), [], ["uin"])
                for c0 in range(0, 128, 4):
                    p = Yp[n1 % 2]
                    pk = f"Yp{n1 % 2}"
                    n1 += 1
                    for cc in range(4):
                        ch = c0 + cc
                        g2, c_ = ch // 64, ch % 64
                        cx.mm(p[:, cc, :], uin[:, :, g2 * 128 + c_], F1[:, 0, :], True, False, ["uin", "F1"], [pk])
                        cx.mm(p[:, cc, :], uin[:, :, g2 * 128 + 64 + c_], F1[:, 1, :], False, True, ["uin", "F1"], [pk])
                    cx.cp(cx.ev(), Ysb[:, :, :, c0:c0 + 4], p[:].rearrange("p c (r k) -> p r k c", r=2), [pk], ["Ysb"])
                for k0 in range(0, 64, 4):
                    p = Xp[n2 % 2]
                    pk = f"Xp{n2 % 2}"
                    n2 += 1
                    for q in range(4):
                        k1 = k0 + q
                        cx.mm(p[:, q, :], T2c[:, k1, :], Ysb[:, 0, k1, :], True, False, ["T2c", "Ysb"], [pk])
                        cx.mm(p[:, q, :], T2s[:, k1, :], Ysb[:, 1, k1, :], False, True, ["T2s", "Ysb"], [pk])
                    cx.cp(cx.ev(), yfn[:, k0:k0 + 4, :], p[:], [pk], ["yfn"])
                cx.dma(T["y"][s, :, 256 + hf * 128:256 + (hf + 1) * 128].rearrange("(k2 k1) c -> k2 k1 c", k1=64), yfn[:], ["yfn"], [])
    S.barrier()


CB = 32


class HyTabs:
    pass


def hy_tables(cx, ph):
    nc, T = cx.nc, cx.T
    sb = lambda n, s, d: ph.enter_context(cx.sbt(n, s, d))
    ps = lambda n, s, d: ph.enter_context(cx.pst(n, s, d))
    H = HyTabs()
    H.F1 = sb("H_F1", [128, 195], BF16)
    H.T2c = sb("H_T2c", [128, 65, 128], BF16)
    H.T2s = sb("H_T2s", [128, 65, 128], BF16)
    H.GA = sb("H_GA", [128, 2, 256], BF16)
    H.TBc = sb("H_TBc", [65, 128, 64], BF16)
    H.TBs = sb("H_TBs", [65, 128, 64], BF16)
    for t, n in ((H.F1, "h_F1"), (H.T2c, "h_T2c"), (H.T2s, "h_T2s"), (H.GA, "h_GA"), (H.TBc, "h_TBc"), (H.TBs, "h_TBs")):
        cx.dma(t[:], T[n], [], ["H_tab"])
    H.Ysb = sb("H_Y", [128, 3, 65, CB], BF16)
    H.Xs = sb("H_Xs", [128, 65, 2, CB], BF16)
    H.Yp = [ps(f"H_Yp{i}", [128, 2, 195], F32) for i in range(2)]
    H.Xp = [ps(f"H_Xp{i}", [128, 8, 2, CB], F32) for i in range(2)]
    H.n1 = 0
    H.n2 = 0
    return H


def hy_fwd(cx, H, src, src_key, K1):
    for c0 in range(0, CB, 2):
        p = H.Yp[H.n1 % 2]
        pk = f"HYp{H.n1 % 2}"
        H.n1 += 1
        for cc in range(2):
            cx.mm(p[:, cc, :], src[0:K1, :, c0 + cc], H.F1[0:K1, :], True, True, [src_key, "H_tab"], [pk])
        cx.cp(cx.ev(), H.Ysb[:, :, :, c0:c0 + 2], p[:].rearrange("p c (r k) -> p r k c", r=3), [pk], ["HYsb"])
    for k0 in range(0, 65, 8):
        nk = min(8, 65 - k0)
        p = H.Xp[H.n2 % 2]
        pk = f"HXp{H.n2 % 2}"
        H.n2 += 1
        for q in range(nk):
            k1 = k0 + q
            cx.mm(p[:, q, 0, :], H.T2c[:, k1, :], H.Ysb[:, 0, k1, :], True, False, ["H_tab", "HYsb"], [pk])
            cx.mm(p[:, q, 0, :], H.T2s[:, k1, :], H.Ysb[:, 1, k1, :], False, True, ["H_tab", "HYsb"], [pk])
            cx.mm(p[:, q, 1, :], H.T2c[:, k1, :], H.Ysb[:, 1, k1, :], True, False, ["H_tab", "HYsb"], [pk])
            cx.mm(p[:, q, 1, :], H.T2s[:, k1, :], H.Ysb[:, 2, k1, :], False, True, ["H_tab", "HYsb"], [pk])
        cx.cp(cx.ev(), H.Xs[:, k0:k0 + nk, :, :], p[:, 0:nk, :, :], [pk], ["HXs"])


def wrap_sin(cx, a, tmp, ak, tk, P):
    pi = math.pi
    cx.ts("dve", tmp[0:P, :], a[0:P, :], pi, -2 * pi, ALU.is_gt, ALU.mult, [ak], [tk])
    cx.tt("dve", a[0:P, :], a[0:P, :], tmp[0:P, :], ALU.add, [ak, tk], [ak])
    cx.ts("dve", tmp[0:P, :], a[0:P, :], -pi, 2 * pi, ALU.is_lt, ALU.mult, [ak], [tk])
    cx.tt("dve", a[0:P, :], a[0:P, :], tmp[0:P, :], ALU.add, [ak, tk], [ak])
    cx.act(a[0:P, :], a[0:P, :], AF.Sin, [ak], [ak])


def phase_HYF(cx, layer):
    nc, S, T = cx.nc, cx.S, cx.T
    with contextlib.ExitStack() as ph:
        sb = lambda n, s, d: ph.enter_context(cx.sbt(n, s, d))
        ps = lambda n, s, d: ph.enter_context(cx.pst(n, s, d))
        w1 = sb("G_w1", [33, 64], F32)
        w2 = sb("G_w2", [64, 64], F32)
        w3 = sb("G_w3", [64, 1024], F32)
        sc = sb("G_sc", [64, 6], F32)
        dec = sb("G_dec", [1, 1024], F32)
        cx.dma(w1[:], T["hy_w1"][layer], [], ["w1"])
        cx.dma(w2[:], T["hy_w2"][layer], [], ["w2"])
        cx.dma(w3[:], T["hy_w3"][layer], [], ["w3"])
        cx.dma(dec[:], T["hy_decay"][layer].rearrange("o d c -> (o d c)").unsqueeze(0), [], ["dec"])
        cx.S.op("sp", lambda e: e.dma_start(out=sc[:, 0:1], in_=T["hy_b1"][layer].unsqueeze(1), allow_slow_non_contiguous=True), writes=["sc"], dma=True)
        cx.S.op("sp", lambda e: e.dma_start(out=sc[:, 1:2], in_=T["hy_b2"][layer].unsqueeze(1), allow_slow_non_contiguous=True), writes=["sc"], dma=True)
        cx.S.op("sp", lambda e: e.dma_start(out=sc[:, 2:4], in_=T["hy_freq"][layer].rearrange("t c -> c t"), allow_slow_non_contiguous=True), writes=["sc"], dma=True)
        cx.tt("dve", sc[:, 4:6], sc[:, 0:2], sc[:, 2:4], ALU.mult, ["sc"], ["sc"])
        fT = [sb(f"G_fT{i}", [33, 512], F32) for i in range(2)]
        tv = [sb(f"G_tv{i}", [1, 512], F32) for i in range(2)]
        a1 = sb("G_a1", [64, 512], F32)
        a2 = sb("G_a2", [64, 512], F32)
        tmp = sb("G_tmp", [64, 512], F32)
        win = [sb(f"G_win{i}", [128, 512], F32) for i in range(2)]
        ff = [sb(f"G_ff{i}", [128, 512], BF16) for i in range(2)]
        p1 = ps("G_p1", [64, 512], F32)
        p2 = ps("G_p2", [64, 512], F32)
        p3 = [ps(f"G_p3{i}", [128, 512], F32) for i in range(2)]
        pd = [ps(f"G_pd{i}", [128, 512], F32) for i in range(2)]
        w3v = w3[:].rearrange("p (o d c) -> p o d c", o=2, d=2)
        decv = dec[:].rearrange("p (o d c) -> p o d c", o=2, d=2)
        n = 0
        for i in range(2 * L // 512):
            R0 = i * 512
            d = 0 if R0 < L else 1
            b = i % 2
            cx.dma(fT[b][:], T["h_feats"][:, R0:R0 + 512], [], [f"fT{b}"])
            cx.dma(tv[b][:], T["h_tvec"][:, R0:R0 + 512], [], [f"tv{b}"])
            cx.mm(p1[:], w1[:], fT[b][:], True, True, ["w1", f"fT{b}"], ["p1"])
            cx.ts("dve", a1[:], p1[:], sc[:, 2:3], sc[:, 4:5], ALU.mult, ALU.add, ["p1", "sc"], ["a1"])
            wrap_sin(cx, a1, tmp, "a1", "tmp", 64)
            cx.mm(p2[:], w2[:], a1[:], True, True, ["w2", "a1"], ["p2"])
            cx.ts("dve", a2[:], p2[:], sc[:, 3:4], sc[:, 5:6], ALU.mult, ALU.add, ["p2", "sc"], ["a2"])
            wrap_sin(cx, a2, tmp, "a2", "tmp", 64)
            for j in range(4):
                q = n % 2
                n += 1
                cx.mm(p3[q][:].rearrange("p (o c) -> p o c", o=2), a2[:, j * 128:(j + 1) * 128], w3v[:, :, d, :], True, True, ["a2", "w3"], [f"p3{q}"])
                cx.mm(pd[q][:].rearrange("p (o c) -> p o c", o=2), tv[b][0:1, j * 128:(j + 1) * 128], decv[:, :, d, :], True, True, [f"tv{b}", "dec"], [f"pd{q}"])
                cx.act(win[q][:], pd[q][:], AF.Exp, [f"pd{q}"], [f"win{q}"], scale=-1.0)
                cx.tt("dve", ff[q][:], p3[q][:], win[q][:], ALU.mult, [f"p3{q}", f"win{q}"], [f"ff{q}"])
                if R0 + j * 128 == L:
                    cx.memset("dve", ff[q][0:1, :], 0.0, [f"ff{q}"])
                cx.dma(T["h_ftm"][R0 + j * 128:R0 + (j + 1) * 128, :], ff[q][:], [f"ff{q}"], ["h_ftm"])
    S.barrier()
    with contextlib.ExitStack() as ph:
        sb = lambda n, s, d: ph.enter_context(cx.sbt(n, s, d))
        ra = [sb(f"G_ra{i}", [128, 16, 512], BF16) for i in range(2)]
        rb = [sb(f"G_rb{i}", [128, 512 // CB, 16, CB], BF16) for i in range(2)]
        for q in range(8):
            b = q % 2
            cx.dma(ra[b][:], T["h_ftm"].rearrange("(a t) c -> a t c", t=128)[:, 16 * q:16 * q + 16, :], ["h_ftm"], [f"ra{b}"])
            cx.cp(("pool", "dve")[b], rb[b][:].rearrange("p b t c -> p t b c"), ra[b][:].rearrange("p t (b c) -> p t b c", c=CB), [f"ra{b}"], [f"rb{b}"])
            cx.dma(T["h_full"].rearrange("b (a t) c -> a b t c", t=128)[:, :, 16 * q:16 * q + 16, :], rb[b][:], [f"rb{b}"], ["h_full"])
    S.barrier()
    with contextlib.ExitStack() as ph:
        sb = lambda n, s, d: ph.enter_context(cx.sbt(n, s, d))
        H = hy_tables(cx, ph)
        fin = [sb(f"G_fin{i}", [128, 128, CB], BF16) for i in range(2)]
        skp = sb("G_skp", [128, 512], F32)
        KFt = [sb(f"G_KFt{i}", [128, 65, 2, CB], BF16) for i in range(2)]
        cx.dma(skp[:], T["hy_skip"][layer].rearrange("o c -> (o c)").partition_broadcast(128), [], ["skp"])
        for fb in range(16):
            b = fb % 2
            cx.dma(fin[b][:], T["h_full"][fb].rearrange("(a t) c -> a t c", t=128), ["h_full"], [f"fin{b}"])
            hy_fwd(cx, H, fin[b], f"fin{b}", 128)
            cx.tt("dve", KFt[b][:, :, 0, :], H.Xs[:, :, 0, :], skp[:, fb * CB:(fb + 1) * CB].unsqueeze(1).to_broadcast([128, 65, CB]),
                  ALU.add, ["HXs", "skp"], [f"KFt{b}"])
            cx.cp("pool", KFt[b][:, :, 1, :], H.Xs[:, :, 1, :], ["HXs"], [f"KFt{b}"])
            cx.dma(T["h_KF"][fb].rearrange("p (k r c) -> p k r c", k=65, r=2), KFt[b][:], [f"KFt{b}"], ["h_KF"])
    S.barrier()


def phase_HC(cx, layer):
    nc, S, T = cx.nc, cx.S, cx.T
    NBLK = 768 // CB
    with contextlib.ExitStack() as ph:
        sb = lambda n, s, d: ph.enter_context(cx.sbt(n, s, d))
        cw = sb("Y_cw", [64, 3, 768], F32)
        cb_ = sb("Y_cb", [64, 768], F32)
        cx.dma(cw[:], T["hy_conv_w"][layer].rearrange("t c -> (t c)").partition_broadcast(64), [], ["cw"])
        cx.dma(cb_[:], T["hy_conv_b"][layer].partition_broadcast(64), [], ["cb"])
        uin = [sb(f"Y_uin{i}", [64, 18, 768], BF16) for i in range(2)]
        t1 = sb("Y_t1", [64, 16, 768], BF16)
        t2 = sb("Y_t2", [64, 16, 768], BF16)
        sg = [sb(f"Y_sgp{i}", [64, NBLK, 16, CB], BF16) for i in range(2)]
        it = 0
        bc = lambda ap: ap.unsqueeze(1).to_broadcast([64, 16, 768])
        for s in range(cx.nseq):
            base = T["z_hy"][s]
            for q in range(8):
                b = it % 2
                it += 1
                uk, sk = f"uin{b}", f"sgp{b}"
                win_ap = bass.AP(base.tensor, base.offset + 16 * q * 768, [[128 * 768, 64], [768, 18], [1, 768]])
                cx.dma(uin[b][:], win_ap, [], [uk])
                cx.tt("pool", t1[:], uin[b][:, 0:16, :], bc(cw[:, 0, :]), ALU.mult, [uk, "cw"], ["t1"])
                cx.tt("dve", t2[:], uin[b][:, 1:17, :], bc(cw[:, 1, :]), ALU.mult, [uk, "cw"], ["t2"])
                cx.tt("pool", t1[:], t1[:], t2[:], ALU.add, ["t1", "t2"], ["t1"])
                cx.tt("dve", t2[:], uin[b][:, 2:18, :], bc(cw[:, 2, :]), ALU.mult, [uk, "cw"], ["t2"])
                cx.tt("pool", t1[:], t1[:], t2[:], ALU.add, ["t1", "t2"], ["t1"])
                cx.tt("dve", sg[b][:].rearrange("p b t c -> p t b c"), t1[:].rearrange("p t (b c) -> p t b c", c=CB),
                      bc(cb_[:, :]).rearrange("p t (b c) -> p t b c", c=CB), ALU.add, ["t1", "cb"], [sk])
                cx.dma(T["z_hc"][s].rearrange("b (a t) c -> a b t c", t=128)[:, :, 16 * q:16 * q + 16, :], sg[b][:], [sk], [])
    S.barrier()


def phase_HY(cx, layer):
    nc, S, T = cx.nc, cx.S, cx.T
    with contextlib.ExitStack() as ph:
        sb = lambda n, s, d: ph.enter_context(cx.sbt(n, s, d))
        ps = lambda n, s, d: ph.enter_context(cx.pst(n, s, d))
        H = hy_tables(cx, ph)
        sig1 = [sb(f"Y_sig{i}", [64, 128, CB], BF16) for i in range(3)]
        sig = [sig1, sig1]
        KF = [sb(f"Y_KF{i}", [128, 65, 2, CB], BF16) for i in range(2)]
        Z = sb("Y_Z", [128, 65, 2, CB], BF16)
        za = sb("Y_za", [128, 65, CB], BF16)
        zb = sb("Y_zb", [128, 65, CB], BF16)
        Vsb = sb("Y_V", [65, 2, 128, CB], BF16)
        y1 = sb("Y_y1", [64, 128, CB], BF16)
        yo = [sb(f"Y_yo{i}", [64, 128, CB], BF16) for i in range(2)]
        rl = sb("Y_rl", [64, 256 // CB, 16, CB], BF16)
        rl2 = sb("Y_rl2", [64, 16, 256 // CB, CB], BF16)
        Vp = [ps(f"Y_Vp{i}", [65, 2, 256], F32) for i in range(2)]
        yp = [ps(f"Y_yp{i}", [64, 16, CB], F32) for i in range(2)]
        nv = 0
        ny = 0
        nk = 0
        it = 0
        for s in range(cx.nseq):
            for cb in range(256 // CB):
                jb = it % 2
                it += 1
                for g in range(3):
                    blk = g * (256 // CB) + cb
                    cx.dma(sig[jb][g][:], T["z_hc"][s, blk].rearrange("(a t) c -> a t c", t=128), [], [f"sig{g}"])
                for o in range(2):
                    src, sk = (sig[jb][0], "sig0") if o == 0 else (y1, "y1")
                    gate, gk = (sig[jb][1], "sig1") if o == 0 else (sig[jb][2], "sig2")
                    dst, dk = (y1, "y1") if o == 0 else (yo[jb], f"yo{jb}")
                    kb = nk % 2
                    nk += 1
                    kfk = f"KF{kb}"
                    cx.dma(KF[kb][:], T["h_KF"][o * (256 // CB) + cb].rearrange("p (k r c) -> p k r c", k=65, r=2), ["h_KF"], [kfk])
                    hy_fwd(cx, H, src, sk, 64)
                    Xr, Xi, Kr, Ki = H.Xs[:, :, 0, :], H.Xs[:, :, 1, :], KF[kb][:, :, 0, :], KF[kb][:, :, 1, :]
                    cx.tt("pool", za[:], Xr, Kr, ALU.mult, ["HXs", kfk], ["za"])
                    cx.tt("dve", zb[:], Xi, Ki, ALU.mult, ["HXs", kfk], ["zb"])
                    cx.tt("dve", Z[:, :, 0, :], za[:], zb[:], ALU.subtract, ["za", "zb"], ["Z"])
                    cx.tt("pool", za[:], Xr, Ki, ALU.mult, ["HXs", kfk], ["za"])
                    cx.tt("dve", zb[:], Xi, Kr, ALU.mult, ["HXs", kfk], ["zb"])
                    cx.tt("pool", Z[:, :, 1, :], za[:], zb[:], ALU.add, ["za", "zb"], ["Z"])
                    for c0 in range(0, CB, 2):
                        p = Vp[nv % 2]
                        pk = f"Vp{nv % 2}"
                        nv += 1
                        for cc in range(2):
                            cx.mm(p[:, cc, :], Z[:, :, 0, c0 + cc], H.GA[:, 0, :], True, False, ["Z", "H_tab"], [pk])
                            cx.mm(p[:, cc, :], Z[:, :, 1, c0 + cc], H.GA[:, 1, :], False, True, ["Z", "H_tab"], [pk])
                        cx.cp(cx.ev(), Vsb[:, :, :, c0:c0 + 2], p[:].rearrange("p c (r t) -> p r t c", r=2), [pk], ["Vsb"])
                    for q0 in range(0, 128, 16):
                        p = yp[ny % 2]
                        pk = f"yp{ny % 2}"
                        ny += 1
                        for q in range(16):
                            tq = q0 + q
                            cx.mm(p[:, q, :], H.TBc[:, tq, :], Vsb[:, 0, tq, :], True, False, ["H_tab", "Vsb"], [pk])
                            cx.mm(p[:, q, :], H.TBs[:, tq, :], Vsb[:, 1, tq, :], False, True, ["H_tab", "Vsb"], [pk])
                        cx.tt("dve", dst[:, q0:q0 + 16, :], p[:], gate[:, q0:q0 + 16, :], ALU.mult, [pk, gk], [dk])
                cx.dma(T["y_hc"][s, cb].rearrange("(a t) c -> a t c", t=128), yo[jb][:], [f"yo{jb}"], ["y_hc"])
            for q in range(8):
                cx.dma(rl[:], T["y_hc"][s].rearrange("b (a t) c -> a b t c", t=128)[:, :, 16 * q:16 * q + 16, :], ["y_hc"], ["rl"])
                cx.cp("pool", rl2[:], rl[:].rearrange("p b t c -> p t b c"), ["rl"], ["rl2"])
                cx.dma(T["y"][s, :, 512:768].rearrange("(a t) c -> a t c", t=128)[:, 16 * q:16 * q + 16, :],
                       rl2[:].rearrange("p t b c -> p t (b c)"), ["rl2"], [])
    S.barrier()


def phase_ML(cx, layer):
    nc, S, T = cx.nc, cx.S, cx.T
    with contextlib.ExitStack() as ph:
        sb = lambda n, s, d: ph.enter_context(cx.sbt(n, s, d))
        ps = lambda n, s, d: ph.enter_context(cx.pst(n, s, d))
        qT = sb("M_qT", [128, 2, L], BF16)
        kT = sb("M_kT", [128, 2, L], BF16)
        hO = [sb(f"M_h{d}", [128, 64, 256], BF16) for d in range(2)]
        gts = sb("M_gts", [128, 64, 16], F32)
        gb = sb("M_gb", [128, 16], F32)
        tri = sb("M_tri", [128, 2, 128], F32)
        ones = sb("M_ones", [128, 128], F32)
        nbc = sb("M_nbc", [128, 256], F32)
        lfn = [sb(f"M_lfn{d}", [128, 64, 4], F32) for d in range(2)]
        eb = [sb(f"M_eb{d}", [128, 64, 4], F32) for d in range(2)]
        rr = [sb(f"M_rr{d}", [128, 64, 4], F32) for d in range(2)]
        eg = [sb(f"M_eg{d}", [128, 64, 4], F32) for d in range(2)]
        S32 = [sb(f"M_S32{d}", [128, 2, 80], F32) for d in range(2)]
        S16 = [sb(f"M_S16{d}", [128, 2, 80], BF16) for d in range(2)]
        NB = 3
        kv = [[sb(f"M_kv{d}{i}", [128, 512], BF16) for i in range(NB)] for d in range(2)]
        vt = [[sb(f"M_vt{d}{i}", [128, 4, 80], BF16) for i in range(NB)] for d in range(2)]
        PT = [[sb(f"M_PT{d}{i}", [128, 4, 128], BF16) for i in range(NB)] for d in range(2)]
        nd = [sb(f"M_nd{d}", [128, 4, 80], F32) for d in range(2)]
        den = [sb(f"M_den{d}", [128, 4, 1], F32) for d in range(2)]
        ob = sb("M_ob", [128, 8, 256], BF16)
        sg = sb("M_sg", [128, 8, 256], F32)
        hs = sb("M_hs", [128, 8, 256], F32)
        yd = sb("M_yd", [128, 8, 256], BF16)
        pc = ps("M_pc", [128, 256], F32)
        Gp = [ps(f"M_Gp{d}", [128, 4, 128], F32) for d in range(2)]
        nump = [ps(f"M_num{d}", [128, 4, 80], F32) for d in range(2)]
        dSp = [ps(f"M_dS{d}", [128, 4, 80], F32) for d in range(2)]
        cx.dma(tri[:], T["m_tri"].rearrange("t s c -> s t c"), [], ["tri"])
        for d in range(2):
            for i in range(NB):
                cx.memset("pool", vt[d][i][:], 0.0, [f"vt{d}{i}"])
        cx.memset("pool", ones[:], 1.0, ["ones"])
        cx.dma(gb[:], T["ml_gate_b"][layer].rearrange("a b -> (a b)").partition_broadcast(128), [], ["gb"])
        for s in range(cx.nseq):
            for pr in range(2):
                cx.dma(qT[:, pr, :], T["zT_d"][s, pr * 128:(pr + 1) * 128, :], [], ["qT"])
                cx.dma(kT[:, pr, :], T["zT_d"][s, 256 + pr * 128:256 + (pr + 1) * 128, :], [], ["kT"])
            cx.dma(gts[:], T["z_g"][s], [], ["gts"])
            cx.tt("dve", gts[:], gts[:], gb[:].unsqueeze(1).to_broadcast([128, 64, 16]), ALU.add, ["gts", "gb"], ["gts"])
            import os
            STG = int(os.environ.get("ML_STAGE", "9"))
            for d in range(2 if STG not in (-2, -3) else 0):
                if STG == -5:
                    cx.act(lfn[d][:], gts[:, :, 4 + 8 * d:8 + 8 * d], AF.Exp, ["gts"], [f"lfn{d}"], scale=-1.0)
                    continue
                fcol = slice(4 + 8 * d, 8 + 8 * d)
                icol = slice(8 * d, 4 + 8 * d)
                cx.act(lfn[d][:], gts[:, :, fcol], AF.Exp, ["gts"], [f"lfn{d}"], scale=-1.0)
                cx.act(lfn[d][:], lfn[d][:], AF.Ln, [f"lfn{d}", "ones"], [f"lfn{d}"], bias=ones[:, 0:1])
                if STG == -4:
                    continue
                lv = lfn[d][:].rearrange("p n h -> p (n h)")
                cx.mm(pc[:], tri[:, d, :], lv, True, True, ["tri", f"lfn{d}"], ["pc"])
                cx.act(eb[d][:].rearrange("p n h -> p (n h)"), pc[:], AF.Exp, ["pc"], [f"eb{d}"], scale=-1.0)
                if STG == -6:
                    continue
                cx.act(nbc[:], pc[:], AF.Copy, ["pc"], ["nbc"])
                cx.tt("dve", rr[d][:], nbc[:].rearrange("p (n h) -> p n h", h=4), gts[:, :, icol], ALU.add, ["nbc", "gts"], [f"rr{d}"])
                cx.act(rr[d][:], rr[d][:], AF.Exp, [f"rr{d}"], [f"rr{d}"])
                cx.ts("dve", rr[d][:], rr[d][:], 0.125, None, ALU.mult, None, [f"rr{d}"], [f"rr{d}"])
                if STG == -7:
                    continue
                cx.mm(pc[:], ones[:], lv, True, True, ["ones", f"lfn{d}"], ["pc"])
                cx.act(eg[d][:].rearrange("p n h -> p (n h)"), pc[:], AF.Exp, ["pc"], [f"eg{d}"], scale=-1.0)
                cx.memset("pool", S32[d][:], 0.0, [f"S32{d}"])
                cx.memset("pool", S16[d][:], 0.0, [f"S16{d}"])
            import os
            STG = int(os.environ.get("ML_STAGE", "9"))
            for i in range(64 if STG >= 1 else 0):
                for d in range(2):
                    n = i if d == 0 else 63 - i
                    b = i % NB
                    tk = slice(n * 128, (n + 1) * 128)
                    kvk, vtk, ptk = f"kv{d}{b}", f"vt{d}{b}", f"PT{d}{b}"
                    cx.dma(kv[d][b][:], T["z_d"][s, tk, 0:512], [], [kvk])
                    for h in (0, 2, 1, 3):
                        P = slice((h % 2) * 64, (h % 2) * 64 + 64)
                        cx.mm(Gp[d][:, h, :], kT[P, h // 2, tk], qT[P, h // 2, tk], True, True, ["kT", "qT"], [f"Gp{d}"])
                    if STG < 2:
                        continue
                    cx.tt("dve", PT[d][b][:], Gp[d][:], tri[:, d, :].unsqueeze(1).to_broadcast([128, 4, 128]), ALU.mult, [f"Gp{d}", "tri"], [ptk])
                    rv = rr[d][:, n, :].unsqueeze(2)
                    cx.tt("dve", vt[d][b][:, :, 0:64], kv[d][b][:, 256:512].rearrange("p (h e) -> p h e", h=4), rv.to_broadcast([128, 4, 64]), ALU.mult, [kvk, f"rr{d}"], [vtk])
                    cx.cp("act", vt[d][b][:, :, 64:65], rv, [f"rr{d}"], [vtk])
                    for h in (0, 2, 1, 3):
                        P = slice((h % 2) * 64, (h % 2) * 64 + 64)
                        cx.mm(nump[d][:, h, :], PT[d][b][:, h, :], vt[d][b][:, h, :], True, False, [ptk, vtk], [f"num{d}"])
                        cx.mm(nump[d][:, h, :], qT[P, h // 2, tk], S16[d][P, h // 2, :], False, True, ["qT", f"S16{d}"], [f"num{d}"])
                    if STG < 3:
                        continue
                    for h in (0, 2, 1, 3):
                        pr = h // 2
                        cx.mm(dSp[d][:, h, :], kv[d][b][:, pr * 128:(pr + 1) * 128], vt[d][b][:, h, :], True, True, [kvk, vtk], [f"dS{d}"])
                    for h2 in range(2):
                        P = slice(h2 * 64, h2 * 64 + 64)
                        dsv = dSp[d][P, :, :].rearrange("p (a b) e -> p a b e", b=2)[:, :, h2, :]
                        egv = eg[d][P, n, :].rearrange("p (a b) -> p a b", b=2)[:, :, h2].unsqueeze(2).to_broadcast([64, 2, 80])
                        cx.tt("dve", S32[d][P], S32[d][P], dsv, ALU.add, [f"S32{d}", f"dS{d}"], [f"S32{d}"])
                        cx.tt("dve", S32[d][P], S32[d][P], egv, ALU.mult, [f"S32{d}", f"eg{d}"], [f"S32{d}"])
                        cx.cp("act", S16[d][P], S32[d][P], [f"S32{d}"], [f"S16{d}"])
                    if STG < 4:
                        continue
                    cx.tt("dve", nd[d][:], nump[d][:], eb[d][:, n, :].unsqueeze(2).to_broadcast([128, 4, 80]), ALU.mult, [f"num{d}", f"eb{d}"], [f"nd{d}"])
                    cx.act(den[d][:], nd[d][:, :, 64:65], AF.Abs, [f"nd{d}"], [f"den{d}"])
                    cx.ts("dve", den[d][:], den[d][:], 1.0, None, ALU.max, None, [f"den{d}"], [f"den{d}"])
                    cx.recip(den[d][:], den[d][:], [f"den{d}"], [f"den{d}"])
                    cx.tt("dve", hO[d][:, n, :].rearrange("p (h e) -> p h e", h=4), nd[d][:, :, 0:64], den[d][:].to_broadcast([128, 4, 64]), ALU.mult,
                          [f"nd{d}", f"den{d}"], [f"hO{d}"])
            for n0 in range(0, 64 if STG not in (-1, -3, -4, -5, -6, -7) else 0, 8):
                cx.dma(ob[:], T["z_d"][s, n0 * 128:(n0 + 8) * 128, 512:768].rearrange("(n p) c -> p n c", p=128), [], ["ob"])
                cx.act(sg[:], ob[:], AF.Sigmoid, ["ob"], ["sg"])
                cx.tt("dve", hs[:], hO[0][:, n0:n0 + 8, :], hO[1][:, n0:n0 + 8, :], ALU.add, ["hO0", "hO1"], ["hs"])
                cx.tt("dve", yd[:], sg[:], hs[:], ALU.mult, ["sg", "hs"], ["yd"])
                cx.dma(T["y"][s, n0 * 128:(n0 + 8) * 128, 768:1024].rearrange("(n p) c -> p n c", p=128), yd[:], ["yd"], [])
    S.barrier()


def phase_C1(cx, layer):
    nc, S, T = cx.nc, cx.S, cx.T
    with contextlib.ExitStack() as ph:
        sb = lambda n, s, d: ph.enter_context(cx.sbt(n, s, d))
        ps = lambda n, s, d: ph.enter_context(cx.pst(n, s, d))
        wo = sb("C_wo", [128, 8, D], BF16)
        wg = sb("C_wg", [128, 8, DFF], BF16)
        wu = sb("C_wu", [128, 8, DFF], BF16)
        identb = sb("C_idb", [128, 128], BF16)
        cx.dma(identb[:], T["ident_b"][:, :], [], ["identb"])
        stg = make_staging(cx, ph, 8, 256, "Cstg")
        load_scaled_weight(cx, ph, wo, "C_wo", T["w_out"][layer], T["out_norm"][layer], 8, D, "Cwo", stg)
        load_scaled_weight(cx, ph, wg, "C_wg", T["w_gate"][layer], T["norm_ffn"][layer], 8, DFF, "Cwg", stg)
        load_scaled_weight(cx, ph, wu, "C_wu", T["w_up"][layer], T["norm_ffn"][layer], 8, DFF, "Cwu", stg)
        yin = sb("C_yin", [128, 4, D], BF16)
        xin = sb("C_xin", [128, 4, D], F32)
        sq = sb("C_sq", [128, D], F32)
        ssh = sb("C_ssh", [128, 16], F32)
        yT = sb("C_yT", [128, 8, 512], BF16)
        xs = sb("C_xs", [128, 4, D], BF16)
        junk = sb("C_junk", [128, D], BF16)
        ss = sb("C_ss", [128, 4], F32)
        rs = sb("C_rs", [128, 4], F32)
        hT = sb("C_hT", [128, 8, 512], BF16)
        sgt = [sb(f"C_sg{i}", [128, 512], F32) for i in range(2)]
        hh = [sb(f"C_hh{i}", [128, 512], BF16) for i in range(3)]
        pT = ps("C_pT", [128, 8, 128], BF16)
        pm = [ps(f"C_pm{i}", [128, 512], F32) for i in range(2)]
        pg = [ps(f"C_pg{i}", [128, 512], F32) for i in range(2)]
        pu = [ps(f"C_pu{i}", [128, 512], F32) for i in range(2)]
        src = T["xs"] if layer == 0 else T["xres"]
        npm = 0
        nf = 0
        for s in range(cx.nseq):
            for i in range(L // 512):
                t0 = i * 512
                tv = lambda ap: ap[s, t0:t0 + 512, :].rearrange("(j p) d -> p j d", p=128)
                cx.dma(yin[:], tv(T["y"]), [], ["yin"])
                cx.dma(xin[:], tv(src), [], ["xin"])
                for j in range(4):
                    cx.tt("pool", sq[:], yin[:, j, :], yin[:, j, :], ALU.mult, ["yin"], ["sq"])
                    cx.S.op("dve", lambda e: e.tensor_reduce(out=ssh[:], in_=sq[:].rearrange("p (g c) -> p g c", c=64), axis=AX.X, op=ALU.add),
                            reads=["sq"], writes=["ssh"])
                    cx.rstd(ssh[:], ssh[:], 1.0 / 64, ["ssh"], ["ssh"], "C")
                    yv = yin[:, j, :].rearrange("p (g c) -> p g c", c=64)
                    cx.tt("dve", yv, yv, ssh[:].unsqueeze(2).to_broadcast([128, 16, 64]), ALU.mult, ["yin", "ssh"], ["yin"])
                    for k in range(8):
                        cx.tr(pT[:, k, :], yin[:, j, k * 128:(k + 1) * 128], identb[:], ["yin", "identb"], ["CpT"])
                    cx.cp(cx.ev(), yT[:, :, j * 128:(j + 1) * 128], pT[:], ["CpT"], ["yT"])
                for j in range(4):
                    tj = slice(j * 128, (j + 1) * 128)
                    for c0 in (0, 512):
                        p = pm[npm % 2]
                        pk = f"Cpm{npm % 2}"
                        npm += 1
                        for k in range(8):
                            cx.mm(p[:], yT[:, k, tj], wo[:, k, c0:c0 + 512], k == 0, k == 7, ["yT", "C_wo"], [pk])
                        cx.tt("dve", xin[:, j, c0:c0 + 512], p[:], xin[:, j, c0:c0 + 512], ALU.add, [pk, "xin"], ["xin"])
                cx.dma(tv(T["x1"]), xin[:], ["xin"], [])
                for j in range(4):
                    norm_transpose(cx, xin, "xin", j, xs, "C_xs", junk, ss, rs, pT, hT, "hT", identb, "C")
                for f in range(DFF // 128):
                    q = nf % 2
                    q3 = nf % 3
                    nf += 1
                    fs = slice(f * 128, (f + 1) * 128)
                    for k in range(8):
                        cx.mm(pg[q][:], wg[:, k, fs], hT[:, k, :], k == 0, k == 7, ["C_wg", "hT"], [f"pg{q}"])
                    for k in range(8):
                        cx.mm(pu[q][:], wu[:, k, fs], hT[:, k, :], k == 0, k == 7, ["C_wu", "hT"], [f"pu{q}"])
                    cx.act(sgt[q][:], pg[q][:], AF.Silu, [f"pg{q}"], [f"sg{q}"])
                    cx.tt("dve", hh[q3][:], pu[q][:], sgt[q][:], ALU.mult, [f"pu{q}", f"sg{q}"], [f"hh{q3}"])
                    cx.dma(T["hscr"][s, fs, t0:t0 + 512], hh[q3][:], [f"hh{q3}"], [])
    S.barrier()


def phase_C2(cx, layer):
    nc, S, T = cx.nc, cx.S, cx.T
    last = layer == DEPTH - 1
    with contextlib.ExitStack() as ph:
        sb = lambda n, s, d: ph.enter_context(cx.sbt(n, s, d))
        ps = lambda n, s, d: ph.enter_context(cx.pst(n, s, d))
        wd = sb("E_wd", [128, 22, D], BF16)
        wpg = sb("E_wpg", [128, 8, D], BF16)
        wpp = sb("E_wpp", [128, 2, D], BF16)
        identb = sb("E_idb", [128, 128], BF16)
        gfin = sb("E_gfin", [128, D], F32)
        cx.dma(identb[:], T["ident_b"][:, :], [], ["identb"])
        cx.dma(gfin[:], T["final_norm"].partition_broadcast(128), [], ["gfin"])
        stg = make_staging(cx, ph, 22, 128, "Estg")
        load_scaled_weight(cx, ph, wd, "E_wd", T["w_down"][layer], None, 22, D, "Ewd", stg)
        load_scaled_weight(cx, ph, wpg, "E_wpg", T["w_ple_gate"][layer], T["ple_norm"][layer], 8, D, "Ewpg", stg)
        load_scaled_weight(cx, ph, wpp, "E_wpp", T["w_ple_proj"][layer], None, 2, D, "Ewpp", stg)
        hT = [sb(f"E_hT{i}", [128, 22, 512], BF16) for i in range(2)]
        x1 = [sb(f"E_x1{i}", [128, 4, D], F32) for i in range(2)]
        pin = sb("E_pin", [128, 4, 256], F32)
        pbf = sb("E_pbf", [128, 4, 256], BF16)
        pTs = sb("E_pTs", [128, 2, 512], BF16)
        xs = sb("E_xs", [128, 4, D], BF16)
        junk = sb("E_junk", [128, D], BF16)
        ss = sb("E_ss", [128, 4], F32)
        rs = sb("E_rs", [128, 4], F32)
        xT = sb("E_xT", [128, 8, 512], BF16)
        sgt = [sb(f"E_sg{i}", [128, 512], F32) for i in range(2)]
        pT = ps("E_pT", [128, 8, 128], BF16)
        pm = [ps(f"E_pm{i}", [128, 512], F32) for i in range(2)]
        pg = [ps(f"E_pg{i}", [128, 512], F32) for i in range(2)]
        pp = [ps(f"E_pp{i}", [128, 512], F32) for i in range(2)]
        npm = 0
        ng = 0
        it = 0
        for s in range(cx.nseq):
            for i in range(L // 512):
                t0 = i * 512
                b = it % 2
                it += 1
                hk, xk = f"EhT{b}", f"Ex1{b}"
                tv = lambda ap: ap[s, t0:t0 + 512, :].rearrange("(j p) d -> p j d", p=128)
                cx.dma(hT[b][:], T["hscr"][s, :, t0:t0 + 512].rearrange("(f p) t -> p f t", p=128), [], [hk])
                cx.dma(x1[b][:], tv(T["x1"]), [], [xk])
                cx.dma(pin[:], T["ps"][layer, s, t0:t0 + 512, :].rearrange("(j p) d -> p j d", p=128), [], ["pin"])
                cx.cp("pool", pbf[:], pin[:], ["pin"], ["pbf"])
                for j in range(4):
                    tj = slice(j * 128, (j + 1) * 128)
                    for c0 in (0, 512):
                        p = pm[npm % 2]
                        pk = f"Epm{npm % 2}"
                        npm += 1
                        for f in range(22):
                            cx.mm(p[:], hT[b][:, f, tj], wd[:, f, c0:c0 + 512], f == 0, f == 21, [hk, "E_wd"], [pk])
                        cx.tt("dve", x1[b][:, j, c0:c0 + 512], p[:], x1[b][:, j, c0:c0 + 512], ALU.add, [pk, xk], [xk])
                for j in range(4):
                    norm_transpose(cx, x1[b], xk, j, xs, "E_xs", junk, ss, rs, pT, xT, "ExT", identb, "E")
                    for k in range(2):
                        cx.tr(pT[:, k, :], pbf[:, j, k * 128:(k + 1) * 128], identb[:], ["pbf", "identb"], ["EpT"])
                    cx.cp(cx.ev(), pTs[:, :, j * 128:(j + 1) * 128], pT[:, 0:2, :], ["EpT"], ["pTs"])
                for j in range(4):
                    tj = slice(j * 128, (j + 1) * 128)
                    for c0 in (0, 512):
                        q = ng % 2
                        ng += 1
                        for k in range(8):
                            cx.mm(pg[q][:], xT[:, k, tj], wpg[:, k, c0:c0 + 512], k == 0, k == 7, ["ExT", "E_wpg"], [f"Epg{q}"])
                        for k in range(2):
                            cx.mm(pp[q][:], pTs[:, k, tj], wpp[:, k, c0:c0 + 512], k == 0, k == 1, ["pTs", "E_wpp"], [f"Epp{q}"])
                        cx.act(sgt[q][:], pg[q][:], AF.Sigmoid, [f"Epg{q}"], [f"Esg{q}"])
                        cx.tt("dve", sgt[q][:], pp[q][:], sgt[q][:], ALU.mult, [f"Epp{q}", f"Esg{q}"], [f"Esg{q}"])
                        cx.tt("pool", x1[b][:, j, c0:c0 + 512], x1[b][:, j, c0:c0 + 512], sgt[q][:], ALU.add, [xk, f"Esg{q}"], [xk])
                if not last:
                    cx.dma(tv(T["xres"]), x1[b][:], [xk], [])
                else:
                    for j in range(4):
                        cx.act(junk[:], x1[b][:, j, :], AF.Square, [xk], ["Ejunk", "Ess"], accum_out=ss[:, j:j + 1])
                        cx.rstd(rs[:, j:j + 1], ss[:, j:j + 1], 1.0 / D, ["Ess"], ["Ers"], "E")
                        cx.stt("dve", x1[b][:, j, :], x1[b][:, j, :], rs[:, j:j + 1], gfin[:], ALU.mult, ALU.mult, [xk, "Ers", "gfin"], [xk])
                    cx.dma(tv(T["out"]), x1[b][:], [xk], [])
    S.barrier()


_CONSTS = None


def bf(a):
    return np.asarray(a, dtype=np.float32).astype(ml_dtypes.bfloat16)


def host_consts():
    global _CONSTS
    if _CONSTS is not None:
        return _CONSTS
    c = {}
    pi2 = 2.0 * np.pi
    c["ident_b"] = bf(np.eye(128))
    m = np.arange(64)
    ang = pi2 * np.outer(m, m) / 64.0
    sc = 1.0 / math.sqrt(L * 64.0)
    cs = np.zeros((64, 2, 2, 128), np.float32)
    for ab, mat in enumerate((np.cos(ang) * sc, np.sin(ang) * sc)):
        cs[:, ab, 0, 0:64] = mat
        cs[:, ab, 1, 64:128] = mat
    c["cs64"] = cs
    l1 = np.arange(64)[:, None]
    k1 = np.arange(64)[None, :]
    th = pi2 * l1 * k1 / 64.0
    F1 = np.zeros((64, 2, 128))
    F1[:, 0, 0:64] = np.cos(th)
    F1[:, 0, 64:128] = -np.sin(th)
    F1[:, 1, 0:64] = -np.sin(th)
    F1[:, 1, 64:128] = -np.cos(th)
    c["f_F1"] = bf(F1)
    l2 = np.arange(128)[:, None, None]
    k1 = np.arange(64)[None, :, None]
    k2 = np.arange(128)[None, None, :]
    ph = pi2 * (((k1 + 64 * k2) * l2) % L) / L
    c["f_T2c"] = bf(np.cos(ph))
    c["f_T2s"] = bf(np.sin(ph))
    l1 = np.arange(128)[:, None]
    k1 = np.arange(65)[None, :]
    th = pi2 * l1 * k1 / 128.0
    c["h_F1"] = bf(np.concatenate([np.cos(th), -np.sin(th), -np.cos(th)], axis=1))
    l2 = np.arange(128)[:, None, None]
    k1 = np.arange(65)[None, :, None]
    k2 = np.arange(128)[None, None, :]
    ph = pi2 * (((k1 + 128 * k2) * l2) % NFFT) / NFFT
    c["h_T2c"] = bf(np.cos(ph))
    c["h_T2s"] = bf(np.sin(ph))
    k2 = np.arange(128)[:, None]
    t2 = np.arange(128)[None, :]
    psi = pi2 * k2 * t2 / 128.0
    GA = np.zeros((128, 2, 256))
    GA[:, 0, 0:128] = np.cos(psi)
    GA[:, 0, 128:256] = np.sin(psi)
    GA[:, 1, 0:128] = -np.sin(psi)
    GA[:, 1, 128:256] = np.cos(psi)
    c["h_GA"] = bf(GA)
    k1 = np.arange(65)[:, None, None]
    t2 = np.arange(128)[None, :, None]
    t1 = np.arange(64)[None, None, :]
    om = pi2 * ((k1 * (128 * t1 + t2)) % NFFT) / NFFT
    wk = np.where((k1 == 0) | (k1 == 64), 1.0, 2.0) / NFFT
    c["h_TBc"] = bf(wk * np.cos(om))
    c["h_TBs"] = bf(-wk * np.sin(om))
    R = np.arange(2 * L)
    pos = np.where(R < L, R, 2 * L - R)
    pos[L] = 0
    s = pos.astype(np.float32)
    t = (s / np.float32(L - 1)).astype(np.float32)
    angp = (np.float32(2.0 * math.pi / L) * s).astype(np.float32)
    bands = np.linspace(1e-4, 15.0, 16, dtype=np.float32)
    fb = angp[:, None] * bands[None, :]
    feats = np.concatenate([t[:, None], np.cos(fb), -np.sin(fb)], axis=-1).astype(np.float32)
    c["h_feats"] = np.ascontiguousarray(feats.T)
    c["h_tvec"] = np.ascontiguousarray(t[None, :])
    si = np.arange(128)[:, None]
    ci = np.arange(128)[None, :]
    c["m_tri"] = np.stack([(si <= ci), (si >= ci)]).astype(np.float32)
    idx = np.zeros((8, 2, 128, 512, 3), np.int64)
    msk = np.zeros((8, 2, 128, 512), bool)
    reps = [0, 1, 2, 3, 64, 125, 126, 127]
    for v, r in enumerate(reps):
        kr0 = min(max(r - 4, 0), 120)
        for hp in range(2):
            for h in range(2):
                for q in range(64):
                    kc0 = min(max(q - 8, 0), 48)
                    for j in range(8):
                        dr = kr0 + j - r + 7
                        kc = np.arange(kc0, kc0 + 16)
                        n = j * 64 + kc
                        idx[v, hp, h * 64 + q, n, 0] = 2 * hp + h
                        idx[v, hp, h * 64 + q, n, 1] = dr
                        idx[v, hp, h * 64 + q, n, 2] = kc - q + 15
                        msk[v, hp, h * 64 + q, n] = True
    c["_na_idx"] = idx
    c["_na_msk"] = msk
    _CONSTS = c
    return c


CONST_SPECS = [("ident_b", [128, 128], BF16), ("cs64", [64, 2, 2, 128], F32), ("f_F1", [64, 2, 128], BF16),
               ("f_T2c", [128, 64, 128], BF16), ("f_T2s", [128, 64, 128], BF16), ("h_F1", [128, 195], BF16),
               ("h_T2c", [128, 65, 128], BF16), ("h_T2s", [128, 65, 128], BF16), ("h_GA", [128, 2, 256], BF16),
               ("h_TBc", [65, 128, 64], BF16), ("h_TBs", [65, 128, 64], BF16), ("h_feats", [33, 2 * L], F32),
               ("h_tvec", [1, 2 * L], F32), ("m_tri", [2, 128, 128], F32)]

WEIGHT_SPECS = [("norm_mix", [DEPTH, D]), ("w_in", [DEPTH, D, NIN]), ("fnet_w", [DEPTH, 4, 64, 64]),
                ("hy_conv_w", [DEPTH, 3, 768]), ("hy_conv_b", [DEPTH, 768]), ("hy_w1", [DEPTH, 33, 64]), ("hy_b1", [DEPTH, 64]),
                ("hy_freq", [DEPTH, 2, 64]), ("hy_w2", [DEPTH, 64, 64]), ("hy_b2", [DEPTH, 64]), ("hy_w3", [DEPTH, 64, 1024]),
                ("hy_decay", [DEPTH, 2, 2, 256]), ("hy_skip", [DEPTH, 2, 256]), ("ml_gate_b", [DEPTH, 4, 4]),
                ("out_norm", [DEPTH, D]), ("w_out", [DEPTH, D, D]), ("norm_ffn", [DEPTH, D]), ("w_gate", [DEPTH, D, DFF]),
                ("w_up", [DEPTH, D, DFF]), ("w_down", [DEPTH, DFF, D]), ("ple_norm", [DEPTH, D]), ("w_ple_gate", [DEPTH, D, D]),
                ("w_ple_proj", [DEPTH, 256, D]), ("final_norm", [D])]


def build(nseq, debug=False, phases=None, scratch_in=()):
    nc = bass.Bass("TRN2", target_bir_lowering=False)
    T = {}

    def din(name, shape, dt=F32):
        T[name] = nc.dram_tensor(name, list(shape), dt, kind="ExternalInput").ap()

    def dscr(name, shape, dt):
        kind = "ExternalInput" if name in scratch_in else ("ExternalOutput" if debug else "Internal")
        T[name] = nc.dram_tensor(name, list(shape), dt, kind=kind).ap()

    din("xs", [nseq, L, D])
    din("ps", [DEPTH, nseq, L, 256])
    for n, shp in WEIGHT_SPECS:
        din(n, shp)
    din("na_bias", [DEPTH, 16, 128, 512])
    for n, shp, dt in CONST_SPECS:
        din(n, shp, dt)
    T["out"] = nc.dram_tensor("out", [nseq, L, D], F32, kind="ExternalOutput").ap()
    dscr("zT_a", [nseq, 512, L], BF16)
    dscr("z_va", [nseq, L, 256], BF16)
    dscr("z_fn", [nseq, L, 512], BF16)
    dscr("z_hy", [nseq, L + 2, 768], BF16)
    dscr("z_hc", [nseq, 768 // CB, L, CB], BF16)
    dscr("y_hc", [nseq, 256 // CB, L, CB], BF16)
    dscr("h_ftm", [2 * L, 512], BF16)
    dscr("zT_d", [nseq, 512, L], BF16)
    dscr("z_d", [nseq, L, 768], BF16)
    dscr("z_g", [nseq, 128, 64, 16], F32)
    dscr("y", [nseq, L, D], BF16)
    dscr("x1", [nseq, L, D], F32)
    dscr("xres", [nseq, L, D], F32)
    dscr("hscr", [nseq, DFF, L], BF16)
    dscr("h_full", [512 // CB, 2 * L, CB], BF16)
    dscr("h_KF", [512 // CB, 128, 65 * 2 * CB], BF16)
    S = Sched(nc)
    cx = Ctx(nc, S, T, nseq)
    with contextlib.ExitStack() as ph:
        zt = ph.enter_context(cx.sbt("zpad", [1, 768], BF16))
        cx.memset("pool", zt[:], 0.0, ["zpad"])
        for s in range(nseq):
            if "z_hy" in scratch_in:
                break
            cx.dma(T["z_hy"][s, 0:1, :], zt[:], ["zpad"], [])
            cx.dma(T["z_hy"][s, L + 1:L + 2, :], zt[:], ["zpad"], [])
    S.barrier()
    allp = ["A", "NA", "FN", "HYF", "HC", "HY", "ML", "C1", "C2"]
    fns = {"A": phase_A, "NA": phase_NA, "FN": phase_FN, "HYF": phase_HYF, "HC": phase_HC, "HY": phase_HY, "ML": phase_ML, "C1": phase_C1, "C2": phase_C2}
    for layer in range(DEPTH):
        for pn in allp:
            if phases is None or (layer, pn) in phases:
                fns[pn](cx, layer)
    S.emit()
    return nc


def na_bias_host(rpb):
    c = host_consts()
    idx, msk = c["_na_idx"], c["_na_msk"]
    out = np.empty((DEPTH, 8, 2, 128, 512), np.float32)
    for l in range(DEPTH):
        g = rpb[l][idx[..., 0], idx[..., 1], idx[..., 2]]
        out[l] = np.where(msk, g, np.float32(-30000.0))
    return out.reshape(DEPTH, 16, 128, 512)


_NC_CACHE = {}


def kernel(**inputs):
    nseq = 2
    n_cores = 8
    if nseq not in _NC_CACHE:
        _NC_CACHE[nseq] = build(nseq)
    nc = _NC_CACHE[nseq]
    c = host_consts()
    f32 = lambda a: np.ascontiguousarray(np.asarray(a, dtype=np.float32))
    shared = {n: f32(inputs[n]) for n, _ in WEIGHT_SPECS}
    shared["hy_w3"] = f32(inputs["hy_w3"])
    shared["na_bias"] = na_bias_host(f32(inputs["attn_rpb"]))
    for n, _, _ in CONST_SPECS:
        shared[n] = c[n]
    xp, xsm = f32(inputs["x_prompt"]), f32(inputs["x_sample"])
    pp, psm = f32(inputs["p_prompt"]), f32(inputs["p_sample"])
    in_maps = []
    for core in range(n_cores):
        m = dict(shared)
        m["xs"] = np.ascontiguousarray(np.stack([xsm[core], xp[core % 2]]))
        m["ps"] = np.ascontiguousarray(np.stack([psm[:, core], pp[:, core % 2]], axis=1))
        in_maps.append(m)
    res = run_bass_kernel_spmd(nc, in_maps, core_ids=list(range(n_cores)))
    y_sample = np.stack([np.asarray(res.results[core]["out"][0], dtype=np.float32) for core in range(n_cores)])
    y_prompt = np.stack([np.asarray(res.results[core]["out"][1], dtype=np.float32) for core in range(2)])
    return (y_prompt, y_sample)
```

```python
import math
import contextlib
import numpy as np
import ml_dtypes
import concourse.bass as bass
import concourse.mybir as mybir
from concourse.bass_utils import run_bass_kernel_spmd

F32 = mybir.dt.float32
BF16 = mybir.dt.bfloat16
ALU = mybir.AluOpType
AF = mybir.ActivationFunctionType
AX = mybir.AxisListType

L = 8192
D = 1024
NIN = 2832
DFF = 2816
DEPTH = 2
EPS = 1e-6
NFFT = 16384
ENGS = ("pe", "act", "dve", "pool", "sp")
NDMA = 8


class Sched:
    def __init__(self, nc):
        self.nc = nc
        self.ops = {e: [] for e in ENGS}
        self.cnt = {e: 0 for e in ENGS}
        self.seen = {e: {} for e in ENGS}
        self.last_w = {}
        self.readers = {}
        self.dma_use = {e: [0] * NDMA for e in ENGS}
        self.dma_rr = {e: 0 for e in ENGS}

    def _need(self, eng, tok, waits):
        key, val = tok
        if self.seen[eng].get(key, 0) >= val:
            return
        self.seen[eng][key] = val
        waits.append(tok)

    def op(self, eng, fn, reads=(), writes=(), dma=False, self_wait=False):
        waits = []
        same = ("c", eng)
        if self_wait and self.cnt[eng]:
            self._need(eng, (same, self.cnt[eng]), waits)
        for r in reads:
            t = self.last_w.get(r)
            if t is not None:
                self._need(eng, t, waits)
        strict = dma or eng != "pe"
        for w in writes:
            t = self.last_w.get(w)
            if t is not None and (strict or t[0] != same):
                self._need(eng, t, waits)
            for k, v in self.readers.get(w, {}).items():
                if strict or k != same:
                    self._need(eng, (k, v), waits)
        if dma:
            i = self.dma_rr[eng]
            self.dma_rr[eng] = (i + 1) % NDMA
            key = ("d", eng, i)
            prev = self.dma_use[eng][i]
            if prev:
                self._need(eng, (key, prev), waits)
            self.dma_use[eng][i] = prev + 16
            tok = (key, prev + 16)
            inc = 16
        else:
            self.cnt[eng] += 1
            tok = (same, self.cnt[eng])
            inc = 1
        self.ops[eng].append((waits, fn, tok[0], inc))
        for w in writes:
            self.last_w[w] = tok
            self.readers[w] = {}
        for r in reads:
            d = self.readers.setdefault(r, {})
            if d.get(tok[0], 0) < tok[1]:
                d[tok[0]] = tok[1]
        return tok

    def latest(self):
        latest = {}
        for e in ENGS:
            if self.cnt[e]:
                latest[("c", e)] = self.cnt[e]
            for i, v in enumerate(self.dma_use[e]):
                if v:
                    latest[("d", e, i)] = v
        return latest

    def barrier(self):
        latest = self.latest()
        for e in ENGS:
            waits = []
            for k, v in latest.items():
                if k == ("c", e):
                    continue
                self._need(e, (k, v), waits)
            if waits:
                self.ops[e].append((waits, None, None, 0))
        self.last_w = {}
        self.readers = {}

    def emit(self):
        nc = self.nc
        final = self.latest()
        keys = set()
        for e in ENGS:
            for waits, fn, key, inc in self.ops[e]:
                if key is not None:
                    keys.add(key)
        keys = sorted(keys)
        with contextlib.ExitStack() as st:
            sems = {}
            for k in keys:
                sems[k] = st.enter_context(nc.semaphore("s_" + "_".join(str(x) for x in k)))
            block = st.enter_context(nc.Block())

            def run(e):
                def body(engine):
                    for waits, fn, key, inc in self.ops[e]:
                        for (k, v) in waits:
                            engine.wait_ge(sems[k], v)
                        if fn is not None:
                            fn(engine).then_inc(sems[key], inc)
                    if e == "sp":
                        for k, v in final.items():
                            engine.wait_ge(sems[k], v)
                return body

            block.tensor(run("pe"))
            block.scalar(run("act"))
            block.vector(run("dve"))
            block.gpsimd(run("pool"))
            block.sync(run("sp"))


class Ctx:
    def __init__(self, nc, S, T, nseq):
        self.nc, self.S, self.T, self.nseq = nc, S, T, nseq
        self._ev = 0

    def sbt(self, name, shape, dt):
        self._uid = getattr(self, "_uid", 0) + 1
        return self.nc.sbuf_tensor(f"{name}_u{self._uid}", shape, dt)

    def pst(self, name, shape, dt):
        self._uid = getattr(self, "_uid", 0) + 1
        return self.nc.psum_tensor(f"{name}_u{self._uid}", shape, dt)

    def dma(self, out, in_, r, w):
        return self.S.op("sp", lambda e: e.dma_start(out=out, in_=in_), reads=r, writes=w, dma=True)

    def mm(self, out, lhsT, rhs, start, stop, r, w):
        sig = (lhsT.base_partition(), lhsT.shape[0])
        sw = getattr(self, "_pe_sig", None) not in (None, sig)
        self._pe_sig = sig
        return self.S.op("pe", lambda e: e.matmul(out=out, lhsT=lhsT, rhs=rhs, start=start, stop=stop), reads=r, writes=w, self_wait=sw)

    def tr(self, out, in_, ident, r, w):
        sig = (in_.base_partition(), in_.shape[0])
        sw = getattr(self, "_pe_sig", None) not in (None, sig)
        self._pe_sig = sig
        return self.S.op("pe", lambda e: e.transpose(out=out, in_=in_, identity=ident), reads=r, writes=w, self_wait=sw)

    def act(self, out, in_, func, r, w, **kw):
        return self.S.op("act", lambda e: e.activation(out=out, in_=in_, func=func, **kw), reads=r, writes=w)

    def cp(self, eng, out, in_, r, w):
        if eng == "act":
            return self.act(out, in_, AF.Copy, r, w)
        return self.S.op(eng, lambda e: e.tensor_copy(out=out, in_=in_), reads=r, writes=w)

    def ev(self):
        self._ev ^= 1
        return "act" if self._ev else "dve"

    def ts(self, eng, out, in0, s1, s2, op0, op1, r, w):
        if op1 is None:
            return self.S.op(eng, lambda e: e.tensor_scalar(out=out, in0=in0, scalar1=s1, scalar2=None, op0=op0), reads=r, writes=w)
        return self.S.op(eng, lambda e: e.tensor_scalar(out=out, in0=in0, scalar1=s1, scalar2=s2, op0=op0, op1=op1), reads=r, writes=w)

    def tt(self, eng, out, in0, in1, op, r, w):
        return self.S.op(eng, lambda e: e.tensor_tensor(out=out, in0=in0, in1=in1, op=op), reads=r, writes=w)

    def stt(self, eng, out, in0, scalar, in1, op0, op1, r, w):
        return self.S.op(eng, lambda e: e.scalar_tensor_tensor(out=out, in0=in0, scalar=scalar, in1=in1, op0=op0, op1=op1), reads=r, writes=w)

    def memset(self, eng, ap, val, w):
        return self.S.op(eng, lambda e: e.memset(ap, val), writes=w)

    def recip(self, out, in_, r, w):
        return self.S.op("dve", lambda e: e.reciprocal(out=out, in_=in_), reads=r, writes=w)

    def rstd(self, out, in_, scale, r, w, name):
        self.ts("dve", out, in_, scale, EPS, ALU.mult, ALU.add, r, w)
        self.act(out, out, AF.Sqrt, w, w)
        self.recip(out, out, w, w)


def make_staging(cx, ph, nkmax, cw, tag):
    return [ph.enter_context(cx.sbt(f"{tag}_st{i}", [128, nkmax, cw], F32)) for i in range(2)], cw, tag


def load_scaled_weight(cx, ph, dst, dst_key, w_ap, g_ap, nk, ncols, tag, staging):
    nc = cx.nc
    st, CW, stag = staging
    g = None
    if g_ap is not None:
        g = ph.enter_context(cx.sbt(f"{tag}_g", [128, nk], F32))
        cx.S.op("sp", lambda e: e.dma_start(out=g[:], in_=g_ap.rearrange("(k p) -> p k", p=128), allow_slow_non_contiguous=True),
                writes=[tag + "_g"], dma=True)
    wv = w_ap.rearrange("(k p) n -> p k n", p=128)
    i = 0
    for c0 in range(0, ncols, CW):
        cw = min(CW, ncols - c0)
        b = i % 2
        sk = f"{stag}_st{b}"
        cx.dma(st[b][:, 0:nk, :cw], wv[:, :, c0:c0 + cw], [], [sk])
        for k in range(nk):
            eng = ("dve", "pool")[k % 2]
            if g is not None:
                cx.ts(eng, dst[:, k, c0:c0 + cw], st[b][:, k, :cw], g[:, k:k + 1], None, ALU.mult, None, [sk, tag + "_g"], [dst_key])
            else:
                cx.cp(eng, dst[:, k, c0:c0 + cw], st[b][:, k, :cw], [sk], [dst_key])
        i += 1


def norm_transpose(cx, src, src_key, j, xs, xs_key, junk, ss, rs, pT, xT, xT_key, identb, tagk):
    cx.act(junk[:], src[:, j, :], AF.Square, [src_key], [tagk + "junk", tagk + "ss"], accum_out=ss[:, j:j + 1])
    cx.rstd(rs[:, j:j + 1], ss[:, j:j + 1], 1.0 / D, [tagk + "ss"], [tagk + "rs"], tagk)
    cx.act(xs[:, j, :], src[:, j, :], AF.Copy, [src_key, tagk + "rs"], [xs_key], scale=rs[:, j:j + 1])
    for k in range(8):
        cx.tr(pT[:, k, :], xs[:, j, k * 128:(k + 1) * 128], identb[:], [xs_key, "identb"], [tagk + "pT"])
    cx.cp(cx.ev(), xT[:, :, j * 128:(j + 1) * 128], pT[:], [tagk + "pT"], [xT_key])


def drive(fronts, backs, n, k=2):
    for _ in fronts(0):
        pass
    for i in range(n):
        nf = fronts(i + 1) if i + 1 < n else None
        for _ in backs(i):
            if nf is not None:
                for _ in range(k):
                    try:
                        next(nf)
                    except StopIteration:
                        nf = None
                        break
        if nf is not None:
            for _ in nf:
                pass


def norm_T4(cx, src, src_key, xs, xs_key, junk, ss, rs, pTs, xT, xT_key, identb, tagk):
    for j in range(4):
        cx.act(junk[:], src[:, j, :], AF.Square, [src_key], [tagk + "junk", tagk + "ss"], accum_out=ss[:, j:j + 1])
    yield
    cx.rstd(rs[:], ss[:], 1.0 / D, [tagk + "ss"], [tagk + "rs"], tagk)
    yield
    for j in range(4):
        if j % 2 == 0:
            cx.act(xs[:, j, :], src[:, j, :], AF.Copy, [src_key, tagk + "rs"], [xs_key], scale=rs[:, j:j + 1])
        else:
            cx.ts("dve", xs[:, j, :], src[:, j, :], rs[:, j:j + 1], None, ALU.mult, None, [src_key, tagk + "rs"], [xs_key])
        yield
    for j in range(4):
        pT = pTs[j % 2]
        pk = tagk + f"pT{j % 2}"
        for k in range(8):
            cx.tr(pT[:, k, :], xs[:, j, k * 128:(k + 1) * 128], identb[:], [xs_key, "identb"], [pk])
        cx.cp(cx.ev(), xT[:, :, j * 128:(j + 1) * 128], pT[:], [pk], [xT_key])
        yield


FM_BLOCKS = [("a", 0, 0), ("a", 1, 128), ("a", 2, 256), ("a", 3, 384), ("u", 0, 768), ("u", 1, 896),
             ("d", 0, 1792), ("d", 1, 1920), ("d", 2, 2048), ("d", 3, 2176)]
TM_GROUPS = [(512, 256, 0), (1024, 512, 256), (1536, 256, 768), (2048, 512, 1024), (2560, 272, 1536)]


def phase_A(cx, layer):
    nc, S, T = cx.nc, cx.S, cx.T
    with contextlib.ExitStack() as ph:
        sb = lambda n, s, d: ph.enter_context(cx.sbt(n, s, d))
        ps = lambda n, s, d: ph.enter_context(cx.pst(n, s, d))
        wb = sb("A_wb", [128, 8, NIN], BF16)
        identb = sb("A_idb", [128, 128], BF16)
        cx.dma(identb[:], T["ident_b"][:, :], [], ["identb"])
        stg = make_staging(cx, ph, 8, 256, "Astg")
        load_scaled_weight(cx, ph, wb, "A_wb", T["w_in"][layer], T["norm_mix"][layer], 8, NIN, "Aw", stg)
        AB = sb("A_AB", [128, 2, 512], BF16)
        fw = sb("A_fw", [64, 4, 64], F32)
        csp = sb("A_csp", [64, 2, 2, 128], F32)
        pab = ps("A_pab", [128, 64], F32)
        cx.memset("pool", AB[:], 0.0, ["AB"])
        cx.dma(fw[:], T["fnet_w"][layer].rearrange("g j c -> j g c"), [], ["fw"])
        cx.dma(csp[:], T["cs64"][:, :, :, :], [], ["csp"])
        for g in range(4):
            for ab in range(2):
                cx.mm(pab[:], csp[:, ab, g % 2, :], fw[:, g, :], True, True, ["csp", "fw"], ["pab"])
                P = slice((g % 2) * 64, (g % 2) * 64 + 64)
                cx.cp("dve", AB[P, g // 2, g * 128 + ab * 64: g * 128 + ab * 64 + 64], pab[P, :], ["pab"], ["AB"])
        xin = [sb(f"A_xin{i}", [128, 4, D], F32) for i in range(2)]
        xs = sb("A_xs", [128, 4, D], BF16)
        xT = [sb(f"A_xT{i}", [128, 8, 512], BF16) for i in range(2)]
        junk = sb("A_junk", [128, D], BF16)
        ss = sb("A_ss", [128, 4], F32)
        rs = sb("A_rs", [128, 4], F32)
        ubT = sb("A_ubT", [128, 2, 512], BF16)
        fm = [sb(f"A_fm{i}", [128, 512], BF16) for i in range(3)]
        ztm = [sb(f"A_ztm{i}", [128, 4, 1792], BF16) for i in range(2)]
        zg = [sb(f"A_zg{i}", [128, 4, 16], F32) for i in range(2)]
        zf = [sb(f"A_zf{i}", [128, 4, 512], BF16) for i in range(2)]
        pT = ps("A_pT", [128, 8, 128], BF16)
        pf = [ps(f"A_pf{i}", [128, 512], F32) for i in range(2)]
        pm = [ps(f"A_pm{i}", [128, 512], F32) for i in range(2)]
        src = T["xs"] if layer == 0 else T["xres"]
        it = 0
        nfm = 0
        npm = 0
        for s in range(cx.nseq):
            for i in range(L // 512):
                t0 = i * 512
                b = it % 2
                it += 1
                xk, xtk, ztk, zgk, zfk = f"xin{b}", f"xT{b}", f"ztm{b}", f"zg{b}", f"zf{b}"
                cx.dma(xin[b][:], src[s, t0:t0 + 512, :].rearrange("(j p) d -> p j d", p=128), [], [xk])
                for j in range(4):
                    norm_transpose(cx, xin[b], xk, j, xs, "A_xs", junk, ss, rs, pT, xT[b], xtk, identb, "A")
                for (kind, bi, c0) in FM_BLOCKS:
                    p = pf[nfm % 2]
                    pk = f"pf{nfm % 2}"
                    for k in range(8):
                        cx.mm(p[:], wb[:, k, c0:c0 + 128], xT[b][:, k, :], k == 0, k == 7, ["A_wb", xtk], [pk])
                    if kind == "u":
                        cx.cp(cx.ev(), ubT[:, bi, :], p[:], [pk], ["ubT"])
                    else:
                        fb = nfm % 3
                        fk = f"fm{fb}"
                        cx.cp(cx.ev(), fm[fb][:], p[:], [pk], [fk])
                        dst = T["zT_a"] if kind == "a" else T["zT_d"]
                        cx.dma(dst[s, bi * 128:(bi + 1) * 128, t0:t0 + 512], fm[fb][:], [fk], [])
                    nfm += 1
                for j in range(4):
                    tj = slice(j * 128, (j + 1) * 128)
                    for (c0, cw, off) in TM_GROUPS:
                        p = pm[npm % 2]
                        pk = f"pm{npm % 2}"
                        npm += 1
                        for k in range(8):
                            cx.mm(p[:, :cw], xT[b][:, k, tj], wb[:, k, c0:c0 + cw], k == 0, k == 7, [xtk, "A_wb"], [pk])
                        if cw == 272:
                            cx.cp(cx.ev(), ztm[b][:, j, off:off + 256], p[:, :256], [pk], [ztk])
                            cx.cp(cx.ev(), zg[b][:, j, :], p[:, 256:272], [pk], [zgk])
                        else:
                            cx.cp(cx.ev(), ztm[b][:, j, off:off + cw], p[:, :cw], [pk], [ztk])
                    p = pm[npm % 2]
                    pk = f"pm{npm % 2}"
                    npm += 1
                    for ct in range(2):
                        cx.mm(p[:], ubT[:, ct, tj], AB[:, ct, :], ct == 0, ct == 1, ["ubT", "AB"], [pk])
                    cx.cp(cx.ev(), zf[b][:, j, :], p[:], [pk], [zfk])
                tv = lambda ap: ap[s, t0:t0 + 512, :].rearrange("(j p) c -> p j c", p=128)
                cx.dma(tv(T["z_va"]), ztm[b][:, :, 0:256], [ztk], [])
                cx.dma(T["z_hy"][s, 1 + t0:1 + t0 + 512, :].rearrange("(j p) c -> p j c", p=128), ztm[b][:, :, 256:1024], [ztk], [])
                cx.dma(tv(T["z_d"]), ztm[b][:, :, 1024:1792], [ztk], [])
                cx.dma(T["z_g"][s, :, 4 * i:4 * i + 4, :], zg[b][:], [zgk], [])
                cx.dma(tv(T["z_fn"]), zf[b][:], [zfk], [])
    S.barrier()


def phase_NA(cx, layer):
    nc, S, T = cx.nc, cx.S, cx.T
    with contextlib.ExitStack() as ph:
        sb = lambda n, s, d: ph.enter_context(cx.sbt(n, s, d))
        ps = lambda n, s, d: ph.enter_context(cx.pst(n, s, d))
        QT = sb("N_QT", [128, 2, L], BF16)
        KT = sb("N_KT", [128, 2, L], BF16)
        Ve = sb("N_Ve", [128, 64, 256], BF16)
        Vo = sb("N_Vo", [128, 63, 256], BF16)
        bias = sb("N_bias", [128, 16, 512], F32)
        identb = sb("N_idb", [128, 128], BF16)
        NB = 2
        qbd = [sb(f"N_qbd{i}", [128, 128], BF16) for i in range(NB)]
        ssb = [sb(f"N_s{i}", [128, 512], F32) for i in range(NB)]
        pb = [sb(f"N_p{i}", [128, 512], BF16) for i in range(NB)]
        pTs = [sb(f"N_pT{i}", [128, 4, 128], BF16) for i in range(NB)]
        mx = [sb(f"N_mx{i}", [128, 1], F32) for i in range(NB)]
        rsum = [sb(f"N_rs{i}", [128, 1], F32) for i in range(NB)]
        ybuf = [sb(f"N_yb{i}", [128, 8, 64], BF16) for i in range(4)]
        Sp = [ps(f"N_Sp{i}", [128, 512], F32) for i in range(2)]
        pTp = [ps(f"N_pTp{i}", [128, 4, 128], BF16) for i in range(2)]
        po = [ps(f"N_po{i}", [128, 128], F32) for i in range(2)]
        cx.dma(identb[:], T["ident_b"][:, :], [], ["identb"])
        cx.dma(bias[:], T["na_bias"][layer].rearrange("v p n -> p v n"), [], ["bias"])
        for i in range(NB):
            cx.memset("pool", qbd[i][:], 0.0, [f"qbd{i}"])
        u = 0
        for s in range(cx.nseq):
            for hp in range(2):
                cx.dma(QT[:, hp, :], T["zT_a"][s, hp * 128:(hp + 1) * 128, :], [], ["QT"])
                cx.dma(KT[:, hp, :], T["zT_a"][s, 256 + hp * 128:256 + (hp + 1) * 128, :], [], ["KT"])
            cx.dma(Ve[:], T["z_va"][s].rearrange("(n p) c -> p n c", p=128), [], ["Ve"])
            cx.dma(Vo[:], T["z_va"][s, 64:64 + 63 * 128, :].rearrange("(n p) c -> p n c", p=128), [], ["Vo"])
            for r in range(128):
                kr0 = min(max(r - 4, 0), 120)
                v = r if r < 4 else (4 if r <= 124 else r - 120)
                for hp in range(2):
                    b = u % NB
                    b2 = u % 2
                    u += 1
                    yb = ybuf[hp * 2 + (r // 8) % 2]
                    ybk = f"yb{hp * 2 + (r // 8) % 2}"
                    tq = slice(r * 64, (r + 1) * 64)
                    cx.cp("pool", qbd[b][0:64, 0:64], QT[0:64, hp, tq], ["QT"], [f"qbd{b}"])
                    cx.cp("pool", qbd[b][64:128, 64:128], QT[64:128, hp, tq], ["QT"], [f"qbd{b}"])
                    cx.mm(Sp[b2][:], qbd[b][:], KT[:, hp, kr0 * 64:kr0 * 64 + 512], True, True, [f"qbd{b}", "KT"], [f"Sp{b2}"])
                    cx.stt("dve", ssb[b][:], Sp[b2][:], 0.125, bias[:, v * 2 + hp, :], ALU.mult, ALU.add, [f"Sp{b2}", "bias"], [f"s{b}"])
                    cx.S.op("dve", lambda e, b=b: e.tensor_reduce(out=mx[b][:], in_=ssb[b][:], axis=AX.X, op=ALU.max), reads=[f"s{b}"], writes=[f"mx{b}"])
                    cx.ts("pool", mx[b][:], mx[b][:], -1.0, None, ALU.mult, None, [f"mx{b}"], [f"mx{b}"])
                    cx.act(pb[b][:], ssb[b][:], AF.Exp, [f"s{b}", f"mx{b}"], [f"p{b}", f"rs{b}"], bias=mx[b][:], accum_out=rsum[b][:])
                    for c in range(4):
                        cx.tr(pTp[b2][:, c, :], pb[b][:, c * 128:(c + 1) * 128], identb[:], [f"p{b}", "identb"], [f"pTp{b2}"])
                    cx.cp(cx.ev(), pTs[b][:], pTp[b2][:], [f"pTp{b2}"], [f"pT{b}"])
                    for c in range(4):
                        if kr0 % 2 == 0:
                            vv = Ve[:, kr0 // 2 + c, hp * 128:(hp + 1) * 128]
                            vk = "Ve"
                        else:
                            vv = Vo[:, (kr0 - 1) // 2 + c, hp * 128:(hp + 1) * 128]
                            vk = "Vo"
                        cx.mm(po[b2][:], pTs[b][:, c, :], vv, c == 0, c == 3, [f"pT{b}", vk], [f"po{b2}"])
                    cx.recip(rsum[b][:], rsum[b][:], [f"rs{b}"], [f"rs{b}"])
                    cx.act(yb[0:64, r % 8, :], po[b2][0:64, 0:64], AF.Copy, [f"po{b2}", f"rs{b}"], [ybk], scale=rsum[b][0:64, :])
                    cx.act(yb[64:128, r % 8, :], po[b2][64:128, 64:128], AF.Copy, [f"po{b2}", f"rs{b}"], [ybk], scale=rsum[b][64:128, :])
                    if r % 8 == 7:
                        r0 = r - 7
                        for h in range(2):
                            col = (2 * hp + h) * 64
                            cx.dma(T["y"][s, r0 * 64:(r0 + 8) * 64, col:col + 64].rearrange("(r q) d -> q r d", q=64),
                                   yb[h * 64:(h + 1) * 64, :, :], [ybk], [])
    S.barrier()


def phase_FN(cx, layer):
    nc, S, T = cx.nc, cx.S, cx.T
    with contextlib.ExitStack() as ph:
        sb = lambda n, s, d: ph.enter_context(cx.sbt(n, s, d))
        ps = lambda n, s, d: ph.enter_context(cx.pst(n, s, d))
        F1 = sb("F_F1", [64, 2, 128], BF16)
        T2c = sb("F_T2c", [128, 64, 128], BF16)
        T2s = sb("F_T2s", [128, 64, 128], BF16)
        uin = sb("F_uin", [64, 128, 256], BF16)
        Ysb = sb("F_Y", [128, 2, 64, 128], BF16)
        yfn = sb("F_yfn", [128, 64, 128], BF16)
        Yp = [ps(f"F_Yp{i}", [128, 4, 128], F32) for i in range(2)]
        Xp = [ps(f"F_Xp{i}", [128, 4, 128], F32) for i in range(2)]
        cx.dma(F1[:], T["f_F1"][:, :, :], [], ["F1"])
        cx.dma(T2c[:], T["f_T2c"][:, :, :], [], ["T2c"])
        cx.dma(T2s[:], T["f_T2s"][:, :, :], [], ["T2s"])
        n1 = 0
        n2 = 0
        for s in range(cx.nseq):
            for hf in range(2):
                cx.dma(uin[:], T["z_fn"][s, :, hf * 256:(hf + 1) * 256].rearrange("(a b) c -> a b c", b=128), [], ["uin"])
                for c0 in range(0, 128, 4):
                    p = Yp[n1 % 2]
                    pk = f"Yp{n1 % 2}"
                    n1 += 1
                    for cc in range(4):
                        ch = c0 + cc
                        g2, c_ = ch // 64, ch % 64
                        cx.mm(p[:, cc, :], uin[:, :, g2 * 128 + c_], F1[:, 0, :], True, False, ["uin", "F1"], [pk])
                        cx.mm(p[:, cc, :], uin[:, :, g2 * 128 + 64 + c_], F1[:, 1, :], False, True, ["uin", "F1"], [pk])
                    cx.cp(cx.ev(), Ysb[:, :, :, c0:c0 + 4], p[:].rearrange("p c (r k) -> p r k c", r=2), [pk], ["Ysb"])
                for k0 in range(0, 64, 4):
                    p = Xp[n2 % 2]
                    pk = f"Xp{n2 % 2}"
                    n2 += 1
                    for q in range(4):
                        k1 = k0 + q
                        cx.mm(p[:, q, :], T2c[:, k1, :], Ysb[:, 0, k1, :], True, False, ["T2c", "Ysb"], [pk])
                        cx.mm(p[:, q, :], T2s[:, k1, :], Ysb[:, 1, k1, :], False, True, ["T2s", "Ysb"], [pk])
                    cx.cp(cx.ev(), yfn[:, k0:k0 + 4, :], p[:], [pk], ["yfn"])
                cx.dma(T["y"][s, :, 256 + hf * 128:256 + (hf + 1) * 128].rearrange("(k2 k1) c -> k2 k1 c", k1=64), yfn[:], ["yfn"], [])
    S.barrier()


CB = 32


class HyTabs:
    pass


def hy_tables(cx, ph):
    nc, T = cx.nc, cx.T
    sb = lambda n, s, d: ph.enter_context(cx.sbt(n, s, d))
    ps = lambda n, s, d: ph.enter_context(cx.pst(n, s, d))
    H = HyTabs()
    H.F1 = sb("H_F1", [128, 195], BF16)
    H.T2c = sb("H_T2c", [128, 65, 128], BF16)
    H.T2s = sb("H_T2s", [128, 65, 128], BF16)
    H.GA = sb("H_GA", [128, 2, 256], BF16)
    H.TBc = sb("H_TBc", [65, 128, 64], BF16)
    H.TBs = sb("H_TBs", [65, 128, 64], BF16)
    for t, n in ((H.F1, "h_F1"), (H.T2c, "h_T2c"), (H.T2s, "h_T2s"), (H.GA, "h_GA"), (H.TBc, "h_TBc"), (H.TBs, "h_TBs")):
        cx.dma(t[:], T[n], [], ["H_tab"])
    H.Ysb = sb("H_Y", [128, 3, 65, CB], BF16)
    H.Xs = sb("H_Xs", [128, 65, 2, CB], BF16)
    H.Yp = [ps(f"H_Yp{i}", [128, 2, 195], F32) for i in range(2)]
    H.Xp = [ps(f"H_Xp{i}", [128, 8, 2, CB], F32) for i in range(2)]
    H.n1 = 0
    H.n2 = 0
    return H


def hy_fwd(cx, H, src, src_key, K1):
    for c0 in range(0, CB, 2):
        p = H.Yp[H.n1 % 2]
        pk = f"HYp{H.n1 % 2}"
        H.n1 += 1
        for cc in range(2):
            cx.mm(p[:, cc, :], src[0:K1, :, c0 + cc], H.F1[0:K1, :], True, True, [src_key, "H_tab"], [pk])
        cx.cp(cx.ev(), H.Ysb[:, :, :, c0:c0 + 2], p[:].rearrange("p c (r k) -> p r k c", r=3), [pk], ["HYsb"])
    for k0 in range(0, 65, 8):
        nk = min(8, 65 - k0)
        p = H.Xp[H.n2 % 2]
        pk = f"HXp{H.n2 % 2}"
        H.n2 += 1
        for q in range(nk):
            k1 = k0 + q
            cx.mm(p[:, q, 0, :], H.T2c[:, k1, :], H.Ysb[:, 0, k1, :], True, False, ["H_tab", "HYsb"], [pk])
            cx.mm(p[:, q, 0, :], H.T2s[:, k1, :], H.Ysb[:, 1, k1, :], False, True, ["H_tab", "HYsb"], [pk])
            cx.mm(p[:, q, 1, :], H.T2c[:, k1, :], H.Ysb[:, 1, k1, :], True, False, ["H_tab", "HYsb"], [pk])
            cx.mm(p[:, q, 1, :], H.T2s[:, k1, :], H.Ysb[:, 2, k1, :], False, True, ["H_tab", "HYsb"], [pk])
        cx.cp(cx.ev(), H.Xs[:, k0:k0 + nk, :, :], p[:, 0:nk, :, :], [pk], ["HXs"])


def wrap_sin(cx, a, tmp, ak, tk, P):
    pi = math.pi
    cx.ts("dve", tmp[0:P, :], a[0:P, :], pi, -2 * pi, ALU.is_gt, ALU.mult, [ak], [tk])
    cx.tt("dve", a[0:P, :], a[0:P, :], tmp[0:P, :], ALU.add, [ak, tk], [ak])
    cx.ts("dve", tmp[0:P, :], a[0:P, :], -pi, 2 * pi, ALU.is_lt, ALU.mult, [ak], [tk])
    cx.tt("dve", a[0:P, :], a[0:P, :], tmp[0:P, :], ALU.add, [ak, tk], [ak])
    cx.act(a[0:P, :], a[0:P, :], AF.Sin, [ak], [ak])


def phase_HYF(cx, layer):
    nc, S, T = cx.nc, cx.S, cx.T
    with contextlib.ExitStack() as ph:
        sb = lambda n, s, d: ph.enter_context(cx.sbt(n, s, d))
        ps = lambda n, s, d: ph.enter_context(cx.pst(n, s, d))
        w1 = sb("G_w1", [33, 64], F32)
        w2 = sb("G_w2", [64, 64], F32)
        w3 = sb("G_w3", [64, 1024], F32)
        sc = sb("G_sc", [64, 6], F32)
        dec = sb("G_dec", [1, 1024], F32)
        cx.dma(w1[:], T["hy_w1"][layer], [], ["w1"])
        cx.dma(w2[:], T["hy_w2"][layer], [], ["w2"])
        cx.dma(w3[:], T["hy_w3"][layer], [], ["w3"])
        cx.dma(dec[:], T["hy_decay"][layer].rearrange("o d c -> (o d c)").unsqueeze(0), [], ["dec"])
        cx.S.op("sp", lambda e: e.dma_start(out=sc[:, 0:1], in_=T["hy_b1"][layer].unsqueeze(1), allow_slow_non_contiguous=True), writes=["sc"], dma=True)
        cx.S.op("sp", lambda e: e.dma_start(out=sc[:, 1:2], in_=T["hy_b2"][layer].unsqueeze(1), allow_slow_non_contiguous=True), writes=["sc"], dma=True)
        cx.S.op("sp", lambda e: e.dma_start(out=sc[:, 2:4], in_=T["hy_freq"][layer].rearrange("t c -> c t"), allow_slow_non_contiguous=True), writes=["sc"], dma=True)
        cx.tt("dve", sc[:, 4:6], sc[:, 0:2], sc[:, 2:4], ALU.mult, ["sc"], ["sc"])
        fT = [sb(f"G_fT{i}", [33, 512], F32) for i in range(2)]
        tv = [sb(f"G_tv{i}", [1, 512], F32) for i in range(2)]
        a1 = sb("G_a1", [64, 512], F32)
        a2 = sb("G_a2", [64, 512], F32)
        tmp = sb("G_tmp", [64, 512], F32)
        win = [sb(f"G_win{i}", [128, 512], F32) for i in range(2)]
        ff = [sb(f"G_ff{i}", [128, 512], BF16) for i in range(2)]
        p1 = ps("G_p1", [64, 512], F32)
        p2 = ps("G_p2", [64, 512], F32)
        p3 = [ps(f"G_p3{i}", [128, 512], F32) for i in range(2)]
        pd = [ps(f"G_pd{i}", [128, 512], F32) for i in range(2)]
        w3v = w3[:].rearrange("p (o d c) -> p o d c", o=2, d=2)
        decv = dec[:].rearrange("p (o d c) -> p o d c", o=2, d=2)
        n = 0
        for i in range(2 * L // 512):
            R0 = i * 512
            d = 0 if R0 < L else 1
            b = i % 2
            cx.dma(fT[b][:], T["h_feats"][:, R0:R0 + 512], [], [f"fT{b}"])
            cx.dma(tv[b][:], T["h_tvec"][:, R0:R0 + 512], [], [f"tv{b}"])
            cx.mm(p1[:], w1[:], fT[b][:], True, True, ["w1", f"fT{b}"], ["p1"])
            cx.ts("dve", a1[:], p1[:], sc[:, 2:3], sc[:, 4:5], ALU.mult, ALU.add, ["p1", "sc"], ["a1"])
            wrap_sin(cx, a1, tmp, "a1", "tmp", 64)
            cx.mm(p2[:], w2[:], a1[:], True, True, ["w2", "a1"], ["p2"])
            cx.ts("dve", a2[:], p2[:], sc[:, 3:4], sc[:, 5:6], ALU.mult, ALU.add, ["p2", "sc"], ["a2"])
            wrap_sin(cx, a2, tmp, "a2", "tmp", 64)
            for j in range(4):
                q = n % 2
                n += 1
                cx.mm(p3[q][:].rearrange("p (o c) -> p o c", o=2), a2[:, j * 128:(j + 1) * 128], w3v[:, :, d, :], True, True, ["a2", "w3"], [f"p3{q}"])
                cx.mm(pd[q][:].rearrange("p (o c) -> p o c", o=2), tv[b][0:1, j * 128:(j + 1) * 128], decv[:, :, d, :], True, True, [f"tv{b}", "dec"], [f"pd{q}"])
                cx.act(win[q][:], pd[q][:], AF.Exp, [f"pd{q}"], [f"win{q}"], scale=-1.0)
                cx.tt("dve", ff[q][:], p3[q][:], win[q][:], ALU.mult, [f"p3{q}", f"win{q}"], [f"ff{q}"])
                if R0 + j * 128 == L:
                    cx.memset("dve", ff[q][0:1, :], 0.0, [f"ff{q}"])
                cx.dma(T["h_ftm"][R0 + j * 128:R0 + (j + 1) * 128, :], ff[q][:], [f"ff{q}"], ["h_ftm"])
    S.barrier()
    with contextlib.ExitStack() as ph:
        sb = lambda n, s, d: ph.enter_context(cx.sbt(n, s, d))
        ra = [sb(f"G_ra{i}", [128, 16, 512], BF16) for i in range(2)]
        rb = [sb(f"G_rb{i}", [128, 512 // CB, 16, CB], BF16) for i in range(2)]
        for q in range(8):
            b = q % 2
            cx.dma(ra[b][:], T["h_ftm"].rearrange("(a t) c -> a t c", t=128)[:, 16 * q:16 * q + 16, :], ["h_ftm"], [f"ra{b}"])
            cx.cp(("pool", "dve")[b], rb[b][:].rearrange("p b t c -> p t b c"), ra[b][:].rearrange("p t (b c) -> p t b c", c=CB), [f"ra{b}"], [f"rb{b}"])
            cx.dma(T["h_full"].rearrange("b (a t) c -> a b t c", t=128)[:, :, 16 * q:16 * q + 16, :], rb[b][:], [f"rb{b}"], ["h_full"])
    S.barrier()
    with contextlib.ExitStack() as ph:
        sb = lambda n, s, d: ph.enter_context(cx.sbt(n, s, d))
        H = hy_tables(cx, ph)
        fin = [sb(f"G_fin{i}", [128, 128, CB], BF16) for i in range(2)]
        skp = sb("G_skp", [128, 512], F32)
        KFt = [sb(f"G_KFt{i}", [128, 65, 2, CB], BF16) for i in range(2)]
        cx.dma(skp[:], T["hy_skip"][layer].rearrange("o c -> (o c)").partition_broadcast(128), [], ["skp"])
        for fb in range(16):
            b = fb % 2
            cx.dma(fin[b][:], T["h_full"][fb].rearrange("(a t) c -> a t c", t=128), ["h_full"], [f"fin{b}"])
            hy_fwd(cx, H, fin[b], f"fin{b}", 128)
            cx.tt("dve", KFt[b][:, :, 0, :], H.Xs[:, :, 0, :], skp[:, fb * CB:(fb + 1) * CB].unsqueeze(1).to_broadcast([128, 65, CB]),
                  ALU.add, ["HXs", "skp"], [f"KFt{b}"])
            cx.cp("pool", KFt[b][:, :, 1, :], H.Xs[:, :, 1, :], ["HXs"], [f"KFt{b}"])
            cx.dma(T["h_KF"][fb].rearrange("p (k r c) -> p k r c", k=65, r=2), KFt[b][:], [f"KFt{b}"], ["h_KF"])
    S.barrier()


def phase_HC(cx, layer):
    nc, S, T = cx.nc, cx.S, cx.T
    NBLK = 768 // CB
    with contextlib.ExitStack() as ph:
        sb = lambda n, s, d: ph.enter_context(cx.sbt(n, s, d))
        cw = sb("Y_cw", [64, 3, 768], F32)
        cb_ = sb("Y_cb", [64, 768], F32)
        cx.dma(cw[:], T["hy_conv_w"][layer].rearrange("t c -> (t c)").partition_broadcast(64), [], ["cw"])
        cx.dma(cb_[:], T["hy_conv_b"][layer].partition_broadcast(64), [], ["cb"])
        uin = [sb(f"Y_uin{i}", [64, 18, 768], BF16) for i in range(2)]
        t1 = sb("Y_t1", [64, 16, 768], BF16)
        t2 = sb("Y_t2", [64, 16, 768], BF16)
        sg = [sb(f"Y_sgp{i}", [64, NBLK, 16, CB], BF16) for i in range(2)]
        it = 0
        bc = lambda ap: ap.unsqueeze(1).to_broadcast([64, 16, 768])
        for s in range(cx.nseq):
            base = T["z_hy"][s]
            for q in range(8):
                b = it % 2
                it += 1
                uk, sk = f"uin{b}", f"sgp{b}"
                win_ap = bass.AP(base.tensor, base.offset + 16 * q * 768, [[128 * 768, 64], [768, 18], [1, 768]])
                cx.dma(uin[b][:], win_ap, [], [uk])
                cx.tt("pool", t1[:], uin[b][:, 0:16, :], bc(cw[:, 0, :]), ALU.mult, [uk, "cw"], ["t1"])
                cx.tt("dve", t2[:], uin[b][:, 1:17, :], bc(cw[:, 1, :]), ALU.mult, [uk, "cw"], ["t2"])
                cx.tt("pool", t1[:], t1[:], t2[:], ALU.add, ["t1", "t2"], ["t1"])
                cx.tt("dve", t2[:], uin[b][:, 2:18, :], bc(cw[:, 2, :]), ALU.mult, [uk, "cw"], ["t2"])
                cx.tt("pool", t1[:], t1[:], t2[:], ALU.add, ["t1", "t2"], ["t1"])
                cx.tt("dve", sg[b][:].rearrange("p b t c -> p t b c"), t1[:].rearrange("p t (b c) -> p t b c", c=CB),
                      bc(cb_[:, :]).rearrange("p t (b c) -> p t b c", c=CB), ALU.add, ["t1", "cb"], [sk])
                cx.dma(T["z_hc"][s].rearrange("b (a t) c -> a b t c", t=128)[:, :, 16 * q:16 * q + 16, :], sg[b][:], [sk], [])
    S.barrier()


def phase_HY(cx, layer):
    nc, S, T = cx.nc, cx.S, cx.T
    with contextlib.ExitStack() as ph:
        sb = lambda n, s, d: ph.enter_context(cx.sbt(n, s, d))
        ps = lambda n, s, d: ph.enter_context(cx.pst(n, s, d))
        H = hy_tables(cx, ph)
        sig1 = [sb(f"Y_sig{i}", [64, 128, CB], BF16) for i in range(3)]
        sig = [sig1, sig1]
        KF = [sb(f"Y_KF{i}", [128, 65, 2, CB], BF16) for i in range(2)]
        Z = sb("Y_Z", [128, 65, 2, CB], BF16)
        za = sb("Y_za", [128, 65, CB], BF16)
        zb = sb("Y_zb", [128, 65, CB], BF16)
        Vsb = sb("Y_V", [65, 2, 128, CB], BF16)
        y1 = sb("Y_y1", [64, 128, CB], BF16)
        yo = [sb(f"Y_yo{i}", [64, 128, CB], BF16) for i in range(2)]
        rl = sb("Y_rl", [64, 256 // CB, 16, CB], BF16)
        rl2 = sb("Y_rl2", [64, 16, 256 // CB, CB], BF16)
        Vp = [ps(f"Y_Vp{i}", [65, 2, 256], F32) for i in range(2)]
        yp = [ps(f"Y_yp{i}", [64, 16, CB], F32) for i in range(2)]
        nv = 0
        ny = 0
        nk = 0
        it = 0
        for s in range(cx.nseq):
            for cb in range(256 // CB):
                jb = it % 2
                it += 1
                for g in range(3):
                    blk = g * (256 // CB) + cb
                    cx.dma(sig[jb][g][:], T["z_hc"][s, blk].rearrange("(a t) c -> a t c", t=128), [], [f"sig{g}"])
                for o in range(2):
                    src, sk = (sig[jb][0], "sig0") if o == 0 else (y1, "y1")
                    gate, gk = (sig[jb][1], "sig1") if o == 0 else (sig[jb][2], "sig2")
                    dst, dk = (y1, "y1") if o == 0 else (yo[jb], f"yo{jb}")
                    kb = nk % 2
                    nk += 1
                    kfk = f"KF{kb}"
                    cx.dma(KF[kb][:], T["h_KF"][o * (256 // CB) + cb].rearrange("p (k r c) -> p k r c", k=65, r=2), ["h_KF"], [kfk])
                    hy_fwd(cx, H, src, sk, 64)
                    Xr, Xi, Kr, Ki = H.Xs[:, :, 0, :], H.Xs[:, :, 1, :], KF[kb][:, :, 0, :], KF[kb][:, :, 1, :]
                    cx.tt("pool", za[:], Xr, Kr, ALU.mult, ["HXs", kfk], ["za"])
                    cx.tt("dve", zb[:], Xi, Ki, ALU.mult, ["HXs", kfk], ["zb"])
                    cx.tt("dve", Z[:, :, 0, :], za[:], zb[:], ALU.subtract, ["za", "zb"], ["Z"])
                    cx.tt("pool", za[:], Xr, Ki, ALU.mult, ["HXs", kfk], ["za"])
                    cx.tt("dve", zb[:], Xi, Kr, ALU.mult, ["HXs", kfk], ["zb"])
                    cx.tt("pool", Z[:, :, 1, :], za[:], zb[:], ALU.add, ["za", "zb"], ["Z"])
                    for c0 in range(0, CB, 2):
                        p = Vp[nv % 2]
                        pk = f"Vp{nv % 2}"
                        nv += 1
                        for cc in range(2):
                            cx.mm(p[:, cc, :], Z[:, :, 0, c0 + cc], H.GA[:, 0, :], True, False, ["Z", "H_tab"], [pk])
                            cx.mm(p[:, cc, :], Z[:, :, 1, c0 + cc], H.GA[:, 1, :], False, True, ["Z", "H_tab"], [pk])
                        cx.cp(cx.ev(), Vsb[:, :, :, c0:c0 + 2], p[:].rearrange("p c (r t) -> p r t c", r=2), [pk], ["Vsb"])
                    for q0 in range(0, 128, 16):
                        p = yp[ny % 2]
                        pk = f"yp{ny % 2}"
                        ny += 1
                        for q in range(16):
                            tq = q0 + q
                            cx.mm(p[:, q, :], H.TBc[:, tq, :], Vsb[:, 0, tq, :], True, False, ["H_tab", "Vsb"], [pk])
                            cx.mm(p[:, q, :], H.TBs[:, tq, :], Vsb[:, 1, tq, :], False, True, ["H_tab", "Vsb"], [pk])
                        cx.tt("dve", dst[:, q0:q0 + 16, :], p[:], gate[:, q0:q0 + 16, :], ALU.mult, [pk, gk], [dk])
                cx.dma(T["y_hc"][s, cb].rearrange("(a t) c -> a t c", t=128), yo[jb][:], [f"yo{jb}"], ["y_hc"])
            for q in range(8):
                cx.dma(rl[:], T["y_hc"][s].rearrange("b (a t) c -> a b t c", t=128)[:, :, 16 * q:16 * q + 16, :], ["y_hc"], ["rl"])
                cx.cp("pool", rl2[:], rl[:].rearrange("p b t c -> p t b c"), ["rl"], ["rl2"])
                cx.dma(T["y"][s, :, 512:768].rearrange("(a t) c -> a t c", t=128)[:, 16 * q:16 * q + 16, :],
                       rl2[:].rearrange("p t b c -> p t (b c)"), ["rl2"], [])
    S.barrier()


def phase_ML(cx, layer):
    nc, S, T = cx.nc, cx.S, cx.T
    with contextlib.ExitStack() as ph:
        sb = lambda n, s, d: ph.enter_context(cx.sbt(n, s, d))
        ps = lambda n, s, d: ph.enter_context(cx.pst(n, s, d))
        qT = sb("M_qT", [128, 2, L], BF16)
        kT = sb("M_kT", [128, 2, L], BF16)
        hO = [sb(f"M_h{d}", [128, 64, 256], BF16) for d in range(2)]
        gts = sb("M_gts", [128, 64, 16], F32)
        gb = sb("M_gb", [128, 16], F32)
        tri = sb("M_tri", [128, 2, 128], F32)
        ones = sb("M_ones", [128, 128], F32)
        nbc = sb("M_nbc", [128, 256], F32)
        lfn = [sb(f"M_lfn{d}", [128, 64, 4], F32) for d in range(2)]
        eb = [sb(f"M_eb{d}", [128, 64, 4], F32) for d in range(2)]
        rr = [sb(f"M_rr{d}", [128, 64, 4], F32) for d in range(2)]
        eg = [sb(f"M_eg{d}", [128, 64, 4], F32) for d in range(2)]
        S32 = [sb(f"M_S32{d}", [128, 2, 80], F32) for d in range(2)]
        S16 = [sb(f"M_S16{d}", [128, 2, 80], BF16) for d in range(2)]
        NB = 3
        kv = [[sb(f"M_kv{d}{i}", [128, 512], BF16) for i in range(NB)] for d in range(2)]
        vt = [[sb(f"M_vt{d}{i}", [128, 4, 80], BF16) for i in range(NB)] for d in range(2)]
        PT = [[sb(f"M_PT{d}{i}", [128, 4, 128], BF16) for i in range(NB)] for d in range(2)]
        nd = [sb(f"M_nd{d}", [128, 4, 80], F32) for d in range(2)]
        den = [sb(f"M_den{d}", [128, 4, 1], F32) for d in range(2)]
        ob = sb("M_ob", [128, 8, 256], BF16)
        sg = sb("M_sg", [128, 8, 256], F32)
        hs = sb("M_hs", [128, 8, 256], F32)
        yd = sb("M_yd", [128, 8, 256], BF16)
        pc = ps("M_pc", [128, 256], F32)
        Gp = [ps(f"M_Gp{d}", [128, 4, 128], F32) for d in range(2)]
        nump = [ps(f"M_num{d}", [128, 4, 80], F32) for d in range(2)]
        dSp = [ps(f"M_dS{d}", [128, 4, 80], F32) for d in range(2)]
        cx.dma(tri[:], T["m_tri"].rearrange("t s c -> s t c"), [], ["tri"])
        for d in range(2):
            for i in range(NB):
                cx.memset("pool", vt[d][i][:], 0.0, [f"vt{d}{i}"])
        cx.memset("pool", ones[:], 1.0, ["ones"])
        cx.dma(gb[:], T["ml_gate_b"][layer].rearrange("a b -> (a b)").partition_broadcast(128), [], ["gb"])
        for s in range(cx.nseq):
            for pr in range(2):
                cx.dma(qT[:, pr, :], T["zT_d"][s, pr * 128:(pr + 1) * 128, :], [], ["qT"])
                cx.dma(kT[:, pr, :], T["zT_d"][s, 256 + pr * 128:256 + (pr + 1) * 128, :], [], ["kT"])
            cx.dma(gts[:], T["z_g"][s], [], ["gts"])
            cx.tt("dve", gts[:], gts[:], gb[:].unsqueeze(1).to_broadcast([128, 64, 16]), ALU.add, ["gts", "gb"], ["gts"])
            import os
            STG = int(os.environ.get("ML_STAGE", "9"))
            for d in range(2 if STG not in (-2, -3) else 0):
                if STG == -5:
                    cx.act(lfn[d][:], gts[:, :, 4 + 8 * d:8 + 8 * d], AF.Exp, ["gts"], [f"lfn{d}"], scale=-1.0)
                    continue
                fcol = slice(4 + 8 * d, 8 + 8 * d)
                icol = slice(8 * d, 4 + 8 * d)
                cx.act(lfn[d][:], gts[:, :, fcol], AF.Exp, ["gts"], [f"lfn{d}"], scale=-1.0)
                cx.act(lfn[d][:], lfn[d][:], AF.Ln, [f"lfn{d}", "ones"], [f"lfn{d}"], bias=ones[:, 0:1])
                if STG == -4:
                    continue
                lv = lfn[d][:].rearrange("p n h -> p (n h)")
                cx.mm(pc[:], tri[:, d, :], lv, True, True, ["tri", f"lfn{d}"], ["pc"])
                cx.act(eb[d][:].rearrange("p n h -> p (n h)"), pc[:], AF.Exp, ["pc"], [f"eb{d}"], scale=-1.0)
                if STG == -6:
                    continue
                cx.act(nbc[:], pc[:], AF.Copy, ["pc"], ["nbc"])
                cx.tt("dve", rr[d][:], nbc[:].rearrange("p (n h) -> p n h", h=4), gts[:, :, icol], ALU.add, ["nbc", "gts"], [f"rr{d}"])
                cx.act(rr[d][:], rr[d][:], AF.Exp, [f"rr{d}"], [f"rr{d}"])
                cx.ts("dve", rr[d][:], rr[d][:], 0.125, None, ALU.mult, None, [f"rr{d}"], [f"rr{d}"])
                if STG == -7:
                    continue
                cx.mm(pc[:], ones[:], lv, True, True, ["ones", f"lfn{d}"], ["pc"])
                cx.act(eg[d][:].rearrange("p n h -> p (n h)"), pc[:], AF.Exp, ["pc"], [f"eg{d}"], scale=-1.0)
                cx.memset("pool", S32[d][:], 0.0, [f"S32{d}"])
                cx.memset("pool", S16[d][:], 0.0, [f"S16{d}"])
            import os
            STG = int(os.environ.get("ML_STAGE", "9"))
            for i in range(64 if STG >= 1 else 0):
                for d in range(2):
                    n = i if d == 0 else 63 - i
                    b = i % NB
                    tk = slice(n * 128, (n + 1) * 128)
                    kvk, vtk, ptk = f"kv{d}{b}", f"vt{d}{b}", f"PT{d}{b}"
                    cx.dma(kv[d][b][:], T["z_d"][s, tk, 0:512], [], [kvk])
                    for h in (0, 2, 1, 3):
                        P = slice((h % 2) * 64, (h % 2) * 64 + 64)
                        cx.mm(Gp[d][:, h, :], kT[P, h // 2, tk], qT[P, h // 2, tk], True, True, ["kT", "qT"], [f"Gp{d}"])
                    if STG < 2:
                        continue
                    cx.tt("dve", PT[d][b][:], Gp[d][:], tri[:, d, :].unsqueeze(1).to_broadcast([128, 4, 128]), ALU.mult, [f"Gp{d}", "tri"], [ptk])
                    rv = rr[d][:, n, :].unsqueeze(2)
                    cx.tt("dve", vt[d][b][:, :, 0:64], kv[d][b][:, 256:512].rearrange("p (h e) -> p h e", h=4), rv.to_broadcast([128, 4, 64]), ALU.mult, [kvk, f"rr{d}"], [vtk])
                    cx.cp("act", vt[d][b][:, :, 64:65], rv, [f"rr{d}"], [vtk])
                    for h in (0, 2, 1, 3):
                        P = slice((h % 2) * 64, (h % 2) * 64 + 64)
                        cx.mm(nump[d][:, h, :], PT[d][b][:, h, :], vt[d][b][:, h, :], True, False, [ptk, vtk], [f"num{d}"])
                        cx.mm(nump[d][:, h, :], qT[P, h // 2, tk], S16[d][P, h // 2, :], False, True, ["qT", f"S16{d}"], [f"num{d}"])
                    if STG < 3:
                        continue
                    for h in (0, 2, 1, 3):
                        pr = h // 2
                        cx.mm(dSp[d][:, h, :], kv[d][b][:, pr * 128:(pr + 1) * 128], vt[d][b][:, h, :], True, True, [kvk, vtk], [f"dS{d}"])
                    for h2 in range(2):
                        P = slice(h2 * 64, h2 * 64 + 64)
                        dsv = dSp[d][P, :, :].rearrange("p (a b) e -> p a b e", b=2)[:, :, h2, :]
                        egv = eg[d][P, n, :].rearrange("p (a b) -> p a b", b=2)[:, :, h2].unsqueeze(2).to_broadcast([64, 2, 80])
                        cx.tt("dve", S32[d][P], S32[d][P], dsv, ALU.add, [f"S32{d}", f"dS{d}"], [f"S32{d}"])
                        cx.tt("dve", S32[d][P], S32[d][P], egv, ALU.mult, [f"S32{d}", f"eg{d}"], [f"S32{d}"])
                        cx.cp("act", S16[d][P], S32[d][P], [f"S32{d}"], [f"S16{d}"])
                    if STG < 4:
                        continue
                    cx.tt("dve", nd[d][:], nump[d][:], eb[d][:, n, :].unsqueeze(2).to_broadcast([128, 4, 80]), ALU.mult, [f"num{d}", f"eb{d}"], [f"nd{d}"])
                    cx.act(den[d][:], nd[d][:, :, 64:65], AF.Abs, [f"nd{d}"], [f"den{d}"])
                    cx.ts("dve", den[d][:], den[d][:], 1.0, None, ALU.max, None, [f"den{d}"], [f"den{d}"])
                    cx.recip(den[d][:], den[d][:], [f"den{d}"], [f"den{d}"])
                    cx.tt("dve", hO[d][:, n, :].rearrange("p (h e) -> p h e", h=4), nd[d][:, :, 0:64], den[d][:].to_broadcast([128, 4, 64]), ALU.mult,
                          [f"nd{d}", f"den{d}"], [f"hO{d}"])
            for n0 in range(0, 64 if STG not in (-1, -3, -4, -5, -6, -7) else 0, 8):
                cx.dma(ob[:], T["z_d"][s, n0 * 128:(n0 + 8) * 128, 512:768].rearrange("(n p) c -> p n c", p=128), [], ["ob"])
                cx.act(sg[:], ob[:], AF.Sigmoid, ["ob"], ["sg"])
                cx.tt("dve", hs[:], hO[0][:, n0:n0 + 8, :], hO[1][:, n0:n0 + 8, :], ALU.add, ["hO0", "hO1"], ["hs"])
                cx.tt("dve", yd[:], sg[:], hs[:], ALU.mult, ["sg", "hs"], ["yd"])
                cx.dma(T["y"][s, n0 * 128:(n0 + 8) * 128, 768:1024].rearrange("(n p) c -> p n c", p=128), yd[:], ["yd"], [])
    S.barrier()


def phase_C1(cx, layer):
    nc, S, T = cx.nc, cx.S, cx.T
    with contextlib.ExitStack() as ph:
        sb = lambda n, s, d: ph.enter_context(cx.sbt(n, s, d))
        ps = lambda n, s, d: ph.enter_context(cx.pst(n, s, d))
        wo = sb("C_wo", [128, 8, D], BF16)
        wg = sb("C_wg", [128, 8, DFF], BF16)
        wu = sb("C_wu", [128, 8, DFF], BF16)
        identb = sb("C_idb", [128, 128], BF16)
        cx.dma(identb[:], T["ident_b"][:, :], [], ["identb"])
        stg = make_staging(cx, ph, 8, 128, "Cstg")
        load_scaled_weight(cx, ph, wo, "C_wo", T["w_out"][layer], T["out_norm"][layer], 8, D, "Cwo", stg)
        load_scaled_weight(cx, ph, wg, "C_wg", T["w_gate"][layer], T["norm_ffn"][layer], 8, DFF, "Cwg", stg)
        load_scaled_weight(cx, ph, wu, "C_wu", T["w_up"][layer], T["norm_ffn"][layer], 8, DFF, "Cwu", stg)
        yin = sb("C_yin", [128, 4, D], BF16)
        xin = sb("C_xin", [128, 4, D], F32)
        sq = sb("C_sq", [128, D], F32)
        ssh = sb("C_ssh", [128, 4, 16], F32)
        yT = sb("C_yT", [128, 8, 512], BF16)
        xs = sb("C_xs", [128, 4, D], BF16)
        junk = sb("C_junk", [128, D], BF16)
        ss = sb("C_ss", [128, 4], F32)
        rs = sb("C_rs", [128, 4], F32)
        hT = [sb(f"C_hT{i}", [128, 8, 512], BF16) for i in range(2)]
        sgt = [sb(f"C_sg{i}", [128, 512], F32) for i in range(2)]
        hh = [sb(f"C_hh{i}", [128, 512], BF16) for i in range(3)]
        pTs = [ps(f"C_pT{i}", [128, 8, 128], BF16) for i in range(2)]
        pm = [ps(f"C_pm{i}", [128, 512], F32) for i in range(2)]
        pg = [ps(f"C_pg{i}", [128, 512], F32) for i in range(2)]
        pu = [ps(f"C_pu{i}", [128, 512], F32) for i in range(2)]
        src = T["xs"] if layer == 0 else T["xres"]
        NT = L // 512
        cnt = {"pm": 0, "nf": 0}

        def front(it):
            s, i = divmod(it, NT)
            t0 = i * 512
            b = it % 2
            hk = f"hT{b}"
            tv = lambda ap: ap[s, t0:t0 + 512, :].rearrange("(j p) d -> p j d", p=128)
            cx.dma(yin[:], tv(T["y"]), [], ["yin"])
            cx.dma(xin[:], tv(src), [], ["xin"])
            yield
            for j in range(4):
                cx.tt("pool", sq[:], yin[:, j, :], yin[:, j, :], ALU.mult, ["yin"], ["sq"])
                cx.S.op("dve", lambda e, j=j: e.tensor_reduce(out=ssh[:, j, :], in_=sq[:].rearrange("p (g c) -> p g c", c=64), axis=AX.X, op=ALU.add),
                        reads=["sq"], writes=["ssh"])
                yield
            cx.rstd(ssh[:], ssh[:], 1.0 / 64, ["ssh"], ["ssh"], "C")
            yield
            for j in range(4):
                yv = yin[:, j, :].rearrange("p (g c) -> p g c", c=64)
                cx.tt("dve", yv, yv, ssh[:, j, :].unsqueeze(2).to_broadcast([128, 16, 64]), ALU.mult, ["yin", "ssh"], ["yin"])
                yield
            for j in range(4):
                pT = pTs[j % 2]
                pk = f"CpT{j % 2}"
                for k in range(8):
                    cx.tr(pT[:, k, :], yin[:, j, k * 128:(k + 1) * 128], identb[:], ["yin", "identb"], [pk])
                cx.cp(cx.ev(), yT[:, :, j * 128:(j + 1) * 128], pT[:], [pk], ["yT"])
                yield
            for j in range(4):
                tj = slice(j * 128, (j + 1) * 128)
                for c0 in (0, 512):
                    p = pm[cnt["pm"] % 2]
                    pk = f"Cpm{cnt['pm'] % 2}"
                    cnt["pm"] += 1
                    for k in range(8):
                        cx.mm(p[:], yT[:, k, tj], wo[:, k, c0:c0 + 512], k == 0, k == 7, ["yT", "C_wo"], [pk])
                    cx.tt("dve", xin[:, j, c0:c0 + 512], p[:], xin[:, j, c0:c0 + 512], ALU.add, [pk, "xin"], ["xin"])
                    yield
            cx.dma(tv(T["x1"]), xin[:], ["xin"], [])
            yield from norm_T4(cx, xin, "xin", xs, "C_xs", junk, ss, rs, pTs, hT[b], hk, identb, "C")

        def back(it):
            s, i = divmod(it, NT)
            t0 = i * 512
            b = it % 2
            hk = f"hT{b}"
            for f in range(DFF // 128):
                nf = cnt["nf"]
                cnt["nf"] += 1
                q = nf % 2
                q3 = nf % 3
                fs = slice(f * 128, (f + 1) * 128)
                for k in range(8):
                    cx.mm(pg[q][:], wg[:, k, fs], hT[b][:, k, :], k == 0, k == 7, ["C_wg", hk], [f"pg{q}"])
                for k in range(8):
                    cx.mm(pu[q][:], wu[:, k, fs], hT[b][:, k, :], k == 0, k == 7, ["C_wu", hk], [f"pu{q}"])
                cx.act(sgt[q][:], pg[q][:], AF.Silu, [f"pg{q}"], [f"sg{q}"])
                cx.tt("dve", hh[q3][:], pu[q][:], sgt[q][:], ALU.mult, [f"pu{q}", f"sg{q}"], [f"hh{q3}"])
                cx.dma(T["hscr"][s, fs, t0:t0 + 512], hh[q3][:], [f"hh{q3}"], [])
                yield

        drive(front, back, cx.nseq * NT, k=2)
    S.barrier()


def phase_C2(cx, layer):
    nc, S, T = cx.nc, cx.S, cx.T
    last = layer == DEPTH - 1
    with contextlib.ExitStack() as ph:
        sb = lambda n, s, d: ph.enter_context(cx.sbt(n, s, d))
        ps = lambda n, s, d: ph.enter_context(cx.pst(n, s, d))
        wd = sb("E_wd", [128, 22, D], BF16)
        wpg = sb("E_wpg", [128, 8, D], BF16)
        wpp = sb("E_wpp", [128, 2, D], BF16)
        identb = sb("E_idb", [128, 128], BF16)
        gfin = sb("E_gfin", [128, D], F32)
        cx.dma(identb[:], T["ident_b"][:, :], [], ["identb"])
        cx.dma(gfin[:], T["final_norm"].partition_broadcast(128), [], ["gfin"])
        stg = make_staging(cx, ph, 22, 128, "Estg")
        load_scaled_weight(cx, ph, wd, "E_wd", T["w_down"][layer], None, 22, D, "Ewd", stg)
        load_scaled_weight(cx, ph, wpg, "E_wpg", T["w_ple_gate"][layer], T["ple_norm"][layer], 8, D, "Ewpg", stg)
        load_scaled_weight(cx, ph, wpp, "E_wpp", T["w_ple_proj"][layer], None, 2, D, "Ewpp", stg)
        hT = [sb(f"E_hT{i}", [128, 22, 512], BF16) for i in range(2)]
        x1 = [sb(f"E_x1{i}", [128, 4, D], F32) for i in range(2)]
        pin = sb("E_pin", [128, 4, 256], F32)
        pbf = sb("E_pbf", [128, 4, 256], BF16)
        pTs_ = sb("E_pTs", [128, 2, 512], BF16)
        xs = sb("E_xs", [128, 4, D], BF16)
        junk = sb("E_junk", [128, D], BF16)
        ss = sb("E_ss", [128, 4], F32)
        rs = sb("E_rs", [128, 4], F32)
        xT = sb("E_xT", [128, 8, 512], BF16)
        sgt = [sb(f"E_sg{i}", [128, 512], F32) for i in range(2)]
        pTp = [ps(f"E_pT{i}", [128, 8, 128], BF16) for i in range(2)]
        pm = [ps(f"E_pm{i}", [128, 512], F32) for i in range(2)]
        pg = [ps(f"E_pg{i}", [128, 512], F32) for i in range(2)]
        pp = [ps(f"E_pp{i}", [128, 512], F32) for i in range(2)]
        NT = L // 512
        cnt = {"pm": 0, "ng": 0}

        def front(it):
            s, i = divmod(it, NT)
            t0 = i * 512
            b = it % 2
            hk, xk = f"EhT{b}", f"Ex1{b}"
            tv = lambda ap: ap[s, t0:t0 + 512, :].rearrange("(j p) d -> p j d", p=128)
            cx.dma(hT[b][:], T["hscr"][s, :, t0:t0 + 512].rearrange("(f p) t -> p f t", p=128), [], [hk])
            cx.dma(x1[b][:], tv(T["x1"]), [], [xk])
            yield
            for j in range(4):
                tj = slice(j * 128, (j + 1) * 128)
                for c0 in (0, 512):
                    p = pm[cnt["pm"] % 2]
                    pk = f"Epm{cnt['pm'] % 2}"
                    cnt["pm"] += 1
                    for f in range(22):
                        cx.mm(p[:], hT[b][:, f, tj], wd[:, f, c0:c0 + 512], f == 0, f == 21, [hk, "E_wd"], [pk])
                    cx.tt("dve", x1[b][:, j, c0:c0 + 512], p[:], x1[b][:, j, c0:c0 + 512], ALU.add, [pk, xk], [xk])
                    yield

        def back(it):
            s, i = divmod(it, NT)
            t0 = i * 512
            b = it % 2
            xk = f"Ex1{b}"
            tv = lambda ap: ap[s, t0:t0 + 512, :].rearrange("(j p) d -> p j d", p=128)
            cx.dma(pin[:], T["ps"][layer, s, t0:t0 + 512, :].rearrange("(j p) d -> p j d", p=128), [], ["pin"])
            cx.cp("pool", pbf[:], pin[:], ["pin"], ["pbf"])
            yield
            yield from norm_T4(cx, x1[b], xk, xs, "E_xs", junk, ss, rs, pTp, xT, "ExT", identb, "E")
            for j in range(4):
                pT = pTp[j % 2]
                pk = f"EpT{j % 2}"
                for k in range(2):
                    cx.tr(pT[:, k, :], pbf[:, j, k * 128:(k + 1) * 128], identb[:], ["pbf", "identb"], [pk])
                cx.cp(cx.ev(), pTs_[:, :, j * 128:(j + 1) * 128], pT[:, 0:2, :], [pk], ["pTs"])
                yield
            for j in range(4):
                tj = slice(j * 128, (j + 1) * 128)
                for c0 in (0, 512):
                    q = cnt["ng"] % 2
                    cnt["ng"] += 1
                    for k in range(8):
                        cx.mm(pg[q][:], xT[:, k, tj], wpg[:, k, c0:c0 + 512], k == 0, k == 7, ["ExT", "E_wpg"], [f"Epg{q}"])
                    for k in range(2):
                        cx.mm(pp[q][:], pTs_[:, k, tj], wpp[:, k, c0:c0 + 512], k == 0, k == 1, ["pTs", "E_wpp"], [f"Epp{q}"])
                    cx.act(sgt[q][:], pg[q][:], AF.Sigmoid, [f"Epg{q}"], [f"Esg{q}"])
                    cx.tt("dve", sgt[q][:], pp[q][:], sgt[q][:], ALU.mult, [f"Epp{q}", f"Esg{q}"], [f"Esg{q}"])
                    cx.tt("pool", x1[b][:, j, c0:c0 + 512], x1[b][:, j, c0:c0 + 512], sgt[q][:], ALU.add, [xk, f"Esg{q}"], [xk])
                    yield
            if not last:
                cx.dma(tv(T["xres"]), x1[b][:], [xk], [])
            else:
                for j in range(4):
                    cx.act(junk[:], x1[b][:, j, :], AF.Square, [xk], ["Ejunk", "Ess"], accum_out=ss[:, j:j + 1])
                cx.rstd(rs[:], ss[:], 1.0 / D, ["Ess"], ["Ers"], "E")
                for j in range(4):
                    cx.stt("dve", x1[b][:, j, :], x1[b][:, j, :], rs[:, j:j + 1], gfin[:], ALU.mult, ALU.mult, [xk, "Ers", "gfin"], [xk])
                cx.dma(tv(T["out"]), x1[b][:], [xk], [])
            yield

        drive(front, back, cx.nseq * NT, k=2)
    S.barrier()


_CONSTS = None


def bf(a):
    return np.asarray(a, dtype=np.float32).astype(ml_dtypes.bfloat16)


def host_consts():
    global _CONSTS
    if _CONSTS is not None:
        return _CONSTS
    c = {}
    pi2 = 2.0 * np.pi
    c["ident_b"] = bf(np.eye(128))
    m = np.arange(64)
    ang = pi2 * np.outer(m, m) / 64.0
    sc = 1.0 / math.sqrt(L * 64.0)
    cs = np.zeros((64, 2, 2, 128), np.float32)
    for ab, mat in enumerate((np.cos(ang) * sc, np.sin(ang) * sc)):
        cs[:, ab, 0, 0:64] = mat
        cs[:, ab, 1, 64:128] = mat
    c["cs64"] = cs
    l1 = np.arange(64)[:, None]
    k1 = np.arange(64)[None, :]
    th = pi2 * l1 * k1 / 64.0
    F1 = np.zeros((64, 2, 128))
    F1[:, 0, 0:64] = np.cos(th)
    F1[:, 0, 64:128] = -np.sin(th)
    F1[:, 1, 0:64] = -np.sin(th)
    F1[:, 1, 64:128] = -np.cos(th)
    c["f_F1"] = bf(F1)
    l2 = np.arange(128)[:, None, None]
    k1 = np.arange(64)[None, :, None]
    k2 = np.arange(128)[None, None, :]
    ph = pi2 * (((k1 + 64 * k2) * l2) % L) / L
    c["f_T2c"] = bf(np.cos(ph))
    c["f_T2s"] = bf(np.sin(ph))
    l1 = np.arange(128)[:, None]
    k1 = np.arange(65)[None, :]
    th = pi2 * l1 * k1 / 128.0
    c["h_F1"] = bf(np.concatenate([np.cos(th), -np.sin(th), -np.cos(th)], axis=1))
    l2 = np.arange(128)[:, None, None]
    k1 = np.arange(65)[None, :, None]
    k2 = np.arange(128)[None, None, :]
    ph = pi2 * (((k1 + 128 * k2) * l2) % NFFT) / NFFT
    c["h_T2c"] = bf(np.cos(ph))
    c["h_T2s"] = bf(np.sin(ph))
    k2 = np.arange(128)[:, None]
    t2 = np.arange(128)[None, :]
    psi = pi2 * k2 * t2 / 128.0
    GA = np.zeros((128, 2, 256))
    GA[:, 0, 0:128] = np.cos(psi)
    GA[:, 0, 128:256] = np.sin(psi)
    GA[:, 1, 0:128] = -np.sin(psi)
    GA[:, 1, 128:256] = np.cos(psi)
    c["h_GA"] = bf(GA)
    k1 = np.arange(65)[:, None, None]
    t2 = np.arange(128)[None, :, None]
    t1 = np.arange(64)[None, None, :]
    om = pi2 * ((k1 * (128 * t1 + t2)) % NFFT) / NFFT
    wk = np.where((k1 == 0) | (k1 == 64), 1.0, 2.0) / NFFT
    c["h_TBc"] = bf(wk * np.cos(om))
    c["h_TBs"] = bf(-wk * np.sin(om))
    R = np.arange(2 * L)
    pos = np.where(R < L, R, 2 * L - R)
    pos[L] = 0
    s = pos.astype(np.float32)
    t = (s / np.float32(L - 1)).astype(np.float32)
    angp = (np.float32(2.0 * math.pi / L) * s).astype(np.float32)
    bands = np.linspace(1e-4, 15.0, 16, dtype=np.float32)
    fb = angp[:, None] * bands[None, :]
    feats = np.concatenate([t[:, None], np.cos(fb), -np.sin(fb)], axis=-1).astype(np.float32)
    c["h_feats"] = np.ascontiguousarray(feats.T)
    c["h_tvec"] = np.ascontiguousarray(t[None, :])
    si = np.arange(128)[:, None]
    ci = np.arange(128)[None, :]
    c["m_tri"] = np.stack([(si <= ci), (si >= ci)]).astype(np.float32)
    idx = np.zeros((8, 2, 128, 512, 3), np.int64)
    msk = np.zeros((8, 2, 128, 512), bool)
    reps = [0, 1, 2, 3, 64, 125, 126, 127]
    for v, r in enumerate(reps):
        kr0 = min(max(r - 4, 0), 120)
        for hp in range(2):
            for h in range(2):
                for q in range(64):
                    kc0 = min(max(q - 8, 0), 48)
                    for j in range(8):
                        dr = kr0 + j - r + 7
                        kc = np.arange(kc0, kc0 + 16)
                        n = j * 64 + kc
                        idx[v, hp, h * 64 + q, n, 0] = 2 * hp + h
                        idx[v, hp, h * 64 + q, n, 1] = dr
                        idx[v, hp, h * 64 + q, n, 2] = kc - q + 15
                        msk[v, hp, h * 64 + q, n] = True
    c["_na_idx"] = idx
    c["_na_msk"] = msk
    _CONSTS = c
    return c


CONST_SPECS = [("ident_b", [128, 128], BF16), ("cs64", [64, 2, 2, 128], F32), ("f_F1", [64, 2, 128], BF16),
               ("f_T2c", [128, 64, 128], BF16), ("f_T2s", [128, 64, 128], BF16), ("h_F1", [128, 195], BF16),
               ("h_T2c", [128, 65, 128], BF16), ("h_T2s", [128, 65, 128], BF16), ("h_GA", [128, 2, 256], BF16),
               ("h_TBc", [65, 128, 64], BF16), ("h_TBs", [65, 128, 64], BF16), ("h_feats", [33, 2 * L], F32),
               ("h_tvec", [1, 2 * L], F32), ("m_tri", [2, 128, 128], F32)]

WEIGHT_SPECS = [("norm_mix", [DEPTH, D]), ("w_in", [DEPTH, D, NIN]), ("fnet_w", [DEPTH, 4, 64, 64]),
                ("hy_conv_w", [DEPTH, 3, 768]), ("hy_conv_b", [DEPTH, 768]), ("hy_w1", [DEPTH, 33, 64]), ("hy_b1", [DEPTH, 64]),
                ("hy_freq", [DEPTH, 2, 64]), ("hy_w2", [DEPTH, 64, 64]), ("hy_b2", [DEPTH, 64]), ("hy_w3", [DEPTH, 64, 1024]),
                ("hy_decay", [DEPTH, 2, 2, 256]), ("hy_skip", [DEPTH, 2, 256]), ("ml_gate_b", [DEPTH, 4, 4]),
                ("out_norm", [DEPTH, D]), ("w_out", [DEPTH, D, D]), ("norm_ffn", [DEPTH, D]), ("w_gate", [DEPTH, D, DFF]),
                ("w_up", [DEPTH, D, DFF]), ("w_down", [DEPTH, DFF, D]), ("ple_norm", [DEPTH, D]), ("w_ple_gate", [DEPTH, D, D]),
                ("w_ple_proj", [DEPTH, 256, D]), ("final_norm", [D])]


def build(nseq, debug=False, phases=None, scratch_in=()):
    nc = bass.Bass("TRN2", target_bir_lowering=False)
    T = {}

    def din(name, shape, dt=F32):
        T[name] = nc.dram_tensor(name, list(shape), dt, kind="ExternalInput").ap()

    def dscr(name, shape, dt):
        kind = "ExternalInput" if name in scratch_in else ("ExternalOutput" if debug else "Internal")
        T[name] = nc.dram_tensor(name, list(shape), dt, kind=kind).ap()

    din("xs", [nseq, L, D])
    din("ps", [DEPTH, nseq, L, 256])
    for n, shp in WEIGHT_SPECS:
        din(n, shp)
    din("na_bias", [DEPTH, 16, 128, 512])
    for n, shp, dt in CONST_SPECS:
        din(n, shp, dt)
    T["out"] = nc.dram_tensor("out", [nseq, L, D], F32, kind="ExternalOutput").ap()
    dscr("zT_a", [nseq, 512, L], BF16)
    dscr("z_va", [nseq, L, 256], BF16)
    dscr("z_fn", [nseq, L, 512], BF16)
    dscr("z_hy", [nseq, L + 2, 768], BF16)
    dscr("z_hc", [nseq, 768 // CB, L, CB], BF16)
    dscr("y_hc", [nseq, 256 // CB, L, CB], BF16)
    dscr("h_ftm", [2 * L, 512], BF16)
    dscr("zT_d", [nseq, 512, L], BF16)
    dscr("z_d", [nseq, L, 768], BF16)
    dscr("z_g", [nseq, 128, 64, 16], F32)
    dscr("y", [nseq, L, D], BF16)
    dscr("x1", [nseq, L, D], F32)
    dscr("xres", [nseq, L, D], F32)
    dscr("hscr", [nseq, DFF, L], BF16)
    dscr("h_full", [512 // CB, 2 * L, CB], BF16)
    dscr("h_KF", [512 // CB, 128, 65 * 2 * CB], BF16)
    S = Sched(nc)
    cx = Ctx(nc, S, T, nseq)
    with contextlib.ExitStack() as ph:
        zt = ph.enter_context(cx.sbt("zpad", [1, 768], BF16))
        cx.memset("pool", zt[:], 0.0, ["zpad"])
        for s in range(nseq):
            if "z_hy" in scratch_in:
                break
            cx.dma(T["z_hy"][s, 0:1, :], zt[:], ["zpad"], [])
            cx.dma(T["z_hy"][s, L + 1:L + 2, :], zt[:], ["zpad"], [])
    S.barrier()
    allp = ["A", "NA", "FN", "HYF", "HC", "HY", "ML", "C1", "C2"]
    fns = {"A": phase_A, "NA": phase_NA, "FN": phase_FN, "HYF": phase_HYF, "HC": phase_HC, "HY": phase_HY, "ML": phase_ML, "C1": phase_C1, "C2": phase_C2}
    for layer in range(DEPTH):
        for pn in allp:
            if phases is None or (layer, pn) in phases:
                fns[pn](cx, layer)
    S.emit()
    return nc


def na_bias_host(rpb):
    c = host_consts()
    idx, msk = c["_na_idx"], c["_na_msk"]
    out = np.empty((DEPTH, 8, 2, 128, 512), np.float32)
    for l in range(DEPTH):
        g = rpb[l][idx[..., 0], idx[..., 1], idx[..., 2]]
        out[l] = np.where(msk, g, np.float32(-30000.0))
    return out.reshape(DEPTH, 16, 128, 512)


_NC_CACHE = {}


def kernel(**inputs):
    nseq = 2
    n_cores = 8
    if nseq not in _NC_CACHE:
        _NC_CACHE[nseq] = build(nseq)
    nc = _NC_CACHE[nseq]
    c = host_consts()
    f32 = lambda a: np.ascontiguousarray(np.asarray(a, dtype=np.float32))
    shared = {n: f32(inputs[n]) for n, _ in WEIGHT_SPECS}
    shared["hy_w3"] = f32(inputs["hy_w3"])
    shared["na_bias"] = na_bias_host(f32(inputs["attn_rpb"]))
    for n, _, _ in CONST_SPECS:
        shared[n] = c[n]
    xp, xsm = f32(inputs["x_prompt"]), f32(inputs["x_sample"])
    pp, psm = f32(inputs["p_prompt"]), f32(inputs["p_sample"])
    in_maps = []
    for core in range(n_cores):
        m = dict(shared)
        m["xs"] = np.ascontiguousarray(np.stack([xsm[core], xp[core % 2]]))
        m["ps"] = np.ascontiguousarray(np.stack([psm[:, core], pp[:, core % 2]], axis=1))
        in_maps.append(m)
    res = run_bass_kernel_spmd(nc, in_maps, core_ids=list(range(n_cores)))
    y_sample = np.stack([np.asarray(res.results[core]["out"][0], dtype=np.float32) for core in range(n_cores)])
    y_prompt = np.stack([np.asarray(res.results[core]["out"][1], dtype=np.float32) for core in range(2)])
    return (y_prompt, y_sample)
```

```python
import math
import contextlib
import numpy as np
import ml_dtypes
import concourse.bass as bass
import concourse.mybir as mybir
from concourse.bass_utils import run_bass_kernel_spmd

F32 = mybir.dt.float32
BF16 = mybir.dt.bfloat16
ALU = mybir.AluOpType
AF = mybir.ActivationFunctionType
AX = mybir.AxisListType

L = 8192
D = 1024
NIN = 2832
DFF = 2816
DEPTH = 2
EPS = 1e-6
NFFT = 16384
ENGS = ("pe", "act", "dve", "pool", "sp")
NDMA = 8


class Sched:
    def __init__(self, nc):
        self.nc = nc
        self.ops = {e: [] for e in ENGS}
        self.cnt = {e: 0 for e in ENGS}
        self.seen = {e: {} for e in ENGS}
        self.last_w = {}
        self.readers = {}
        self.dma_use = {e: [0] * NDMA for e in ENGS}
        self.dma_rr = {e: 0 for e in ENGS}

    def _need(self, eng, tok, waits):
        key, val = tok
        if self.seen[eng].get(key, 0) >= val:
            return
        self.seen[eng][key] = val
        waits.append(tok)

    def op(self, eng, fn, reads=(), writes=(), dma=False, self_wait=False):
        waits = []
        same = ("c", eng)
        if self_wait and self.cnt[eng]:
            self._need(eng, (same, self.cnt[eng]), waits)
        for r in reads:
            t = self.last_w.get(r)
            if t is not None:
                self._need(eng, t, waits)
        strict = dma or eng != "pe"
        for w in writes:
            t = self.last_w.get(w)
            if t is not None and (strict or t[0] != same):
                self._need(eng, t, waits)
            for k, v in self.readers.get(w, {}).items():
                if strict or k != same:
                    self._need(eng, (k, v), waits)
        if dma:
            i = self.dma_rr[eng]
            self.dma_rr[eng] = (i + 1) % NDMA
            key = ("d", eng, i)
            prev = self.dma_use[eng][i]
            if prev:
                self._need(eng, (key, prev), waits)
            self.dma_use[eng][i] = prev + 16
            tok = (key, prev + 16)
            inc = 16
        else:
            self.cnt[eng] += 1
            tok = (same, self.cnt[eng])
            inc = 1
        self.ops[eng].append((waits, fn, tok[0], inc))
        for w in writes:
            self.last_w[w] = tok
            self.readers[w] = {}
        for r in reads:
            d = self.readers.setdefault(r, {})
            if d.get(tok[0], 0) < tok[1]:
                d[tok[0]] = tok[1]
        return tok

    def latest(self):
        latest = {}
        for e in ENGS:
            if self.cnt[e]:
                latest[("c", e)] = self.cnt[e]
            for i, v in enumerate(self.dma_use[e]):
                if v:
                    latest[("d", e, i)] = v
        return latest

    def barrier(self):
        latest = self.latest()
        for e in ENGS:
            waits = []
            for k, v in latest.items():
                if k == ("c", e):
                    continue
                self._need(e, (k, v), waits)
            if waits:
                self.ops[e].append((waits, None, None, 0))
        self.last_w = {}
        self.readers = {}

    def emit(self):
        nc = self.nc
        final = self.latest()
        keys = set()
        for e in ENGS:
            for waits, fn, key, inc in self.ops[e]:
                if key is not None:
                    keys.add(key)
        keys = sorted(keys)
        with contextlib.ExitStack() as st:
            sems = {}
            for k in keys:
                sems[k] = st.enter_context(nc.semaphore("s_" + "_".join(str(x) for x in k)))
            block = st.enter_context(nc.Block())

            def run(e):
                def body(engine):
                    for waits, fn, key, inc in self.ops[e]:
                        for (k, v) in waits:
                            engine.wait_ge(sems[k], v)
                        if fn is not None:
                            fn(engine).then_inc(sems[key], inc)
                    if e == "sp":
                        for k, v in final.items():
                            engine.wait_ge(sems[k], v)
                return body

            block.tensor(run("pe"))
            block.scalar(run("act"))
            block.vector(run("dve"))
            block.gpsimd(run("pool"))
            block.sync(run("sp"))


class Ctx:
    def __init__(self, nc, S, T, nseq):
        self.nc, self.S, self.T, self.nseq = nc, S, T, nseq
        self._ev = 0

    def sbt(self, name, shape, dt):
        self._uid = getattr(self, "_uid", 0) + 1
        return self.nc.sbuf_tensor(f"{name}_u{self._uid}", shape, dt)

    def pst(self, name, shape, dt):
        self._uid = getattr(self, "_uid", 0) + 1
        return self.nc.psum_tensor(f"{name}_u{self._uid}", shape, dt)

    def dma(self, out, in_, r, w):
        return self.S.op("sp", lambda e: e.dma_start(out=out, in_=in_), reads=r, writes=w, dma=True)

    def mm(self, out, lhsT, rhs, start, stop, r, w):
        sig = (lhsT.base_partition(), lhsT.shape[0])
        sw = getattr(self, "_pe_sig", None) not in (None, sig)
        self._pe_sig = sig
        return self.S.op("pe", lambda e: e.matmul(out=out, lhsT=lhsT, rhs=rhs, start=start, stop=stop), reads=r, writes=w, self_wait=sw)

    def tr(self, out, in_, ident, r, w):
        sig = (in_.base_partition(), in_.shape[0])
        sw = getattr(self, "_pe_sig", None) not in (None, sig)
        self._pe_sig = sig
        return self.S.op("pe", lambda e: e.transpose(out=out, in_=in_, identity=ident), reads=r, writes=w, self_wait=sw)

    def act(self, out, in_, func, r, w, **kw):
        return self.S.op("act", lambda e: e.activation(out=out, in_=in_, func=func, **kw), reads=r, writes=w)

    def cp(self, eng, out, in_, r, w):
        if eng == "act":
            return self.act(out, in_, AF.Copy, r, w)
        return self.S.op(eng, lambda e: e.tensor_copy(out=out, in_=in_), reads=r, writes=w)

    def ev(self):
        self._ev ^= 1
        return "act" if self._ev else "dve"

    def ts(self, eng, out, in0, s1, s2, op0, op1, r, w):
        if op1 is None:
            return self.S.op(eng, lambda e: e.tensor_scalar(out=out, in0=in0, scalar1=s1, scalar2=None, op0=op0), reads=r, writes=w)
        return self.S.op(eng, lambda e: e.tensor_scalar(out=out, in0=in0, scalar1=s1, scalar2=s2, op0=op0, op1=op1), reads=r, writes=w)

    def tt(self, eng, out, in0, in1, op, r, w):
        return self.S.op(eng, lambda e: e.tensor_tensor(out=out, in0=in0, in1=in1, op=op), reads=r, writes=w)

    def stt(self, eng, out, in0, scalar, in1, op0, op1, r, w):
        return self.S.op(eng, lambda e: e.scalar_tensor_tensor(out=out, in0=in0, scalar=scalar, in1=in1, op0=op0, op1=op1), reads=r, writes=w)

    def memset(self, eng, ap, val, w):
        return self.S.op(eng, lambda e: e.memset(ap, val), writes=w)

    def recip(self, out, in_, r, w):
        return self.S.op("dve", lambda e: e.reciprocal(out=out, in_=in_), reads=r, writes=w)

    def rstd(self, out, in_, scale, r, w, name):
        self.ts("dve", out, in_, scale, EPS, ALU.mult, ALU.add, r, w)
        self.act(out, out, AF.Sqrt, w, w)
        self.recip(out, out, w, w)


def make_staging(cx, ph, nkmax, cw, tag):
    return [ph.enter_context(cx.sbt(f"{tag}_st{i}", [128, nkmax, cw], F32)) for i in range(2)], cw, tag


def load_scaled_weight(cx, ph, dst, dst_key, w_ap, g_ap, nk, ncols, tag, staging):
    nc = cx.nc
    st, CW, stag = staging
    g = None
    if g_ap is not None:
        g = ph.enter_context(cx.sbt(f"{tag}_g", [128, nk], F32))
        cx.S.op("sp", lambda e: e.dma_start(out=g[:], in_=g_ap.rearrange("(k p) -> p k", p=128), allow_slow_non_contiguous=True),
                writes=[tag + "_g"], dma=True)
    wv = w_ap.rearrange("(k p) n -> p k n", p=128)
    i = 0
    for c0 in range(0, ncols, CW):
        cw = min(CW, ncols - c0)
        b = i % 2
        sk = f"{stag}_st{b}"
        cx.dma(st[b][:, 0:nk, :cw], wv[:, :, c0:c0 + cw], [], [sk])
        for k in range(nk):
            eng = ("dve", "pool")[k % 2]
            if g is not None:
                cx.ts(eng, dst[:, k, c0:c0 + cw], st[b][:, k, :cw], g[:, k:k + 1], None, ALU.mult, None, [sk, tag + "_g"], [dst_key])
            else:
                cx.cp(eng, dst[:, k, c0:c0 + cw], st[b][:, k, :cw], [sk], [dst_key])
        i += 1


def norm_transpose(cx, src, src_key, j, xs, xs_key, junk, ss, rs, pT, xT, xT_key, identb, tagk):
    cx.act(junk[:], src[:, j, :], AF.Square, [src_key], [tagk + "junk", tagk + "ss"], accum_out=ss[:, j:j + 1])
    cx.rstd(rs[:, j:j + 1], ss[:, j:j + 1], 1.0 / D, [tagk + "ss"], [tagk + "rs"], tagk)
    cx.act(xs[:, j, :], src[:, j, :], AF.Copy, [src_key, tagk + "rs"], [xs_key], scale=rs[:, j:j + 1])
    for k in range(8):
        cx.tr(pT[:, k, :], xs[:, j, k * 128:(k + 1) * 128], identb[:], [xs_key, "identb"], [tagk + "pT"])
    cx.cp(cx.ev(), xT[:, :, j * 128:(j + 1) * 128], pT[:], [tagk + "pT"], [xT_key])


def drive(fronts, backs, n, k=2):
    for _ in fronts(0):
        pass
    for i in range(n):
        nf = fronts(i + 1) if i + 1 < n else None
        for _ in backs(i):
            if nf is not None:
                for _ in range(k):
                    try:
                        next(nf)
                    except StopIteration:
                        nf = None
                        break
        if nf is not None:
            for _ in nf:
                pass


def norm_T4(cx, src, src_key, xs, xs_key, junk, ss, rs, pTs, xT, xT_key, identb, tagk):
    for j in range(4):
        cx.act(junk[:], src[:, j, :], AF.Square, [src_key], [tagk + "junk", tagk + "ss"], accum_out=ss[:, j:j + 1])
    yield
    cx.rstd(rs[:], ss[:], 1.0 / D, [tagk + "ss"], [tagk + "rs"], tagk)
    yield
    for j in range(4):
        if j % 2 == 0:
            cx.act(xs[:, j, :], src[:, j, :], AF.Copy, [src_key, tagk + "rs"], [xs_key], scale=rs[:, j:j + 1])
        else:
            cx.ts("dve", xs[:, j, :], src[:, j, :], rs[:, j:j + 1], None, ALU.mult, None, [src_key, tagk + "rs"], [xs_key])
        yield
    for j in range(4):
        pT = pTs[j % 2]
        pk = tagk + f"pT{j % 2}"
        for k in range(8):
            cx.tr(pT[:, k, :], xs[:, j, k * 128:(k + 1) * 128], identb[:], [xs_key, "identb"], [pk])
        cx.cp(cx.ev(), xT[:, :, j * 128:(j + 1) * 128], pT[:], [pk], [xT_key])
        yield


FM_BLOCKS = [("a", 0, 0), ("a", 1, 128), ("a", 2, 256), ("a", 3, 384), ("u", 0, 768), ("u", 1, 896),
             ("d", 0, 1792), ("d", 1, 1920), ("d", 2, 2048), ("d", 3, 2176)]
TM_GROUPS = [(512, 256, 0), (1024, 512, 256), (1536, 256, 768), (2048, 512, 1024), (2560, 272, 1536)]


def phase_A(cx, layer):
    nc, S, T = cx.nc, cx.S, cx.T
    with contextlib.ExitStack() as ph:
        sb = lambda n, s, d: ph.enter_context(cx.sbt(n, s, d))
        ps = lambda n, s, d: ph.enter_context(cx.pst(n, s, d))
        wb = sb("A_wb", [128, 8, NIN], BF16)
        identb = sb("A_idb", [128, 128], BF16)
        cx.dma(identb[:], T["ident_b"][:, :], [], ["identb"])
        stg = make_staging(cx, ph, 8, 256, "Astg")
        load_scaled_weight(cx, ph, wb, "A_wb", T["w_in"][layer], T["norm_mix"][layer], 8, NIN, "Aw", stg)
        AB = sb("A_AB", [128, 2, 512], BF16)
        fw = sb("A_fw", [64, 4, 64], F32)
        csp = sb("A_csp", [64, 2, 2, 128], F32)
        pab = ps("A_pab", [128, 64], F32)
        cx.memset("pool", AB[:], 0.0, ["AB"])
        cx.dma(fw[:], T["fnet_w"][layer].rearrange("g j c -> j g c"), [], ["fw"])
        cx.dma(csp[:], T["cs64"][:, :, :, :], [], ["csp"])
        for g in range(4):
            for ab in range(2):
                cx.mm(pab[:], csp[:, ab, g % 2, :], fw[:, g, :], True, True, ["csp", "fw"], ["pab"])
                P = slice((g % 2) * 64, (g % 2) * 64 + 64)
                cx.cp("dve", AB[P, g // 2, g * 128 + ab * 64: g * 128 + ab * 64 + 64], pab[P, :], ["pab"], ["AB"])
        xin = [sb(f"A_xin{i}", [128, 4, D], F32) for i in range(2)]
        xs = sb("A_xs", [128, 4, D], BF16)
        xT = [sb(f"A_xT{i}", [128, 8, 512], BF16) for i in range(2)]
        junk = sb("A_junk", [128, D], BF16)
        ss = sb("A_ss", [128, 4], F32)
        rs = sb("A_rs", [128, 4], F32)
        ubT = sb("A_ubT", [128, 2, 512], BF16)
        fm = [sb(f"A_fm{i}", [128, 512], BF16) for i in range(3)]
        ztm = [sb(f"A_ztm{i}", [128, 4, 1792], BF16) for i in range(2)]
        zg = [sb(f"A_zg{i}", [128, 4, 16], F32) for i in range(2)]
        zf = [sb(f"A_zf{i}", [128, 4, 512], BF16) for i in range(2)]
        pT = ps("A_pT", [128, 8, 128], BF16)
        pf = [ps(f"A_pf{i}", [128, 512], F32) for i in range(2)]
        pm = [ps(f"A_pm{i}", [128, 512], F32) for i in range(2)]
        src = T["xs"] if layer == 0 else T["xres"]
        it = 0
        nfm = 0
        npm = 0
        for s in range(cx.nseq):
            for i in range(L // 512):
                t0 = i * 512
                b = it % 2
                it += 1
                xk, xtk, ztk, zgk, zfk = f"xin{b}", f"xT{b}", f"ztm{b}", f"zg{b}", f"zf{b}"
                cx.dma(xin[b][:], src[s, t0:t0 + 512, :].rearrange("(j p) d -> p j d", p=128), [], [xk])
                for j in range(4):
                    norm_transpose(cx, xin[b], xk, j, xs, "A_xs", junk, ss, rs, pT, xT[b], xtk, identb, "A")
                for (kind, bi, c0) in FM_BLOCKS:
                    p = pf[nfm % 2]
                    pk = f"pf{nfm % 2}"
                    for k in range(8):
                        cx.mm(p[:], wb[:, k, c0:c0 + 128], xT[b][:, k, :], k == 0, k == 7, ["A_wb", xtk], [pk])
                    if kind == "u":
                        cx.cp(cx.ev(), ubT[:, bi, :], p[:], [pk], ["ubT"])
                    else:
                        fb = nfm % 3
                        fk = f"fm{fb}"
                        cx.cp(cx.ev(), fm[fb][:], p[:], [pk], [fk])
                        dst = T["zT_a"] if kind == "a" else T["zT_d"]
                        cx.dma(dst[s, bi * 128:(bi + 1) * 128, t0:t0 + 512], fm[fb][:], [fk], [])
                    nfm += 1
                for j in range(4):
                    tj = slice(j * 128, (j + 1) * 128)
                    for (c0, cw, off) in TM_GROUPS:
                        p = pm[npm % 2]
                        pk = f"pm{npm % 2}"
                        npm += 1
                        for k in range(8):
                            cx.mm(p[:, :cw], xT[b][:, k, tj], wb[:, k, c0:c0 + cw], k == 0, k == 7, [xtk, "A_wb"], [pk])
                        if cw == 272:
                            cx.cp(cx.ev(), ztm[b][:, j, off:off + 256], p[:, :256], [pk], [ztk])
                            cx.cp(cx.ev(), zg[b][:, j, :], p[:, 256:272], [pk], [zgk])
                        else:
                            cx.cp(cx.ev(), ztm[b][:, j, off:off + cw], p[:, :cw], [pk], [ztk])
                    p = pm[npm % 2]
                    pk = f"pm{npm % 2}"
                    npm += 1
                    for ct in range(2):
                        cx.mm(p[:], ubT[:, ct, tj], AB[:, ct, :], ct == 0, ct == 1, ["ubT", "AB"], [pk])
                    cx.cp(cx.ev(), zf[b][:, j, :], p[:], [pk], [zfk])
                tv = lambda ap: ap[s, t0:t0 + 512, :].rearrange("(j p) c -> p j c", p=128)
                cx.dma(tv(T["z_va"]), ztm[b][:, :, 0:256], [ztk], [])
                cx.dma(T["z_hy"][s, 1 + t0:1 + t0 + 512, :].rearrange("(j p) c -> p j c", p=128), ztm[b][:, :, 256:1024], [ztk], [])
                cx.dma(tv(T["z_d"]), ztm[b][:, :, 1024:1792], [ztk], [])
                cx.dma(T["z_g"][s, :, 4 * i:4 * i + 4, :], zg[b][:], [zgk], [])
                cx.dma(tv(T["z_fn"]), zf[b][:], [zfk], [])
    S.barrier()


def phase_NA(cx, layer):
    nc, S, T = cx.nc, cx.S, cx.T
    with contextlib.ExitStack() as ph:
        sb = lambda n, s, d: ph.enter_context(cx.sbt(n, s, d))
        ps = lambda n, s, d: ph.enter_context(cx.pst(n, s, d))
        QT = sb("N_QT", [128, 2, L], BF16)
        KT = sb("N_KT", [128, 2, L], BF16)
        Ve = sb("N_Ve", [128, 64, 256], BF16)
        Vo = sb("N_Vo", [128, 63, 256], BF16)
        bias = sb("N_bias", [128, 16, 512], F32)
        identb = sb("N_idb", [128, 128], BF16)
        NB = 2
        qbd = [sb(f"N_qbd{i}", [128, 128], BF16) for i in range(NB)]
        ssb = [sb(f"N_s{i}", [128, 512], F32) for i in range(NB)]
        pb = [sb(f"N_p{i}", [128, 512], BF16) for i in range(NB)]
        pTs = [sb(f"N_pT{i}", [128, 4, 128], BF16) for i in range(NB)]
        mx = [sb(f"N_mx{i}", [128, 1], F32) for i in range(NB)]
        rsum = [sb(f"N_rs{i}", [128, 1], F32) for i in range(NB)]
        ybuf = [sb(f"N_yb{i}", [128, 8, 64], BF16) for i in range(4)]
        Sp = [ps(f"N_Sp{i}", [128, 512], F32) for i in range(2)]
        pTp = [ps(f"N_pTp{i}", [128, 4, 128], BF16) for i in range(2)]
        po = [ps(f"N_po{i}", [128, 128], F32) for i in range(2)]
        cx.dma(identb[:], T["ident_b"][:, :], [], ["identb"])
        cx.dma(bias[:], T["na_bias"][layer].rearrange("v p n -> p v n"), [], ["bias"])
        for i in range(NB):
            cx.memset("pool", qbd[i][:], 0.0, [f"qbd{i}"])
        u = 0
        for s in range(cx.nseq):
            for hp in range(2):
                cx.dma(QT[:, hp, :], T["zT_a"][s, hp * 128:(hp + 1) * 128, :], [], ["QT"])
                cx.dma(KT[:, hp, :], T["zT_a"][s, 256 + hp * 128:256 + (hp + 1) * 128, :], [], ["KT"])
            cx.dma(Ve[:], T["z_va"][s].rearrange("(n p) c -> p n c", p=128), [], ["Ve"])
            cx.dma(Vo[:], T["z_va"][s, 64:64 + 63 * 128, :].rearrange("(n p) c -> p n c", p=128), [], ["Vo"])
            for r in range(128):
                kr0 = min(max(r - 4, 0), 120)
                v = r if r < 4 else (4 if r <= 124 else r - 120)
                for hp in range(2):
                    b = u % NB
                    b2 = u % 2
                    u += 1
                    yb = ybuf[hp * 2 + (r // 8) % 2]
                    ybk = f"yb{hp * 2 + (r // 8) % 2}"
                    tq = slice(r * 64, (r + 1) * 64)
                    cx.cp("pool", qbd[b][0:64, 0:64], QT[0:64, hp, tq], ["QT"], [f"qbd{b}"])
                    cx.cp("pool", qbd[b][64:128, 64:128], QT[64:128, hp, tq], ["QT"], [f"qbd{b}"])
                    cx.mm(Sp[b2][:], qbd[b][:], KT[:, hp, kr0 * 64:kr0 * 64 + 512], True, True, [f"qbd{b}", "KT"], [f"Sp{b2}"])
                    cx.stt("dve", ssb[b][:], Sp[b2][:], 0.125, bias[:, v * 2 + hp, :], ALU.mult, ALU.add, [f"Sp{b2}", "bias"], [f"s{b}"])
                    cx.S.op("dve", lambda e, b=b: e.tensor_reduce(out=mx[b][:], in_=ssb[b][:], axis=AX.X, op=ALU.max), reads=[f"s{b}"], writes=[f"mx{b}"])
                    cx.ts("pool", mx[b][:], mx[b][:], -1.0, None, ALU.mult, None, [f"mx{b}"], [f"mx{b}"])
                    cx.act(pb[b][:], ssb[b][:], AF.Exp, [f"s{b}", f"mx{b}"], [f"p{b}", f"rs{b}"], bias=mx[b][:], accum_out=rsum[b][:])
                    for c in range(4):
                        cx.tr(pTp[b2][:, c, :], pb[b][:, c * 128:(c + 1) * 128], identb[:], [f"p{b}", "identb"], [f"pTp{b2}"])
                    cx.cp(cx.ev(), pTs[b][:], pTp[b2][:], [f"pTp{b2}"], [f"pT{b}"])
                    for c in range(4):
                        if kr0 % 2 == 0:
                            vv = Ve[:, kr0 // 2 + c, hp * 128:(hp + 1) * 128]
                            vk = "Ve"
                        else:
                            vv = Vo[:, (kr0 - 1) // 2 + c, hp * 128:(hp + 1) * 128]
                            vk = "Vo"
                        cx.mm(po[b2][:], pTs[b][:, c, :], vv, c == 0, c == 3, [f"pT{b}", vk], [f"po{b2}"])
                    cx.recip(rsum[b][:], rsum[b][:], [f"rs{b}"], [f"rs{b}"])
                    cx.act(yb[0:64, r % 8, :], po[b2][0:64, 0:64], AF.Copy, [f"po{b2}", f"rs{b}"], [ybk], scale=rsum[b][0:64, :])
                    cx.act(yb[64:128, r % 8, :], po[b2][64:128, 64:128], AF.Copy, [f"po{b2}", f"rs{b}"], [ybk], scale=rsum[b][64:128, :])
                    if r % 8 == 7:
                        r0 = r - 7
                        for h in range(2):
                            col = (2 * hp + h) * 64
                            cx.dma(T["y"][s, r0 * 64:(r0 + 8) * 64, col:col + 64].rearrange("(r q) d -> q r d", q=64),
                                   yb[h * 64:(h + 1) * 64, :, :], [ybk], [])
    S.barrier()


def phase_FN(cx, layer):
    nc, S, T = cx.nc, cx.S, cx.T
    with contextlib.ExitStack() as ph:
        sb = lambda n, s, d: ph.enter_context(cx.sbt(n, s, d))
        ps = lambda n, s, d: ph.enter_context(cx.pst(n, s, d))
        F1 = sb("F_F1", [64, 2, 128], BF16)
        T2c = sb("F_T2c", [128, 64, 128], BF16)
        T2s = sb("F_T2s", [128, 64, 128], BF16)
        uin = sb("F_uin", [64, 128, 256], BF16)
        Ysb = sb("F_Y", [128, 2, 64, 128], BF16)
        yfn = sb("F_yfn", [128, 64, 128], BF16)
        Yp = [ps(f"F_Yp{i}", [128, 4, 128], F32) for i in range(2)]
        Xp = [ps(f"F_Xp{i}", [128, 4, 128], F32) for i in range(2)]
        cx.dma(F1[:], T["f_F1"][:, :, :], [], ["F1"])
        cx.dma(T2c[:], T["f_T2c"][:, :, :], [], ["T2c"])
        cx.dma(T2s[:], T["f_T2s"][:, :, :], [], ["T2s"])
        n1 = 0
        n2 = 0
        for s in range(cx.nseq):
            for hf in range(2):
                cx.dma(uin[:], T["z_fn"][s, :, hf * 256:(hf + 1) * 256].rearrange("(a b) c -> a b c", b=128), [], ["uin"])
                for c0 in range(0, 128, 4):
                    p = Yp[n1 % 2]
                    pk = f"Yp{n1 % 2}"
                    n1 += 1
                    for cc in range(4):
                        ch = c0 + cc
                        g2, c_ = ch // 64, ch % 64
                        cx.mm(p[:, cc, :], uin[:, :, g2 * 128 + c_], F1[:, 0, :], True, False, ["uin", "F1"], [pk])
                        cx.mm(p[:, cc, :], uin[:, :, g2 * 128 + 64 + c_], F1[:, 1, :], False, True, ["uin", "F1"], [pk])
                    cx.cp(cx.ev(), Ysb[:, :, :, c0:c0 + 4], p[:].rearrange("p c (r k) -> p r k c", r=2), [pk], ["Ysb"])
                for k0 in range(0, 64, 4):
                    p = Xp[n2 % 2]
                    pk = f"Xp{n2 % 2}"
                    n2 += 1
                    for q in range(4):
                        k1 = k0 + q
                        cx.mm(p[:, q, :], T2c[:, k1, :], Ysb[:, 0, k1, :], True, False, ["T2c", "Ysb"], [pk])
                        cx.mm(p[:, q, :], T2s[:, k1, :], Ysb[:, 1, k1, :], False, True, ["T2s", "Ysb"], [pk])
                    cx.cp(cx.ev(), yfn[:, k0:k0 + 4, :], p[:], [pk], ["yfn"])
                cx.dma(T["y"][s, :, 256 + hf * 128:256 + (hf + 1) * 128].rearrange("(k2 k1) c -> k2 k1 c", k1=64), yfn[:], ["yfn"], [])
    S.barrier()


CB = 32


class HyTabs:
    pass


def hy_tables(cx, ph):
    nc, T = cx.nc, cx.T
    sb = lambda n, s, d: ph.enter_context(cx.sbt(n, s, d))
    ps = lambda n, s, d: ph.enter_context(cx.pst(n, s, d))
    H = HyTabs()
    H.F1 = sb("H_F1", [128, 195], BF16)
    H.T2c = sb("H_T2c", [128, 65, 128], BF16)
    H.T2s = sb("H_T2s", [128, 65, 128], BF16)
    H.GA = sb("H_GA", [128, 2, 256], BF16)
    H.TBc = sb("H_TBc", [65, 128, 64], BF16)
    H.TBs = sb("H_TBs", [65, 128, 64], BF16)
    for t, n in ((H.F1, "h_F1"), (H.T2c, "h_T2c"), (H.T2s, "h_T2s"), (H.GA, "h_GA"), (H.TBc, "h_TBc"), (H.TBs, "h_TBs")):
        cx.dma(t[:], T[n], [], ["H_tab"])
    H.Ysb = sb("H_Y", [128, 3, 65, CB], BF16)
    H.Xs = sb("H_Xs", [128, 65, 2, CB], BF16)
    H.Yp = [ps(f"H_Yp{i}", [128, 2, 195], F32) for i in range(2)]
    H.Xp = [ps(f"H_Xp{i}", [128, 8, 2, CB], F32) for i in range(2)]
    H.n1 = 0
    H.n2 = 0
    return H


def hy_fwd(cx, H, src, src_key, K1):
    for c0 in range(0, CB, 2):
        p = H.Yp[H.n1 % 2]
        pk = f"HYp{H.n1 % 2}"
        H.n1 += 1
        for cc in range(2):
            cx.mm(p[:, cc, :], src[0:K1, :, c0 + cc], H.F1[0:K1, :], True, True, [src_key, "H_tab"], [pk])
        cx.cp(cx.ev(), H.Ysb[:, :, :, c0:c0 + 2], p[:].rearrange("p c (r k) -> p r k c", r=3), [pk], ["HYsb"])
    for k0 in range(0, 65, 8):
        nk = min(8, 65 - k0)
        p = H.Xp[H.n2 % 2]
        pk = f"HXp{H.n2 % 2}"
        H.n2 += 1
        for q in range(nk):
            k1 = k0 + q
            cx.mm(p[:, q, 0, :], H.T2c[:, k1, :], H.Ysb[:, 0, k1, :], True, False, ["H_tab", "HYsb"], [pk])
            cx.mm(p[:, q, 0, :], H.T2s[:, k1, :], H.Ysb[:, 1, k1, :], False, True, ["H_tab", "HYsb"], [pk])
            cx.mm(p[:, q, 1, :], H.T2c[:, k1, :], H.Ysb[:, 1, k1, :], True, False, ["H_tab", "HYsb"], [pk])
            cx.mm(p[:, q, 1, :], H.T2s[:, k1, :], H.Ysb[:, 2, k1, :], False, True, ["H_tab", "HYsb"], [pk])
        cx.cp(cx.ev(), H.Xs[:, k0:k0 + nk, :, :], p[:, 0:nk, :, :], [pk], ["HXs"])


def wrap_sin(cx, a, tmp, ak, tk, P):
    pi = math.pi
    cx.ts("dve", tmp[0:P, :], a[0:P, :], pi, -2 * pi, ALU.is_gt, ALU.mult, [ak], [tk])
    cx.tt("dve", a[0:P, :], a[0:P, :], tmp[0:P, :], ALU.add, [ak, tk], [ak])
    cx.ts("dve", tmp[0:P, :], a[0:P, :], -pi, 2 * pi, ALU.is_lt, ALU.mult, [ak], [tk])
    cx.tt("dve", a[0:P, :], a[0:P, :], tmp[0:P, :], ALU.add, [ak, tk], [ak])
    cx.act(a[0:P, :], a[0:P, :], AF.Sin, [ak], [ak])


def phase_HYF(cx, layer):
    nc, S, T = cx.nc, cx.S, cx.T
    with contextlib.ExitStack() as ph:
        sb = lambda n, s, d: ph.enter_context(cx.sbt(n, s, d))
        ps = lambda n, s, d: ph.enter_context(cx.pst(n, s, d))
        w1 = sb("G_w1", [33, 64], F32)
        w2 = sb("G_w2", [64, 64], F32)
        w3 = sb("G_w3", [64, 1024], F32)
        sc = sb("G_sc", [64, 6], F32)
        dec = sb("G_dec", [1, 1024], F32)
        cx.dma(w1[:], T["hy_w1"][layer], [], ["w1"])
        cx.dma(w2[:], T["hy_w2"][layer], [], ["w2"])
        cx.dma(w3[:], T["hy_w3"][layer], [], ["w3"])
        cx.dma(dec[:], T["hy_decay"][layer].rearrange("o d c -> (o d c)").unsqueeze(0), [], ["dec"])
        cx.S.op("sp", lambda e: e.dma_start(out=sc[:, 0:1], in_=T["hy_b1"][layer].unsqueeze(1), allow_slow_non_contiguous=True), writes=["sc"], dma=True)
        cx.S.op("sp", lambda e: e.dma_start(out=sc[:, 1:2], in_=T["hy_b2"][layer].unsqueeze(1), allow_slow_non_contiguous=True), writes=["sc"], dma=True)
        cx.S.op("sp", lambda e: e.dma_start(out=sc[:, 2:4], in_=T["hy_freq"][layer].rearrange("t c -> c t"), allow_slow_non_contiguous=True), writes=["sc"], dma=True)
        cx.tt("dve", sc[:, 4:6], sc[:, 0:2], sc[:, 2:4], ALU.mult, ["sc"], ["sc"])
        fT = [sb(f"G_fT{i}", [33, 512], F32) for i in range(2)]
        tv = [sb(f"G_tv{i}", [1, 512], F32) for i in range(2)]
        a1 = sb("G_a1", [64, 512], F32)
        a2 = sb("G_a2", [64, 512], F32)
        tmp = sb("G_tmp", [64, 512], F32)
        win = [sb(f"G_win{i}", [128, 512], F32) for i in range(2)]
        ff = [sb(f"G_ff{i}", [128, 512], BF16) for i in range(2)]
        p1 = ps("G_p1", [64, 512], F32)
        p2 = ps("G_p2", [64, 512], F32)
        p3 = [ps(f"G_p3{i}", [128, 512], F32) for i in range(2)]
        pd = [ps(f"G_pd{i}", [128, 512], F32) for i in range(2)]
        w3v = w3[:].rearrange("p (o d c) -> p o d c", o=2, d=2)
        decv = dec[:].rearrange("p (o d c) -> p o d c", o=2, d=2)
        n = 0
        for i in range(2 * L // 512):
            R0 = i * 512
            d = 0 if R0 < L else 1
            b = i % 2
            cx.dma(fT[b][:], T["h_feats"][:, R0:R0 + 512], [], [f"fT{b}"])
            cx.dma(tv[b][:], T["h_tvec"][:, R0:R0 + 512], [], [f"tv{b}"])
            cx.mm(p1[:], w1[:], fT[b][:], True, True, ["w1", f"fT{b}"], ["p1"])
            cx.ts("dve", a1[:], p1[:], sc[:, 2:3], sc[:, 4:5], ALU.mult, ALU.add, ["p1", "sc"], ["a1"])
            wrap_sin(cx, a1, tmp, "a1", "tmp", 64)
            cx.mm(p2[:], w2[:], a1[:], True, True, ["w2", "a1"], ["p2"])
            cx.ts("dve", a2[:], p2[:], sc[:, 3:4], sc[:, 5:6], ALU.mult, ALU.add, ["p2", "sc"], ["a2"])
            wrap_sin(cx, a2, tmp, "a2", "tmp", 64)
            for j in range(4):
                q = n % 2
                n += 1
                cx.mm(p3[q][:].rearrange("p (o c) -> p o c", o=2), a2[:, j * 128:(j + 1) * 128], w3v[:, :, d, :], True, True, ["a2", "w3"], [f"p3{q}"])
                cx.mm(pd[q][:].rearrange("p (o c) -> p o c", o=2), tv[b][0:1, j * 128:(j + 1) * 128], decv[:, :, d, :], True, True, [f"tv{b}", "dec"], [f"pd{q}"])
                cx.act(win[q][:], pd[q][:], AF.Exp, [f"pd{q}"], [f"win{q}"], scale=-1.0)
                cx.tt("dve", ff[q][:], p3[q][:], win[q][:], ALU.mult, [f"p3{q}", f"win{q}"], [f"ff{q}"])
                if R0 + j * 128 == L:
                    cx.memset("dve", ff[q][0:1, :], 0.0, [f"ff{q}"])
                cx.dma(T["h_ftm"][R0 + j * 128:R0 + (j + 1) * 128, :], ff[q][:], [f"ff{q}"], ["h_ftm"])
    S.barrier()
    with contextlib.ExitStack() as ph:
        sb = lambda n, s, d: ph.enter_context(cx.sbt(n, s, d))
        ra = [sb(f"G_ra{i}", [128, 16, 512], BF16) for i in range(2)]
        rb = [sb(f"G_rb{i}", [128, 512 // CB, 16, CB], BF16) for i in range(2)]
        for q in range(8):
            b = q % 2
            cx.dma(ra[b][:], T["h_ftm"].rearrange("(a t) c -> a t c", t=128)[:, 16 * q:16 * q + 16, :], ["h_ftm"], [f"ra{b}"])
            cx.cp(("pool", "dve")[b], rb[b][:].rearrange("p b t c -> p t b c"), ra[b][:].rearrange("p t (b c) -> p t b c", c=CB), [f"ra{b}"], [f"rb{b}"])
            cx.dma(T["h_full"].rearrange("b (a t) c -> a b t c", t=128)[:, :, 16 * q:16 * q + 16, :], rb[b][:], [f"rb{b}"], ["h_full"])
    S.barrier()
    with contextlib.ExitStack() as ph:
        sb = lambda n, s, d: ph.enter_context(cx.sbt(n, s, d))
        H = hy_tables(cx, ph)
        fin = [sb(f"G_fin{i}", [128, 128, CB], BF16) for i in range(2)]
        skp = sb("G_skp", [128, 512], F32)
        KFt = [sb(f"G_KFt{i}", [128, 65, 2, CB], BF16) for i in range(2)]
        cx.dma(skp[:], T["hy_skip"][layer].rearrange("o c -> (o c)").partition_broadcast(128), [], ["skp"])
        for fb in range(16):
            b = fb % 2
            cx.dma(fin[b][:], T["h_full"][fb].rearrange("(a t) c -> a t c", t=128), ["h_full"], [f"fin{b}"])
            hy_fwd(cx, H, fin[b], f"fin{b}", 128)
            cx.tt("dve", KFt[b][:, :, 0, :], H.Xs[:, :, 0, :], skp[:, fb * CB:(fb + 1) * CB].unsqueeze(1).to_broadcast([128, 65, CB]),
                  ALU.add, ["HXs", "skp"], [f"KFt{b}"])
            cx.cp("pool", KFt[b][:, :, 1, :], H.Xs[:, :, 1, :], ["HXs"], [f"KFt{b}"])
            cx.dma(T["h_KF"][fb].rearrange("p (k r c) -> p k r c", k=65, r=2), KFt[b][:], [f"KFt{b}"], ["h_KF"])
    S.barrier()


def phase_HC(cx, layer):
    nc, S, T = cx.nc, cx.S, cx.T
    NBLK = 768 // CB
    with contextlib.ExitStack() as ph:
        sb = lambda n, s, d: ph.enter_context(cx.sbt(n, s, d))
        cw = sb("Y_cw", [128, 3, 768], F32)
        cb_ = sb("Y_cb", [128, 768], F32)
        cx.dma(cw[:], T["hy_conv_w"][layer].rearrange("t c -> (t c)").partition_broadcast(128), [], ["cw"])
        cx.dma(cb_[:], T["hy_conv_b"][layer].partition_broadcast(128), [], ["cb"])
        uin = [sb(f"Y_uin{i}", [128, 10, 768], BF16) for i in range(2)]
        t1 = sb("Y_t1", [128, 8, 768], BF16)
        t2 = sb("Y_t2", [128, 8, 768], BF16)
        sg = [sb(f"Y_sgp{i}", [128, NBLK, 8, CB], BF16) for i in range(2)]
        it = 0
        bc = lambda ap: ap.unsqueeze(1).to_broadcast([128, 8, 768])
        for s in range(cx.nseq):
            base = T["z_hy"][s]
            for q in range(8):
                b = it % 2
                it += 1
                uk, sk = f"uin{b}", f"sgp{b}"
                win_ap = bass.AP(base.tensor, base.offset + 8 * q * 768, [[64 * 768, 128], [768, 10], [1, 768]])
                cx.dma(uin[b][:], win_ap, [], [uk])
                cx.tt("pool", t1[:], uin[b][:, 0:8, :], bc(cw[:, 0, :]), ALU.mult, [uk, "cw"], ["t1"])
                cx.tt("dve", t2[:], uin[b][:, 1:9, :], bc(cw[:, 1, :]), ALU.mult, [uk, "cw"], ["t2"])
                cx.tt("pool", t1[:], t1[:], t2[:], ALU.add, ["t1", "t2"], ["t1"])
                cx.tt("dve", t2[:], uin[b][:, 2:10, :], bc(cw[:, 2, :]), ALU.mult, [uk, "cw"], ["t2"])
                cx.tt("pool", t1[:], t1[:], t2[:], ALU.add, ["t1", "t2"], ["t1"])
                cx.tt("dve", sg[b][:].rearrange("p b t c -> p t b c"), t1[:].rearrange("p t (b c) -> p t b c", c=CB),
                      bc(cb_[:, :]).rearrange("p t (b c) -> p t b c", c=CB), ALU.add, ["t1", "cb"], [sk])
                cx.dma(T["z_hc"][s].rearrange("b (a t) c -> a b t c", t=64)[:, :, 8 * q:8 * q + 8, :], sg[b][:], [sk], [])
    S.barrier()


def phase_HY(cx, layer):
    nc, S, T = cx.nc, cx.S, cx.T
    with contextlib.ExitStack() as ph:
        sb = lambda n, s, d: ph.enter_context(cx.sbt(n, s, d))
        ps = lambda n, s, d: ph.enter_context(cx.pst(n, s, d))
        H = hy_tables(cx, ph)
        sig1 = [sb(f"Y_sig{i}", [64, 128, CB], BF16) for i in range(3)]
        sig = [sig1, sig1]
        KF = [sb(f"Y_KF{i}", [128, 65, 2, CB], BF16) for i in range(2)]
        Z = sb("Y_Z", [128, 65, 2, CB], BF16)
        za = sb("Y_za", [128, 65, CB], BF16)
        zb = sb("Y_zb", [128, 65, CB], BF16)
        Vsb = sb("Y_V", [65, 2, 128, CB], BF16)
        y1 = sb("Y_y1", [64, 128, CB], BF16)
        yo = [sb(f"Y_yo{i}", [64, 128, CB], BF16) for i in range(2)]
        rl = sb("Y_rl", [64, 256 // CB, 16, CB], BF16)
        rl2 = sb("Y_rl2", [64, 16, 256 // CB, CB], BF16)
        Vp = [ps(f"Y_Vp{i}", [65, 2, 256], F32) for i in range(2)]
        yp = [ps(f"Y_yp{i}", [64, 16, CB], F32) for i in range(2)]
        nv = 0
        ny = 0
        nk = 0
        it = 0
        for s in range(cx.nseq):
            for cb in range(256 // CB):
                jb = it % 2
                it += 1
                for g in range(3):
                    blk = g * (256 // CB) + cb
                    cx.dma(sig[jb][g][:], T["z_hc"][s, blk].rearrange("(a t) c -> a t c", t=128), [], [f"sig{g}"])
                for o in range(2):
                    src, sk = (sig[jb][0], "sig0") if o == 0 else (y1, "y1")
                    gate, gk = (sig[jb][1], "sig1") if o == 0 else (sig[jb][2], "sig2")
                    dst, dk = (y1, "y1") if o == 0 else (yo[jb], f"yo{jb}")
                    kb = nk % 2
                    nk += 1
                    kfk = f"KF{kb}"
                    cx.dma(KF[kb][:], T["h_KF"][o * (256 // CB) + cb].rearrange("p (k r c) -> p k r c", k=65, r=2), ["h_KF"], [kfk])
                    hy_fwd(cx, H, src, sk, 64)
                    Xr, Xi, Kr, Ki = H.Xs[:, :, 0, :], H.Xs[:, :, 1, :], KF[kb][:, :, 0, :], KF[kb][:, :, 1, :]
                    cx.tt("pool", za[:], Xr, Kr, ALU.mult, ["HXs", kfk], ["za"])
                    cx.tt("dve", zb[:], Xi, Ki, ALU.mult, ["HXs", kfk], ["zb"])
                    cx.tt("dve", Z[:, :, 0, :], za[:], zb[:], ALU.subtract, ["za", "zb"], ["Z"])
                    cx.tt("pool", za[:], Xr, Ki, ALU.mult, ["HXs", kfk], ["za"])
                    cx.tt("dve", zb[:], Xi, Kr, ALU.mult, ["HXs", kfk], ["zb"])
                    cx.tt("pool", Z[:, :, 1, :], za[:], zb[:], ALU.add, ["za", "zb"], ["Z"])
                    for c0 in range(0, CB, 2):
                        p = Vp[nv % 2]
                        pk = f"Vp{nv % 2}"
                        nv += 1
                        for cc in range(2):
                            cx.mm(p[:, cc, :], Z[:, :, 0, c0 + cc], H.GA[:, 0, :], True, False, ["Z", "H_tab"], [pk])
                            cx.mm(p[:, cc, :], Z[:, :, 1, c0 + cc], H.GA[:, 1, :], False, True, ["Z", "H_tab"], [pk])
                        cx.cp(cx.ev(), Vsb[:, :, :, c0:c0 + 2], p[:].rearrange("p c (r t) -> p r t c", r=2), [pk], ["Vsb"])
                    for q0 in range(0, 128, 16):
                        p = yp[ny % 2]
                        pk = f"yp{ny % 2}"
                        ny += 1
                        for q in range(16):
                            tq = q0 + q
                            cx.mm(p[:, q, :], H.TBc[:, tq, :], Vsb[:, 0, tq, :], True, False, ["H_tab", "Vsb"], [pk])
                            cx.mm(p[:, q, :], H.TBs[:, tq, :], Vsb[:, 1, tq, :], False, True, ["H_tab", "Vsb"], [pk])
                        cx.tt("dve", dst[:, q0:q0 + 16, :], p[:], gate[:, q0:q0 + 16, :], ALU.mult, [pk, gk], [dk])
                cx.dma(T["y_hc"][s, cb].rearrange("(a t) c -> a t c", t=128), yo[jb][:], [f"yo{jb}"], ["y_hc"])
            for q in range(8):
                cx.dma(rl[:], T["y_hc"][s].rearrange("b (a t) c -> a b t c", t=128)[:, :, 16 * q:16 * q + 16, :], ["y_hc"], ["rl"])
                cx.cp("pool", rl2[:], rl[:].rearrange("p b t c -> p t b c"), ["rl"], ["rl2"])
                cx.dma(T["y"][s, :, 512:768].rearrange("(a t) c -> a t c", t=128)[:, 16 * q:16 * q + 16, :],
                       rl2[:].rearrange("p t b c -> p t (b c)"), ["rl2"], [])
    S.barrier()


def phase_ML(cx, layer):
    nc, S, T = cx.nc, cx.S, cx.T
    with contextlib.ExitStack() as ph:
        sb = lambda n, s, d: ph.enter_context(cx.sbt(n, s, d))
        ps = lambda n, s, d: ph.enter_context(cx.pst(n, s, d))
        qT = sb("M_qT", [128, 2, L], BF16)
        kT = sb("M_kT", [128, 2, L], BF16)
        hO = [sb(f"M_h{d}", [128, 64, 256], BF16) for d in range(2)]
        gts = sb("M_gts", [128, 64, 16], F32)
        gb = sb("M_gb", [128, 16], F32)
        tri = sb("M_tri", [128, 2, 128], F32)
        ones = sb("M_ones", [128, 128], F32)
        nbc = sb("M_nbc", [128, 256], F32)
        lfn = [sb(f"M_lfn{d}", [128, 64, 4], F32) for d in range(2)]
        eb = [sb(f"M_eb{d}", [128, 64, 4], F32) for d in range(2)]
        rr = [sb(f"M_rr{d}", [128, 64, 4], F32) for d in range(2)]
        eg = [sb(f"M_eg{d}", [128, 64, 4], F32) for d in range(2)]
        S32 = [sb(f"M_S32{d}", [128, 2, 80], F32) for d in range(2)]
        S16 = [sb(f"M_S16{d}", [128, 2, 80], BF16) for d in range(2)]
        NB = 3
        kv = [[sb(f"M_kv{d}{i}", [128, 512], BF16) for i in range(NB)] for d in range(2)]
        vt = [[sb(f"M_vt{d}{i}", [128, 4, 80], BF16) for i in range(NB)] for d in range(2)]
        PT = [[sb(f"M_PT{d}{i}", [128, 4, 128], BF16) for i in range(NB)] for d in range(2)]
        nd = [sb(f"M_nd{d}", [128, 4, 80], F32) for d in range(2)]
        den = [sb(f"M_den{d}", [128, 4, 1], F32) for d in range(2)]
        ob = sb("M_ob", [128, 8, 256], BF16)
        sg = sb("M_sg", [128, 8, 256], F32)
        hs = sb("M_hs", [128, 8, 256], F32)
        yd = sb("M_yd", [128, 8, 256], BF16)
        pc = ps("M_pc", [128, 256], F32)
        Gp = [ps(f"M_Gp{d}", [128, 4, 128], F32) for d in range(2)]
        nump = [ps(f"M_num{d}", [128, 4, 80], F32) for d in range(2)]
        dSp = [ps(f"M_dS{d}", [128, 4, 80], F32) for d in range(2)]
        cx.dma(tri[:], T["m_tri"].rearrange("t s c -> s t c"), [], ["tri"])
        for d in range(2):
            for i in range(NB):
                cx.memset("pool", vt[d][i][:], 0.0, [f"vt{d}{i}"])
        cx.memset("pool", ones[:], 1.0, ["ones"])
        cx.dma(gb[:], T["ml_gate_b"][layer].rearrange("a b -> (a b)").partition_broadcast(128), [], ["gb"])
        for s in range(cx.nseq):
            for pr in range(2):
                cx.dma(qT[:, pr, :], T["zT_d"][s, pr * 128:(pr + 1) * 128, :], [], ["qT"])
                cx.dma(kT[:, pr, :], T["zT_d"][s, 256 + pr * 128:256 + (pr + 1) * 128, :], [], ["kT"])
            cx.dma(gts[:], T["z_g"][s], [], ["gts"])
            cx.tt("dve", gts[:], gts[:], gb[:].unsqueeze(1).to_broadcast([128, 64, 16]), ALU.add, ["gts", "gb"], ["gts"])
            import os
            STG = int(os.environ.get("ML_STAGE", "9"))
            for d in range(2 if STG not in (-2, -3) else 0):
                if STG == -5:
                    cx.act(lfn[d][:], gts[:, :, 4 + 8 * d:8 + 8 * d], AF.Exp, ["gts"], [f"lfn{d}"], scale=-1.0)
                    continue
                fcol = slice(4 + 8 * d, 8 + 8 * d)
                icol = slice(8 * d, 4 + 8 * d)
                cx.act(lfn[d][:], gts[:, :, fcol], AF.Exp, ["gts"], [f"lfn{d}"], scale=-1.0)
                cx.act(lfn[d][:], lfn[d][:], AF.Ln, [f"lfn{d}", "ones"], [f"lfn{d}"], bias=ones[:, 0:1])
                if STG == -4:
                    continue
                lv = lfn[d][:].rearrange("p n h -> p (n h)")
                cx.mm(pc[:], tri[:, d, :], lv, True, True, ["tri", f"lfn{d}"], ["pc"])
                cx.act(eb[d][:].rearrange("p n h -> p (n h)"), pc[:], AF.Exp, ["pc"], [f"eb{d}"], scale=-1.0)
                if STG == -6:
                    continue
                cx.act(nbc[:], pc[:], AF.Copy, ["pc"], ["nbc"])
                cx.tt("dve", rr[d][:], nbc[:].rearrange("p (n h) -> p n h", h=4), gts[:, :, icol], ALU.add, ["nbc", "gts"], [f"rr{d}"])
                cx.act(rr[d][:], rr[d][:], AF.Exp, [f"rr{d}"], [f"rr{d}"])
                cx.ts("dve", rr[d][:], rr[d][:], 0.125, None, ALU.mult, None, [f"rr{d}"], [f"rr{d}"])
                if STG == -7:
                    continue
                cx.mm(pc[:], ones[:], lv, True, True, ["ones", f"lfn{d}"], ["pc"])
                cx.act(eg[d][:].rearrange("p n h -> p (n h)"), pc[:], AF.Exp, ["pc"], [f"eg{d}"], scale=-1.0)
                cx.memset("pool", S32[d][:], 0.0, [f"S32{d}"])
                cx.memset("pool", S16[d][:], 0.0, [f"S16{d}"])
            import os
            STG = int(os.environ.get("ML_STAGE", "9"))
            for i in range(64 if STG >= 1 else 0):
                for d in range(2):
                    n = i if d == 0 else 63 - i
                    b = i % NB
                    tk = slice(n * 128, (n + 1) * 128)
                    kvk, vtk, ptk = f"kv{d}{b}", f"vt{d}{b}", f"PT{d}{b}"
                    cx.dma(kv[d][b][:], T["z_d"][s, tk, 0:512], [], [kvk])
                    for h in (0, 2, 1, 3):
                        P = slice((h % 2) * 64, (h % 2) * 64 + 64)
                        cx.mm(Gp[d][:, h, :], kT[P, h // 2, tk], qT[P, h // 2, tk], True, True, ["kT", "qT"], [f"Gp{d}"])
                    if STG < 2:
                        continue
                    cx.tt("dve", PT[d][b][:], Gp[d][:], tri[:, d, :].unsqueeze(1).to_broadcast([128, 4, 128]), ALU.mult, [f"Gp{d}", "tri"], [ptk])
                    rv = rr[d][:, n, :].unsqueeze(2)
                    cx.tt("dve", vt[d][b][:, :, 0:64], kv[d][b][:, 256:512].rearrange("p (h e) -> p h e", h=4), rv.to_broadcast([128, 4, 64]), ALU.mult, [kvk, f"rr{d}"], [vtk])
                    cx.cp("act", vt[d][b][:, :, 64:65], rv, [f"rr{d}"], [vtk])
                    for h in (0, 2, 1, 3):
                        P = slice((h % 2) * 64, (h % 2) * 64 + 64)
                        cx.mm(nump[d][:, h, :], PT[d][b][:, h, :], vt[d][b][:, h, :], True, False, [ptk, vtk], [f"num{d}"])
                        cx.mm(nump[d][:, h, :], qT[P, h // 2, tk], S16[d][P, h // 2, :], False, True, ["qT", f"S16{d}"], [f"num{d}"])
                    if STG < 3:
                        continue
                    for h in (0, 2, 1, 3):
                        pr = h // 2
                        cx.mm(dSp[d][:, h, :], kv[d][b][:, pr * 128:(pr + 1) * 128], vt[d][b][:, h, :], True, True, [kvk, vtk], [f"dS{d}"])
                    for h2 in range(2):
                        P = slice(h2 * 64, h2 * 64 + 64)
                        dsv = dSp[d][P, :, :].rearrange("p (a b) e -> p a b e", b=2)[:, :, h2, :]
                        egv = eg[d][P, n, :].rearrange("p (a b) -> p a b", b=2)[:, :, h2].unsqueeze(2).to_broadcast([64, 2, 80])
                        cx.tt("dve", S32[d][P], S32[d][P], dsv, ALU.add, [f"S32{d}", f"dS{d}"], [f"S32{d}"])
                        cx.tt("dve", S32[d][P], S32[d][P], egv, ALU.mult, [f"S32{d}", f"eg{d}"], [f"S32{d}"])
                        cx.cp("act", S16[d][P], S32[d][P], [f"S32{d}"], [f"S16{d}"])
                    if STG < 4:
                        continue
                    cx.tt("dve", nd[d][:], nump[d][:], eb[d][:, n, :].unsqueeze(2).to_broadcast([128, 4, 80]), ALU.mult, [f"num{d}", f"eb{d}"], [f"nd{d}"])
                    cx.act(den[d][:], nd[d][:, :, 64:65], AF.Abs, [f"nd{d}"], [f"den{d}"])
                    cx.ts("dve", den[d][:], den[d][:], 1.0, None, ALU.max, None, [f"den{d}"], [f"den{d}"])
                    cx.recip(den[d][:], den[d][:], [f"den{d}"], [f"den{d}"])
                    cx.tt("dve", hO[d][:, n, :].rearrange("p (h e) -> p h e", h=4), nd[d][:, :, 0:64], den[d][:].to_broadcast([128, 4, 64]), ALU.mult,
                          [f"nd{d}", f"den{d}"], [f"hO{d}"])
            for n0 in range(0, 64 if STG not in (-1, -3, -4, -5, -6, -7) else 0, 8):
                cx.dma(ob[:], T["z_d"][s, n0 * 128:(n0 + 8) * 128, 512:768].rearrange("(n p) c -> p n c", p=128), [], ["ob"])
                cx.act(sg[:], ob[:], AF.Sigmoid, ["ob"], ["sg"])
                cx.tt("dve", hs[:], hO[0][:, n0:n0 + 8, :], hO[1][:, n0:n0 + 8, :], ALU.add, ["hO0", "hO1"], ["hs"])
                cx.tt("dve", yd[:], sg[:], hs[:], ALU.mult, ["sg", "hs"], ["yd"])
                cx.dma(T["y"][s, n0 * 128:(n0 + 8) * 128, 768:1024].rearrange("(n p) c -> p n c", p=128), yd[:], ["yd"], [])
    S.barrier()


def phase_C1(cx, layer):
    nc, S, T = cx.nc, cx.S, cx.T
    with contextlib.ExitStack() as ph:
        sb = lambda n, s, d: ph.enter_context(cx.sbt(n, s, d))
        ps = lambda n, s, d: ph.enter_context(cx.pst(n, s, d))
        wo = sb("C_wo", [128, 8, D], BF16)
        wg = sb("C_wg", [128, 8, DFF], BF16)
        wu = sb("C_wu", [128, 8, DFF], BF16)
        identb = sb("C_idb", [128, 128], BF16)
        cx.dma(identb[:], T["ident_b"][:, :], [], ["identb"])
        stg = make_staging(cx, ph, 8, 128, "Cstg")
        load_scaled_weight(cx, ph, wo, "C_wo", T["w_out"][layer], T["out_norm"][layer], 8, D, "Cwo", stg)
        load_scaled_weight(cx, ph, wg, "C_wg", T["w_gate"][layer], T["norm_ffn"][layer], 8, DFF, "Cwg", stg)
        load_scaled_weight(cx, ph, wu, "C_wu", T["w_up"][layer], T["norm_ffn"][layer], 8, DFF, "Cwu", stg)
        yin = sb("C_yin", [128, 4, D], BF16)
        xin = sb("C_xin", [128, 4, D], F32)
        sq = sb("C_sq", [128, D], F32)
        ssh = sb("C_ssh", [128, 4, 16], F32)
        yT = sb("C_yT", [128, 8, 512], BF16)
        xs = sb("C_xs", [128, 4, D], BF16)
        junk = sb("C_junk", [128, D], BF16)
        ss = sb("C_ss", [128, 4], F32)
        rs = sb("C_rs", [128, 4], F32)
        hT = [sb(f"C_hT{i}", [128, 8, 512], BF16) for i in range(2)]
        sgt = [sb(f"C_sg{i}", [128, 512], F32) for i in range(2)]
        hh = [sb(f"C_hh{i}", [128, 512], BF16) for i in range(3)]
        pTs = [ps(f"C_pT{i}", [128, 8, 128], BF16) for i in range(2)]
        pm = [ps(f"C_pm{i}", [128, 512], F32) for i in range(2)]
        pg = [ps(f"C_pg{i}", [128, 512], F32) for i in range(2)]
        pu = [ps(f"C_pu{i}", [128, 512], F32) for i in range(2)]
        src = T["xs"] if layer == 0 else T["xres"]
        NT = L // 512
        cnt = {"pm": 0, "nf": 0}

        def front(it):
            s, i = divmod(it, NT)
            t0 = i * 512
            b = it % 2
            hk = f"hT{b}"
            tv = lambda ap: ap[s, t0:t0 + 512, :].rearrange("(j p) d -> p j d", p=128)
            cx.dma(yin[:], tv(T["y"]), [], ["yin"])
            cx.dma(xin[:], tv(src), [], ["xin"])
            yield
            for j in range(4):
                cx.tt("pool", sq[:], yin[:, j, :], yin[:, j, :], ALU.mult, ["yin"], ["sq"])
                cx.S.op("dve", lambda e, j=j: e.tensor_reduce(out=ssh[:, j, :], in_=sq[:].rearrange("p (g c) -> p g c", c=64), axis=AX.X, op=ALU.add),
                        reads=["sq"], writes=["ssh"])
                yield
            cx.rstd(ssh[:], ssh[:], 1.0 / 64, ["ssh"], ["ssh"], "C")
            yield
            for j in range(4):
                yv = yin[:, j, :].rearrange("p (g c) -> p g c", c=64)
                cx.tt("dve", yv, yv, ssh[:, j, :].unsqueeze(2).to_broadcast([128, 16, 64]), ALU.mult, ["yin", "ssh"], ["yin"])
                yield
            for j in range(4):
                pT = pTs[j % 2]
                pk = f"CpT{j % 2}"
                for k in range(8):
                    cx.tr(pT[:, k, :], yin[:, j, k * 128:(k + 1) * 128], identb[:], ["yin", "identb"], [pk])
                cx.cp(cx.ev(), yT[:, :, j * 128:(j + 1) * 128], pT[:], [pk], ["yT"])
                yield
            for j in range(4):
                tj = slice(j * 128, (j + 1) * 128)
                for c0 in (0, 512):
                    p = pm[cnt["pm"] % 2]
                    pk = f"Cpm{cnt['pm'] % 2}"
                    cnt["pm"] += 1
                    for k in range(8):
                        cx.mm(p[:], yT[:, k, tj], wo[:, k, c0:c0 + 512], k == 0, k == 7, ["yT", "C_wo"], [pk])
                    cx.tt("dve", xin[:, j, c0:c0 + 512], p[:], xin[:, j, c0:c0 + 512], ALU.add, [pk, "xin"], ["xin"])
                    yield
            cx.dma(tv(T["x1"]), xin[:], ["xin"], [])
            yield from norm_T4(cx, xin, "xin", xs, "C_xs", junk, ss, rs, pTs, hT[b], hk, identb, "C")

        def back(it):
            s, i = divmod(it, NT)
            t0 = i * 512
            b = it % 2
            hk = f"hT{b}"
            for f in range(DFF // 128):
                nf = cnt["nf"]
                cnt["nf"] += 1
                q = nf % 2
                q3 = nf % 3
                fs = slice(f * 128, (f + 1) * 128)
                for k in range(8):
                    cx.mm(pg[q][:], wg[:, k, fs], hT[b][:, k, :], k == 0, k == 7, ["C_wg", hk], [f"pg{q}"])
                for k in range(8):
                    cx.mm(pu[q][:], wu[:, k, fs], hT[b][:, k, :], k == 0, k == 7, ["C_wu", hk], [f"pu{q}"])
                cx.act(sgt[q][:], pg[q][:], AF.Silu, [f"pg{q}"], [f"sg{q}"])
                cx.tt("dve", hh[q3][:], pu[q][:], sgt[q][:], ALU.mult, [f"pu{q}", f"sg{q}"], [f"hh{q3}"])
                cx.dma(T["hscr"][s, fs, t0:t0 + 512], hh[q3][:], [f"hh{q3}"], [])
                yield

        drive(front, back, cx.nseq * NT, k=2)
    S.barrier()


def phase_C2(cx, layer):
    nc, S, T = cx.nc, cx.S, cx.T
    last = layer == DEPTH - 1
    with contextlib.ExitStack() as ph:
        sb = lambda n, s, d: ph.enter_context(cx.sbt(n, s, d))
        ps = lambda n, s, d: ph.enter_context(cx.pst(n, s, d))
        wd = sb("E_wd", [128, 22, D], BF16)
        wpg = sb("E_wpg", [128, 8, D], BF16)
        wpp = sb("E_wpp", [128, 2, D], BF16)
        identb = sb("E_idb", [128, 128], BF16)
        gfin = sb("E_gfin", [128, D], F32)
        cx.dma(identb[:], T["ident_b"][:, :], [], ["identb"])
        cx.dma(gfin[:], T["final_norm"].partition_broadcast(128), [], ["gfin"])
        stg = make_staging(cx, ph, 22, 128, "Estg")
        load_scaled_weight(cx, ph, wd, "E_wd", T["w_down"][layer], None, 22, D, "Ewd", stg)
        load_scaled_weight(cx, ph, wpg, "E_wpg", T["w_ple_gate"][layer], T["ple_norm"][layer], 8, D, "Ewpg", stg)
        load_scaled_weight(cx, ph, wpp, "E_wpp", T["w_ple_proj"][layer], None, 2, D, "Ewpp", stg)
        hT = [sb(f"E_hT{i}", [128, 22, 512], BF16) for i in range(2)]
        x1 = [sb(f"E_x1{i}", [128, 4, D], F32) for i in range(2)]
        pin = sb("E_pin", [128, 4, 256], F32)
        pbf = sb("E_pbf", [128, 4, 256], BF16)
        pTs_ = sb("E_pTs", [128, 2, 512], BF16)
        xs = sb("E_xs", [128, 4, D], BF16)
        junk = sb("E_junk", [128, D], BF16)
        ss = sb("E_ss", [128, 4], F32)
        rs = sb("E_rs", [128, 4], F32)
        xT = sb("E_xT", [128, 8, 512], BF16)
        sgt = [sb(f"E_sg{i}", [128, 512], F32) for i in range(2)]
        pTp = [ps(f"E_pT{i}", [128, 8, 128], BF16) for i in range(2)]
        pm = [ps(f"E_pm{i}", [128, 512], F32) for i in range(2)]
        pg = [ps(f"E_pg{i}", [128, 512], F32) for i in range(2)]
        pp = [ps(f"E_pp{i}", [128, 512], F32) for i in range(2)]
        NT = L // 512
        cnt = {"pm": 0, "ng": 0}

        def front(it):
            s, i = divmod(it, NT)
            t0 = i * 512
            b = it % 2
            hk, xk = f"EhT{b}", f"Ex1{b}"
            tv = lambda ap: ap[s, t0:t0 + 512, :].rearrange("(j p) d -> p j d", p=128)
            cx.dma(hT[b][:], T["hscr"][s, :, t0:t0 + 512].rearrange("(f p) t -> p f t", p=128), [], [hk])
            cx.dma(x1[b][:], tv(T["x1"]), [], [xk])
            yield
            for j in range(4):
                tj = slice(j * 128, (j + 1) * 128)
                for c0 in (0, 512):
                    p = pm[cnt["pm"] % 2]
                    pk = f"Epm{cnt['pm'] % 2}"
                    cnt["pm"] += 1
                    for f in range(22):
                        cx.mm(p[:], hT[b][:, f, tj], wd[:, f, c0:c0 + 512], f == 0, f == 21, [hk, "E_wd"], [pk])
                    cx.tt("dve", x1[b][:, j, c0:c0 + 512], p[:], x1[b][:, j, c0:c0 + 512], ALU.add, [pk, xk], [xk])
                    yield

        def back(it):
            s, i = divmod(it, NT)
            t0 = i * 512
            b = it % 2
            xk = f"Ex1{b}"
            tv = lambda ap: ap[s, t0:t0 + 512, :].rearrange("(j p) d -> p j d", p=128)
            cx.dma(pin[:], T["ps"][layer, s, t0:t0 + 512, :].rearrange("(j p) d -> p j d", p=128), [], ["pin"])
            cx.cp("pool", pbf[:], pin[:], ["pin"], ["pbf"])
            yield
            yield from norm_T4(cx, x1[b], xk, xs, "E_xs", junk, ss, rs, pTp, xT, "ExT", identb, "E")
            for j in range(4):
                pT = pTp[j % 2]
                pk = f"EpT{j % 2}"
                for k in range(2):
                    cx.tr(pT[:, k, :], pbf[:, j, k * 128:(k + 1) * 128], identb[:], ["pbf", "identb"], [pk])
                cx.cp(cx.ev(), pTs_[:, :, j * 128:(j + 1) * 128], pT[:, 0:2, :], [pk], ["pTs"])
                yield
            for j in range(4):
                tj = slice(j * 128, (j + 1) * 128)
                for c0 in (0, 512):
                    q = cnt["ng"] % 2
                    cnt["ng"] += 1
                    for k in range(8):
                        cx.mm(pg[q][:], xT[:, k, tj], wpg[:, k, c0:c0 + 512], k == 0, k == 7, ["ExT", "E_wpg"], [f"Epg{q}"])
                    for k in range(2):
                        cx.mm(pp[q][:], pTs_[:, k, tj], wpp[:, k, c0:c0 + 512], k == 0, k == 1, ["pTs", "E_wpp"], [f"Epp{q}"])
                    cx.act(sgt[q][:], pg[q][:], AF.Sigmoid, [f"Epg{q}"], [f"Esg{q}"])
                    cx.tt("dve", sgt[q][:], pp[q][:], sgt[q][:], ALU.mult, [f"Epp{q}", f"Esg{q}"], [f"Esg{q}"])
                    cx.tt("pool", x1[b][:, j, c0:c0 + 512], x1[b][:, j, c0:c0 + 512], sgt[q][:], ALU.add, [xk, f"Esg{q}"], [xk])
                    yield
            if not last:
                cx.dma(tv(T["xres"]), x1[b][:], [xk], [])
            else:
                for j in range(4):
                    cx.act(junk[:], x1[b][:, j, :], AF.Square, [xk], ["Ejunk", "Ess"], accum_out=ss[:, j:j + 1])
                cx.rstd(rs[:], ss[:], 1.0 / D, ["Ess"], ["Ers"], "E")
                for j in range(4):
                    cx.stt("dve", x1[b][:, j, :], x1[b][:, j, :], rs[:, j:j + 1], gfin[:], ALU.mult, ALU.mult, [xk, "Ers", "gfin"], [xk])
                cx.dma(tv(T["out"]), x1[b][:], [xk], [])
            yield

        drive(front, back, cx.nseq * NT, k=2)
    S.barrier()


_CONSTS = None


def bf(a):
    return np.asarray(a, dtype=np.float32).astype(ml_dtypes.bfloat16)


def host_consts():
    global _CONSTS
    if _CONSTS is not None:
        return _CONSTS
    c = {}
    pi2 = 2.0 * np.pi
    c["ident_b"] = bf(np.eye(128))
    m = np.arange(64)
    ang = pi2 * np.outer(m, m) / 64.0
    sc = 1.0 / math.sqrt(L * 64.0)
    cs = np.zeros((64, 2, 2, 128), np.float32)
    for ab, mat in enumerate((np.cos(ang) * sc, np.sin(ang) * sc)):
        cs[:, ab, 0, 0:64] = mat
        cs[:, ab, 1, 64:128] = mat
    c["cs64"] = cs
    l1 = np.arange(64)[:, None]
    k1 = np.arange(64)[None, :]
    th = pi2 * l1 * k1 / 64.0
    F1 = np.zeros((64, 2, 128))
    F1[:, 0, 0:64] = np.cos(th)
    F1[:, 0, 64:128] = -np.sin(th)
    F1[:, 1, 0:64] = -np.sin(th)
    F1[:, 1, 64:128] = -np.cos(th)
    c["f_F1"] = bf(F1)
    l2 = np.arange(128)[:, None, None]
    k1 = np.arange(64)[None, :, None]
    k2 = np.arange(128)[None, None, :]
    ph = pi2 * (((k1 + 64 * k2) * l2) % L) / L
    c["f_T2c"] = bf(np.cos(ph))
    c["f_T2s"] = bf(np.sin(ph))
    l1 = np.arange(128)[:, None]
    k1 = np.arange(65)[None, :]
    th = pi2 * l1 * k1 / 128.0
    c["h_F1"] = bf(np.concatenate([np.cos(th), -np.sin(th), -np.cos(th)], axis=1))
    l2 = np.arange(128)[:, None, None]
    k1 = np.arange(65)[None, :, None]
    k2 = np.arange(128)[None, None, :]
    ph = pi2 * (((k1 + 128 * k2) * l2) % NFFT) / NFFT
    c["h_T2c"] = bf(np.cos(ph))
    c["h_T2s"] = bf(np.sin(ph))
    k2 = np.arange(128)[:, None]
    t2 = np.arange(128)[None, :]
    psi = pi2 * k2 * t2 / 128.0
    GA = np.zeros((128, 2, 256))
    GA[:, 0, 0:128] = np.cos(psi)
    GA[:, 0, 128:256] = np.sin(psi)
    GA[:, 1, 0:128] = -np.sin(psi)
    GA[:, 1, 128:256] = np.cos(psi)
    c["h_GA"] = bf(GA)
    k1 = np.arange(65)[:, None, None]
    t2 = np.arange(128)[None, :, None]
    t1 = np.arange(64)[None, None, :]
    om = pi2 * ((k1 * (128 * t1 + t2)) % NFFT) / NFFT
    wk = np.where((k1 == 0) | (k1 == 64), 1.0, 2.0) / NFFT
    c["h_TBc"] = bf(wk * np.cos(om))
    c["h_TBs"] = bf(-wk * np.sin(om))
    R = np.arange(2 * L)
    pos = np.where(R < L, R, 2 * L - R)
    pos[L] = 0
    s = pos.astype(np.float32)
    t = (s / np.float32(L - 1)).astype(np.float32)
    angp = (np.float32(2.0 * math.pi / L) * s).astype(np.float32)
    bands = np.linspace(1e-4, 15.0, 16, dtype=np.float32)
    fb = angp[:, None] * bands[None, :]
    feats = np.concatenate([t[:, None], np.cos(fb), -np.sin(fb)], axis=-1).astype(np.float32)
    c["h_feats"] = np.ascontiguousarray(feats.T)
    c["h_tvec"] = np.ascontiguousarray(t[None, :])
    si = np.arange(128)[:, None]
    ci = np.arange(128)[None, :]
    c["m_tri"] = np.stack([(si <= ci), (si >= ci)]).astype(np.float32)
    idx = np.zeros((8, 2, 128, 512, 3), np.int64)
    msk = np.zeros((8, 2, 128, 512), bool)
    reps = [0, 1, 2, 3, 64, 125, 126, 127]
    for v, r in enumerate(reps):
        kr0 = min(max(r - 4, 0), 120)
        for hp in range(2):
            for h in range(2):
                for q in range(64):
                    kc0 = min(max(q - 8, 0), 48)
                    for j in range(8):
                        dr = kr0 + j - r + 7
                        kc = np.arange(kc0, kc0 + 16)
                        n = j * 64 + kc
                        idx[v, hp, h * 64 + q, n, 0] = 2 * hp + h
                        idx[v, hp, h * 64 + q, n, 1] = dr
                        idx[v, hp, h * 64 + q, n, 2] = kc - q + 15
                        msk[v, hp, h * 64 + q, n] = True
    c["_na_idx"] = idx
    c["_na_msk"] = msk
    _CONSTS = c
    return c


CONST_SPECS = [("ident_b", [128, 128], BF16), ("cs64", [64, 2, 2, 128], F32), ("f_F1", [64, 2, 128], BF16),
               ("f_T2c", [128, 64, 128], BF16), ("f_T2s", [128, 64, 128], BF16), ("h_F1", [128, 195], BF16),
               ("h_T2c", [128, 65, 128], BF16), ("h_T2s", [128, 65, 128], BF16), ("h_GA", [128, 2, 256], BF16),
               ("h_TBc", [65, 128, 64], BF16), ("h_TBs", [65, 128, 64], BF16), ("h_feats", [33, 2 * L], F32),
               ("h_tvec", [1, 2 * L], F32), ("m_tri", [2, 128, 128], F32)]

WEIGHT_SPECS = [("norm_mix", [DEPTH, D]), ("w_in", [DEPTH, D, NIN]), ("fnet_w", [DEPTH, 4, 64, 64]),
                ("hy_conv_w", [DEPTH, 3, 768]), ("hy_conv_b", [DEPTH, 768]), ("hy_w1", [DEPTH, 33, 64]), ("hy_b1", [DEPTH, 64]),
                ("hy_freq", [DEPTH, 2, 64]), ("hy_w2", [DEPTH, 64, 64]), ("hy_b2", [DEPTH, 64]), ("hy_w3", [DEPTH, 64, 1024]),
                ("hy_decay", [DEPTH, 2, 2, 256]), ("hy_skip", [DEPTH, 2, 256]), ("ml_gate_b", [DEPTH, 4, 4]),
                ("out_norm", [DEPTH, D]), ("w_out", [DEPTH, D, D]), ("norm_ffn", [DEPTH, D]), ("w_gate", [DEPTH, D, DFF]),
                ("w_up", [DEPTH, D, DFF]), ("w_down", [DEPTH, DFF, D]), ("ple_norm", [DEPTH, D]), ("w_ple_gate", [DEPTH, D, D]),
                ("w_ple_proj", [DEPTH, 256, D]), ("final_norm", [D])]


def build(nseq, debug=False, phases=None, scratch_in=()):
    nc = bass.Bass("TRN2", target_bir_lowering=False)
    T = {}

    def din(name, shape, dt=F32):
        T[name] = nc.dram_tensor(name, list(shape), dt, kind="ExternalInput").ap()

    def dscr(name, shape, dt):
        kind = "ExternalInput" if name in scratch_in else ("ExternalOutput" if debug else "Internal")
        T[name] = nc.dram_tensor(name, list(shape), dt, kind=kind).ap()

    din("xs", [nseq, L, D])
    din("ps", [DEPTH, nseq, L, 256])
    for n, shp in WEIGHT_SPECS:
        din(n, shp)
    din("na_bias", [DEPTH, 16, 128, 512])
    for n, shp, dt in CONST_SPECS:
        din(n, shp, dt)
    T["out"] = nc.dram_tensor("out", [nseq, L, D], F32, kind="ExternalOutput").ap()
    dscr("zT_a", [nseq, 512, L], BF16)
    dscr("z_va", [nseq, L, 256], BF16)
    dscr("z_fn", [nseq, L, 512], BF16)
    dscr("z_hy", [nseq, L + 2, 768], BF16)
    dscr("z_hc", [nseq, 768 // CB, L, CB], BF16)
    dscr("y_hc", [nseq, 256 // CB, L, CB], BF16)
    dscr("h_ftm", [2 * L, 512], BF16)
    dscr("zT_d", [nseq, 512, L], BF16)
    dscr("z_d", [nseq, L, 768], BF16)
    dscr("z_g", [nseq, 128, 64, 16], F32)
    dscr("y", [nseq, L, D], BF16)
    dscr("x1", [nseq, L, D], F32)
    dscr("xres", [nseq, L, D], F32)
    dscr("hscr", [nseq, DFF, L], BF16)
    dscr("h_full", [512 // CB, 2 * L, CB], BF16)
    dscr("h_KF", [512 // CB, 128, 65 * 2 * CB], BF16)
    S = Sched(nc)
    cx = Ctx(nc, S, T, nseq)
    with contextlib.ExitStack() as ph:
        zt = ph.enter_context(cx.sbt("zpad", [1, 768], BF16))
        cx.memset("pool", zt[:], 0.0, ["zpad"])
        for s in range(nseq):
            if "z_hy" in scratch_in:
                break
            cx.dma(T["z_hy"][s, 0:1, :], zt[:], ["zpad"], [])
            cx.dma(T["z_hy"][s, L + 1:L + 2, :], zt[:], ["zpad"], [])
    S.barrier()
    allp = ["A", "NA", "FN", "HYF", "HC", "HY", "ML", "C1", "C2"]
    fns = {"A": phase_A, "NA": phase_NA, "FN": phase_FN, "HYF": phase_HYF, "HC": phase_HC, "HY": phase_HY, "ML": phase_ML, "C1": phase_C1, "C2": phase_C2}
    for layer in range(DEPTH):
        for pn in allp:
            if phases is None or (layer, pn) in phases:
                fns[pn](cx, layer)
    S.emit()
    return nc


def na_bias_host(rpb):
    c = host_consts()
    idx, msk = c["_na_idx"], c["_na_msk"]
    out = np.empty((DEPTH, 8, 2, 128, 512), np.float32)
    for l in range(DEPTH):
        g = rpb[l][idx[..., 0], idx[..., 1], idx[..., 2]]
        out[l] = np.where(msk, g, np.float32(-30000.0))
    return out.reshape(DEPTH, 16, 128, 512)


_NC_CACHE = {}


def kernel(**inputs):
    nseq = 2
    n_cores = 8
    if nseq not in _NC_CACHE:
        _NC_CACHE[nseq] = build(nseq)
    nc = _NC_CACHE[nseq]
    c = host_consts()
    f32 = lambda a: np.ascontiguousarray(np.asarray(a, dtype=np.float32))
    shared = {n: f32(inputs[n]) for n, _ in WEIGHT_SPECS}
    shared["hy_w3"] = f32(inputs["hy_w3"])
    shared["na_bias"] = na_bias_host(f32(inputs["attn_rpb"]))
    for n, _, _ in CONST_SPECS:
        shared[n] = c[n]
    xp, xsm = f32(inputs["x_prompt"]), f32(inputs["x_sample"])
    pp, psm = f32(inputs["p_prompt"]), f32(inputs["p_sample"])
    in_maps = []
    for core in range(n_cores):
        m = dict(shared)
        m["xs"] = np.ascontiguousarray(np.stack([xsm[core], xp[core % 2]]))
        m["ps"] = np.ascontiguousarray(np.stack([psm[:, core], pp[:, core % 2]], axis=1))
        in_maps.append(m)
    res = run_bass_kernel_spmd(nc, in_maps, core_ids=list(range(n_cores)))
    y_sample = np.stack([np.asarray(res.results[core]["out"][0], dtype=np.float32) for core in range(n_cores)])
    y_prompt = np.stack([np.asarray(res.results[core]["out"][1], dtype=np.float32) for core in range(2)])
    return (y_prompt, y_sample)
```
